# Optimizing a Trainium2 kernel written in Bass

```python
import math
import jax, jax.numpy as jnp
from jax import lax
import numpy as np

D_MODEL = 2048
BATCH = 16
SEQ = 2048
DEPTH = 4

GRID_W = 64
CTX_LEN = 256
N_MIXERS = 4
MIXER_GLA, MIXER_GQA, MIXER_DIFF, MIXER_FOURIER = 0, 1, 2, 3
NORM_EPS = 1e-6
ROPE_THETA = 10000.0
Q_BLOCK = 128
HEAD_DIM = 128
FFN_HIDDEN = -(-8 * D_MODEL // (3 * 256)) * 256

GLA_HEADS = 4
GLA_DK = D_MODEL // 2 // GLA_HEADS
GLA_DV = D_MODEL // GLA_HEADS
GLA_QK = GLA_HEADS * GLA_DK
GLA_RANK = 16
GLA_TAU = 16.0
GLA_CHUNK = 64
GLA_SPLITS = (GLA_QK, 2 * GLA_QK, 2 * GLA_QK + D_MODEL, 2 * GLA_QK + 2 * D_MODEL,
              2 * GLA_QK + 2 * D_MODEL + GLA_RANK)
GLA_IN_DIM = 2 * GLA_QK + 2 * D_MODEL + 2 * GLA_RANK

GQA_HEADS = D_MODEL // HEAD_DIM
GQA_KV_HEADS = 4
GQA_GROUP = GQA_HEADS // GQA_KV_HEADS
GQA_IN_DIM = (GQA_HEADS + 2 * GQA_KV_HEADS) * HEAD_DIM

DIFF_HEADS = D_MODEL // (2 * HEAD_DIM)
DIFF_DH = HEAD_DIM
DIFF_IN_DIM = 3 * D_MODEL

FNET_GROUPS = 4
FNET_GROUP_DIM = D_MODEL // FNET_GROUPS

kernel_name = "hybrid_interleaved_prefix_dit_block"


def rms_norm(x, gain):
    xf = x.astype(jnp.float32)
    y = xf * lax.rsqrt(jnp.mean(xf * xf, axis=-1, keepdims=True) + NORM_EPS)
    return (y * gain.astype(jnp.float32)).astype(x.dtype)


def modulate(h, shift, scale):
    return h * (1 + scale) + shift


def swiglu(h, w_in, w_out):
    gate, up = jnp.split(h @ w_in, 2, axis=-1)
    return (jax.nn.silu(gate) * up) @ w_out


def axial_rope_tables(n_tokens, dim, dtype):
    t = jnp.arange(n_tokens)
    row = (t // GRID_W).astype(jnp.float32)
    col = (t % GRID_W).astype(jnp.float32)
    half = dim // 2
    inv_freq = ROPE_THETA ** (-jnp.arange(0, half, 2, dtype=jnp.float32) / half)
    ang_r = row[:, None] * inv_freq[None, :]
    ang_c = col[:, None] * inv_freq[None, :]
    ang = jnp.concatenate([ang_r, ang_r, ang_c, ang_c], axis=-1)
    return jnp.cos(ang).astype(dtype), jnp.sin(ang).astype(dtype)


def apply_axial_rope(x, cos, sin):
    x1, x2, x3, x4 = jnp.split(x, 4, axis=-1)
    rot = jnp.concatenate([-x2, x1, -x4, x3], axis=-1)
    return x * cos[:, None, :] + rot * sin[:, None, :]


def sweep_query_blocks(fn, *qs):
    b, s = qs[0].shape[:2]
    nb = s // Q_BLOCK
    blocks = tuple(jnp.moveaxis(q.reshape(b, nb, Q_BLOCK, *q.shape[2:]), 1, 0) for q in qs)
    out = lax.map(lambda qb: fn(*qb), blocks)
    return jnp.moveaxis(out, 0, 1).reshape(b, s, *out.shape[3:])


def gla_chunked(q, k, v, g, state0):
    b_, l_, h_, dk = q.shape
    dv = v.shape[-1]
    n = l_ // GLA_CHUNK

    def chunks(a):
        return a.reshape(b_, n, GLA_CHUNK, h_, a.shape[-1]).transpose(1, 0, 3, 2, 4)

    qc, kc, vc, gc = chunks(q), chunks(k), chunks(v), chunks(g)
    cum = jnp.cumsum(gc, axis=3)
    cum_last = cum[:, :, :, -1:, :]
    cum_mid = cum[:, :, :, GLA_CHUNK // 2 - 1:GLA_CHUNK // 2, :]
    a = jnp.einsum('nbhid,nbhjd->nbhij', qc * jnp.exp(cum - cum_mid), kc * jnp.exp(cum_mid - cum))
    mask = jnp.tril(jnp.ones((GLA_CHUNK, GLA_CHUNK), dtype=bool))
    o_intra = jnp.einsum('nbhij,nbhje->nbhie', jnp.where(mask, a, 0.0), vc)
    q_inter = qc * jnp.exp(cum)
    k_carry = kc * jnp.exp(cum_last - cum)
    decay = jnp.exp(cum_last[:, :, :, 0, :])

    def step(state, xs):
        qi, ki, vi, di = xs
        o = jnp.einsum('bhcd,bhde->bhce', qi, state)
        state = state * di[..., None] + jnp.einsum('bhcd,bhce->bhde', ki, vi)
        return state, o

    state_final, o_inter = lax.scan(step, state0, (q_inter, k_carry, vc, decay))
    o = (o_intra + o_inter).transpose(1, 0, 3, 2, 4).reshape(b_, l_, h_, dv)
    return o, state_final


def gla_mixer(h_lat, h_ctx, w_in, wg_f, bg_f, wg_b, bg_b, out_norm, w_out, ctx_out):
    def project(h):
        b_, l_, _ = h.shape
        q, k, v, r, zf, zb = jnp.split(h @ w_in, GLA_SPLITS, axis=-1)
        heads = lambda a, d: a.astype(jnp.float32).reshape(b_, l_, GLA_HEADS, d)
        gate = lambda z, w, bias: (jax.nn.log_sigmoid((z @ w + bias).astype(jnp.float32)) / GLA_TAU
                                   ).reshape(b_, l_, GLA_HEADS, GLA_DK)
        return (heads(q, GLA_DK) * GLA_DK ** -0.5, heads(k, GLA_DK), heads(v, GLA_DV), r,
                gate(zf, wg_f, bg_f), gate(zb, wg_b, bg_b))

    flip = lambda a: a[:, ::-1]

    def bidir(q, k, v, gf, gb, s_f, s_b):
        o_f, s_f = gla_chunked(q, k, v, gf, s_f)
        o_b, s_b = gla_chunked(flip(q), flip(k), flip(v), flip(gb), s_b)
        return o_f + flip(o_b), s_f, s_b

    def finish(o, r):
        b_, l_ = r.shape[:2]
        o = rms_norm(o, out_norm).reshape(b_, l_, D_MODEL).astype(r.dtype)
        return (o * jax.nn.silu(r)) @ w_out

    qc, kc, vc, rc, gcf, gcb = project(h_ctx)
    zero = jnp.zeros((h_ctx.shape[0], GLA_HEADS, GLA_DK, GLA_DV), jnp.float32)
    o_c, s_f, s_b = bidir(qc, kc, vc, gcf, gcb, zero, zero)
    ql, kl, vl, rl, glf, glb = project(h_lat)
    o_l, _, _ = bidir(ql, kl, vl, glf, glb, s_f, s_b)
    y_ctx = finish(o_c, rc) if ctx_out else None
    return finish(o_l, rl), y_ctx


def gqa_attend(q, k, v):
    s = jnp.einsum('bqhgd,bkhd->bhgqk', q, k).astype(jnp.float32) * HEAD_DIM ** -0.5
    p = jax.nn.softmax(s, axis=-1).astype(v.dtype)
    return jnp.einsum('bhgqk,bkhd->bqhgd', p, v)


def gqa_mixer(h_lat, h_ctx, w_in, q_norm, k_norm, w_out, ctx_out):
    b_, s_, _ = h_lat.shape

    def project(h):
        l_ = h.shape[1]
        q, k, v = jnp.split(h @ w_in, (GQA_HEADS * HEAD_DIM, (GQA_HEADS + GQA_KV_HEADS) * HEAD_DIM), axis=-1)
        q = rms_norm(q.reshape(b_, l_, GQA_HEADS, HEAD_DIM), q_norm)
        k = rms_norm(k.reshape(b_, l_, GQA_KV_HEADS, HEAD_DIM), k_norm)
        return q, k, v.reshape(b_, l_, GQA_KV_HEADS, HEAD_DIM)

    group = lambda q: q.reshape(b_, q.shape[1], GQA_KV_HEADS, GQA_GROUP, HEAD_DIM)
    q_c, k_c, v_c = project(h_ctx)
    q_l, k_l, v_l = project(h_lat)
    cos, sin = axial_rope_tables(s_, HEAD_DIM, q_l.dtype)
    q_l, k_l = apply_axial_rope(q_l, cos, sin), apply_axial_rope(k_l, cos, sin)
    k_all = jnp.concatenate([k_c, k_l], axis=1)
    v_all = jnp.concatenate([v_c, v_l], axis=1)
    o_l = sweep_query_blocks(lambda qb: gqa_attend(qb, k_all, v_all), group(q_l))
    y_lat = o_l.reshape(b_, s_, D_MODEL) @ w_out
    y_ctx = None
    if ctx_out:
        y_ctx = gqa_attend(group(q_c), k_c, v_c).reshape(b_, q_c.shape[1], D_MODEL) @ w_out
    return y_lat, y_ctx


def diff_attend(q, k, v, lam):
    s = jnp.einsum('bqhmd,bkhmd->bhmqk', q, k).astype(jnp.float32) * DIFF_DH ** -0.5
    p = jax.nn.softmax(s, axis=-1)
    w = (p[:, :, 0] - lam * p[:, :, 1]).astype(v.dtype)
    return jnp.einsum('bhqk,bkhe->bqhe', w, v)


def diff_mixer(h_lat, h_ctx, w_in, q_norm, k_norm, lq1, lk1, lq2, lk2, out_norm, w_out, layer_idx, ctx_out):
    b_, s_, _ = h_lat.shape
    lam_init = 0.8 - 0.6 * math.exp(-0.3 * layer_idx)
    f32 = jnp.float32
    lam = (jnp.exp(jnp.sum(lq1.astype(f32) * lk1.astype(f32)))
           - jnp.exp(jnp.sum(lq2.astype(f32) * lk2.astype(f32))) + lam_init)

    def project(h):
        l_ = h.shape[1]
        q, k, v = jnp.split(h @ w_in, (D_MODEL, 2 * D_MODEL), axis=-1)
        q = rms_norm(q.reshape(b_, l_, 2 * DIFF_HEADS, DIFF_DH), q_norm)
        k = rms_norm(k.reshape(b_, l_, 2 * DIFF_HEADS, DIFF_DH), k_norm)
        return q, k, v.reshape(b_, l_, DIFF_HEADS, 2 * DIFF_DH)

    pair = lambda a: a.reshape(b_, a.shape[1], DIFF_HEADS, 2, DIFF_DH)

    def finish(o):
        o = rms_norm(o, out_norm) * (1.0 - lam_init)
        return o.reshape(b_, o.shape[1], D_MODEL) @ w_out

    q_c, k_c, v_c = project(h_ctx)
    q_l, k_l, v_l = project(h_lat)
    cos, sin = axial_rope_tables(s_, DIFF_DH, q_l.dtype)
    q_l, k_l = apply_axial_rope(q_l, cos, sin), apply_axial_rope(k_l, cos, sin)
    k_all = pair(jnp.concatenate([k_c, k_l], axis=1))
    v_all = jnp.concatenate([v_c, v_l], axis=1)
    o_l = sweep_query_blocks(lambda qb: diff_attend(qb, k_all, v_all, lam), pair(q_l))
    y_ctx = finish(diff_attend(pair(q_c), pair(k_c), v_c, lam)) if ctx_out else None
    return finish(o_l), y_ctx


def fnet_mixer(h, w_out):
    b_, l_, _ = h.shape
    hg = h.astype(jnp.float32).reshape(b_, l_, FNET_GROUPS, FNET_GROUP_DIM)
    y = jnp.real(jnp.fft.fftn(hg, axes=(1, 3), norm="ortho"))
    return y.reshape(b_, l_, D_MODEL).astype(h.dtype) @ w_out


def setup_inputs(seed: int = 0) -> dict:
    key = jax.random.key(seed)
    counter = [0]

    def nxt():
        counter[0] += 1
        return jax.random.fold_in(key, counter[0])

    normal = lambda shape: jax.random.normal(nxt(), shape, jnp.float32)
    dense = lambda shape, s=1.0: normal(shape) * (s * shape[0] ** -0.5)
    gain = lambda n: 1.0 + 0.02 * normal((n,))
    small = lambda n, s=0.02: s * normal((n,))

    p = {
        "x": normal((BATCH, SEQ, D_MODEL)),
        "c": normal((BATCH, D_MODEL)),
        "ctx": normal((BATCH, CTX_LEN, D_MODEL)),
        "c_ctx": normal((D_MODEL,)),
    }
    for i in range(DEPTH):
        kind = i % N_MIXERS
        pre = f"l{i}_"
        p[pre + "mod_w"] = dense((D_MODEL, 6 * D_MODEL), 0.5)
        p[pre + "mod_b"] = small(6 * D_MODEL)
        p[pre + "norm1"] = gain(D_MODEL)
        if kind == MIXER_GLA:
            p[pre + "gla_w_in"] = dense((D_MODEL, GLA_IN_DIM))
            p[pre + "gla_wg_f"] = dense((GLA_RANK, GLA_QK))
            p[pre + "gla_bg_f"] = small(GLA_QK, 0.1)
            p[pre + "gla_wg_b"] = dense((GLA_RANK, GLA_QK))
            p[pre + "gla_bg_b"] = small(GLA_QK, 0.1)
            p[pre + "gla_out_norm"] = gain(GLA_DV)
            p[pre + "gla_w_out"] = dense((D_MODEL, D_MODEL))
        elif kind == MIXER_GQA:
            p[pre + "gqa_w_in"] = dense((D_MODEL, GQA_IN_DIM))
            p[pre + "gqa_q_norm"] = gain(HEAD_DIM)
            p[pre + "gqa_k_norm"] = gain(HEAD_DIM)
            p[pre + "gqa_w_out"] = dense((D_MODEL, D_MODEL))
        elif kind == MIXER_DIFF:
            p[pre + "diff_w_in"] = dense((D_MODEL, DIFF_IN_DIM))
            p[pre + "diff_q_norm"] = gain(DIFF_DH)
            p[pre + "diff_k_norm"] = gain(DIFF_DH)
            p[pre + "diff_lq1"] = small(DIFF_DH, 0.1)
            p[pre + "diff_lk1"] = small(DIFF_DH, 0.1)
            p[pre + "diff_lq2"] = small(DIFF_DH, 0.1)
            p[pre + "diff_lk2"] = small(DIFF_DH, 0.1)
            p[pre + "diff_out_norm"] = gain(2 * DIFF_DH)
            p[pre + "diff_w_out"] = dense((D_MODEL, D_MODEL))
        else:
            p[pre + "fnet_w_out"] = dense((D_MODEL, D_MODEL))
        p[pre + "norm2"] = gain(D_MODEL)
        p[pre + "ffn_w_in"] = dense((D_MODEL, 2 * FFN_HIDDEN))
        p[pre + "ffn_w_out"] = dense((FFN_HIDDEN, D_MODEL))
    return p


def reference(x, c, ctx, c_ctx,
              l0_mod_w, l0_mod_b, l0_norm1, l0_gla_w_in, l0_gla_wg_f, l0_gla_bg_f, l0_gla_wg_b, l0_gla_bg_b,
              l0_gla_out_norm, l0_gla_w_out, l0_norm2, l0_ffn_w_in, l0_ffn_w_out,
              l1_mod_w, l1_mod_b, l1_norm1, l1_gqa_w_in, l1_gqa_q_norm, l1_gqa_k_norm, l1_gqa_w_out,
              l1_norm2, l1_ffn_w_in, l1_ffn_w_out,
              l2_mod_w, l2_mod_b, l2_norm1, l2_diff_w_in, l2_diff_q_norm, l2_diff_k_norm, l2_diff_lq1, l2_diff_lk1,
              l2_diff_lq2, l2_diff_lk2, l2_diff_out_norm, l2_diff_w_out, l2_norm2, l2_ffn_w_in, l2_ffn_w_out,
              l3_mod_w, l3_mod_b, l3_norm1, l3_fnet_w_out, l3_norm2, l3_ffn_w_in, l3_ffn_w_out):
    common = [
        (l0_mod_w, l0_mod_b, l0_norm1, l0_norm2, l0_ffn_w_in, l0_ffn_w_out),
        (l1_mod_w, l1_mod_b, l1_norm1, l1_norm2, l1_ffn_w_in, l1_ffn_w_out),
        (l2_mod_w, l2_mod_b, l2_norm1, l2_norm2, l2_ffn_w_in, l2_ffn_w_out),
        (l3_mod_w, l3_mod_b, l3_norm1, l3_norm2, l3_ffn_w_in, l3_ffn_w_out),
    ]
    mixer_params = [
        (l0_gla_w_in, l0_gla_wg_f, l0_gla_bg_f, l0_gla_wg_b, l0_gla_bg_b, l0_gla_out_norm, l0_gla_w_out),
        (l1_gqa_w_in, l1_gqa_q_norm, l1_gqa_k_norm, l1_gqa_w_out),
        (l2_diff_w_in, l2_diff_q_norm, l2_diff_k_norm, l2_diff_lq1, l2_diff_lk1, l2_diff_lq2, l2_diff_lk2,
         l2_diff_out_norm, l2_diff_w_out),
        (l3_fnet_w_out,),
    ]
    x_lat, x_ctx = x, ctx
    s_lat = jax.nn.silu(c)
    s_ctx = jax.nn.silu(c_ctx)
    for i in range(DEPTH):
        kind = i % N_MIXERS
        mod_w, mod_b, n1, n2, f_in, f_out = common[i]
        mp = mixer_params[i]
        ctx_out = any((j % N_MIXERS) != MIXER_FOURIER for j in range(i + 1, DEPTH))
        ctx_in = ctx_out or kind != MIXER_FOURIER
        sh1, sc1, g1, sh2, sc2, g2 = jnp.split((s_lat @ mod_w + mod_b)[:, None, :], 6, axis=-1)
        h_lat = modulate(rms_norm(x_lat, n1), sh1, sc1)
        h_ctx = None
        if ctx_in:
            csh1, csc1, cg1, csh2, csc2, cg2 = jnp.split(s_ctx @ mod_w + mod_b, 6, axis=-1)
            h_ctx = modulate(rms_norm(x_ctx, n1), csh1, csc1)
        if kind == MIXER_GLA:
            y_lat, y_ctx = gla_mixer(h_lat, h_ctx, *mp, ctx_out=ctx_out)
        elif kind == MIXER_GQA:
            y_lat, y_ctx = gqa_mixer(h_lat, h_ctx, *mp, ctx_out=ctx_out)
        elif kind == MIXER_DIFF:
            y_lat, y_ctx = diff_mixer(h_lat, h_ctx, *mp, layer_idx=i, ctx_out=ctx_out)
        else:
            y_lat = fnet_mixer(h_lat, mp[0])
            y_ctx = fnet_mixer(h_ctx, mp[0]) if ctx_out else None
        x_lat = x_lat + g1 * y_lat
        x_lat = x_lat + g2 * swiglu(modulate(rms_norm(x_lat, n2), sh2, sc2), f_in, f_out)
        if ctx_out:
            x_ctx = x_ctx + cg1 * y_ctx
            x_ctx = x_ctx + cg2 * swiglu(modulate(rms_norm(x_ctx, n2), csh2, csc2), f_in, f_out)
    return x_lat
```

```python
import math
from contextlib import ExitStack, contextmanager

import numpy as np
import ml_dtypes

import concourse.bass as bass
import concourse.mybir as mybir
from concourse.bass_utils import run_bass_kernel_spmd

F32 = mybir.dt.float32
BF16 = mybir.dt.bfloat16
AF = mybir.ActivationFunctionType
ALU = mybir.AluOpType
AX = mybir.AxisListType

N_CORES = 8
D = 2048
KD = 16
SEQ = 2048
CTX = 256
NTOK = CTX + SEQ
TB = 1152
FFN_H = 5632
NF = FFN_H // 128
EPS = 1e-6
SEM_LIMIT = 30000


class Dep:
    __slots__ = ("w", "r", "ep")

    def __init__(self):
        self.w = {}
        self.r = {}
        self.ep = -1


class Eng:
    def __init__(self, fw, name, eng):
        self.fw = fw
        self.name = name
        self.eng = eng
        self.sem = None
        self.count = 0
        self.known = {}
        self.nsem = 0
        self.n_inst = 0
        self.n_wait = 0

    def rotate(self):
        self.sem = self.fw.new_sem(f"{self.name}_s{self.nsem}")
        self.nsem += 1
        self.count = 0

    def wait(self, toks):
        for s, v in toks.items():
            if self.known.get(s, 0) < v:
                self.eng.wait_ge(s, v)
                self.known[s] = v
                self.n_wait += 1


class FW:
    def __init__(self, nc, n_dma_slots=24):
        self.nc = nc
        self.stack = ExitStack()
        self.sem_count = 0
        self.epoch = 0
        self.pe = Eng(self, "pe", nc.tensor)
        self.act = Eng(self, "act", nc.scalar)
        self.dve = Eng(self, "dve", nc.vector)
        self.pool = Eng(self, "pool", nc.gpsimd)
        self.sp = Eng(self, "sp", nc.sync)
        self.engs = [self.pe, self.act, self.dve, self.pool, self.sp]
        for e in self.engs:
            e.rotate()
        self.slots = [[self.new_sem(f"dma{i}"), 0] for i in range(n_dma_slots)]
        self.slot_i = 0
        self.n_dma = 0

    def new_sem(self, name):
        self.sem_count += 1
        return self.stack.enter_context(self.nc.semaphore(name))

    def _sync(self, d):
        if d.ep != self.epoch:
            d.w = {}
            d.r = {}
            d.ep = self.epoch

    def _collect(self, reads, writes):
        toks = {}
        for d in reads:
            self._sync(d)
            for s, v in d.w.items():
                if toks.get(s, 0) < v:
                    toks[s] = v
        for d in writes:
            self._sync(d)
            for s, v in d.w.items():
                if toks.get(s, 0) < v:
                    toks[s] = v
            for s, v in d.r.items():
                if toks.get(s, 0) < v:
                    toks[s] = v
        return toks

    def _record(self, tok, reads, writes):
        s, v = tok
        for d in reads:
            if d.r.get(s, 0) < v:
                d.r[s] = v
        for d in writes:
            d.w = {s: v}
            d.r = {}

    def op(self, E, fn, reads=(), writes=()):
        return self.group(E, [fn], reads, writes)

    def group(self, E, fns, reads=(), writes=()):
        toks = self._collect(reads, writes)
        if E is self.pe:
            toks.pop(E.sem, None)
        E.wait(toks)
        ins = None
        for fn in fns:
            ins = fn()
            E.n_inst += 1
        if E.count >= SEM_LIMIT:
            E.rotate()
        ins.then_inc(E.sem, 1)
        E.count += 1
        tok = (E.sem, E.count)
        self._record(tok, reads, writes)
        return tok

    def dma(self, Q, out, in_, reads=(), writes=()):
        toks = self._collect(reads, writes)
        slot = self.slots[self.slot_i]
        self.slot_i = (self.slot_i + 1) % len(self.slots)
        if slot[1] > 0:
            toks[slot[0]] = max(toks.get(slot[0], 0), slot[1])
        Q.wait(toks)
        if slot[1] + 16 > SEM_LIMIT:
            slot[0] = self.new_sem(f"dmar{self.sem_count}")
            slot[1] = 0
        self.nc_dma(Q, out, in_).then_inc(slot[0], 16)
        slot[1] += 16
        self.n_dma += 1
        tok = (slot[0], slot[1])
        self._record(tok, reads, writes)
        return tok

    def nc_dma(self, Q, out, in_):
        return Q.eng.dma_start(out=out, in_=in_)

    def barrier(self):
        toks = {}
        for E in self.engs:
            if E.count > 0:
                toks[E.sem] = E.count
        for s, v in self.slots:
            if v > 0:
                toks[s] = v
        for E in self.engs:
            E.wait(dict(toks))
        self.epoch += 1

    def close(self):
        self.stack.close()


class Ring:
    def __init__(self, items):
        self.items = items
        self.i = 0

    def next(self):
        it = self.items[self.i]
        self.i = (self.i + 1) % len(self.items)
        return it


class Builder:
    def __init__(self, layers=(0, 1, 2, 3), debug_out=None):
        self.layers = list(layers)
        self.nc = bass.Bass("TRN2", target_bir_lowering=False)
        Builder.last = self
        self.fw = FW(self.nc)
        self.gs = ExitStack()
        self.ps = None
        self.inputs = {}
        self.consts = {}

    def din(self, name, shape, dt=F32):
        t = self.nc.dram_tensor(name, list(shape), dt, kind="ExternalInput").ap()
        self.inputs[name] = t
        return t

    def dscr(self, name, shape, dt):
        return self.nc.dram_tensor(name, list(shape), dt, kind="Internal").ap()

    def _uname(self, name):
        self.ucount = getattr(self, "ucount", 0) + 1
        return f"s{self.ucount}_{name}"

    def gsb(self, name, shape, dt):
        return self.gs.enter_context(self.nc.sbuf_tensor(self._uname(name), list(shape), dt)), Dep()

    def sb(self, name, shape, dt):
        return self.ps.enter_context(self.nc.sbuf_tensor(self._uname(name), list(shape), dt)), Dep()

    def ring(self, name, n, shape, dt):
        return Ring([self.sb(f"{name}{i}", shape, dt) for i in range(n)])

    @contextmanager
    def phase(self, name):
        self.fw.barrier()
        with ExitStack() as ps:
            self.ps = ps
            yield
            self.fw.barrier()
        self.ps = None

    def mm(self, out, lhsT, rhs, start, stop):
        nc = self.nc
        return lambda: nc.tensor.matmul(out, lhsT, rhs, start=start, stop=stop)

    def build(self):
        nc, fw = self.nc, self.fw
        L = self.layers
        self.x_in = self.din("x", [2, SEQ, D])
        self.ctx_in = self.din("ctx", [2, CTX, D])
        self.cvecT = self.din("cvecT", [128, KD * 3])
        self.identF_d = self.din("identF", [128, 128])
        self.onesB_d = self.din("onesB", [128, 128], BF16)
        self.W = {}
        for l in L:
            p = f"l{l}_"
            self.W[p + "mod_w"] = self.din(p + "mod_w", [D, 6 * D])
            self.W[p + "mod_b"] = self.din(p + "mod_b", [1, 6 * D])
            self.W[p + "n1T"] = self.din(p + "n1T", [128, KD])
            self.W[p + "n2T"] = self.din(p + "n2T", [128, KD])
            self.W[p + "ffn_w_in"] = self.din(p + "ffn_w_in", [D, 2 * FFN_H])
            self.W[p + "ffn_w_out"] = self.din(p + "ffn_w_out", [FFN_H, D])
        if 0 in L:
            p = "l0_"
            self.W[p + "w_in"] = self.din(p + "w_in", [D, 6176])
            self.W[p + "wgaf"] = self.din(p + "wgaf", [33, 1024])
            self.W[p + "wgab"] = self.din(p + "wgab", [33, 1024])
            self.W[p + "onT"] = self.din(p + "onT", [128, 4])
            self.W[p + "w_out"] = self.din(p + "w_out", [D, D])
            self.gla_c = self.din("gla_c", [128, 1024])
        if 1 in L:
            p = "l1_"
            self.W[p + "w_in"] = self.din(p + "w_in", [D, 3072])
            self.W[p + "qk"] = self.din(p + "qk", [128, 2])
            self.W[p + "w_out"] = self.din(p + "w_out", [D, D])
        if 2 in L:
            p = "l2_"
            self.W[p + "w_in"] = self.din(p + "w_in", [D, 6144])
            self.W[p + "qk"] = self.din(p + "qk", [128, 2])
            self.W[p + "lvec"] = self.din(p + "lvec", [128, 512])
            self.W[p + "onT"] = self.din(p + "onT", [128, 2])
            self.W[p + "w_out"] = self.din(p + "w_out", [D, D])
        if 1 in L or 2 in L:
            self.rope_d = self.din("rope", [128, 2 * SEQ])
            self.RT_d = self.din("RT", [128, 128], BF16)
        if 3 in L:
            p = "l3_"
            self.W[p + "w_out"] = self.din(p + "w_out", [D, D])
            self.dftc_d = self.din("dftc", [128, 2 * 4 * 512], BF16)
            self.dftl_d = self.din("dftl", [128, 2 * 16 * SEQ], BF16)
        self.y_out = nc.dram_tensor("y", [2, SEQ, D], F32, kind="ExternalOutput").ap()

        self.xT = self.dscr("xT", [4, KD, 128, TB], F32)
        self.OT = self.dscr("OT", [4, KD, 128, TB], BF16)
        self.RS = self.dscr("RS", [2, 4, 128, TB], F32)
        self.PA = self.dscr("PA", [2, 32, 128, NTOK], BF16)
        self.VT = self.dscr("VT", [2, NTOK, 3072], BF16)
        if 0 in L:
            self.PAF = self.dscr("PAF", [2, 16, 128, NTOK], F32)
            self.SR = self.dscr("SR", [2, 16, 128, NTOK], BF16)
            self.LT = self.dscr("LT", [2, NTOK, 2048], F32)
            self.OF = self.dscr("OF", [2, 16, 128, NTOK], F32)
        if 3 in L:
            self.AB = self.dscr("AB", [2, 4, 2, 16, 128, 512], BF16)

        self.banks = []
        for i in range(8):
            t = self.gs.enter_context(nc.psum_tensor(f"bank{i}", [128, 512], F32))
            self.banks.append((t, Dep()))
        self.bank_ring = Ring(self.banks)
        self.identF, self.d_ident = self.gsb("identF_s", [128, 128], F32)
        self.onesB, self.d_ones = self.gsb("onesB_s", [128, 128], BF16)
        self.epsc, self.d_epsc = self.gsb("epsc", [128, 2], F32)
        self.coef = {}
        for l in L:
            self.coef[l] = self.gsb(f"coef{l}", [128, 96, 3], F32)

        with nc.Block() as block:
            @block.sync
            def _(sync):
                self.emit_all()
        self.gs.close()
        fw.close()
        return nc

    def emit_all(self):
        nc, fw = self.nc, self.fw
        fw.dma(fw.sp, self.identF[:], self.identF_d, writes=[self.d_ident])
        fw.dma(fw.sp, self.onesB[:], self.onesB_d, writes=[self.d_ones])
        fw.op(fw.dve, lambda: nc.vector.memset(self.epsc[:, 0:1], EPS), writes=[self.d_epsc])
        fw.op(fw.dve, lambda: nc.vector.memset(self.epsc[:, 1:2], 1.0), writes=[self.d_epsc])
        self.phase_mod()
        self.phase_in()
        for l in self.layers:
            kind = l % 4
            ctx_out = l < 2
            if kind == 0:
                self.gla_A(l)
                self.gla_B(l)
            elif kind == 1:
                self.attn_A(l, diff=False)
                self.gqa_B(l)
            elif kind == 2:
                self.attn_A(l, diff=True)
                self.diff_B(l)
            else:
                self.fnet_A(l)
                self.fnet_B(l)
            self.phase_C1(l, ctx_out)
            self.phase_C2(l, ctx_out)
        self.phase_out()

    def phase_mod(self):
        nc, fw = self.nc, self.fw
        with self.phase("mod"):
            cT, d_cT = self.sb("cT", [128, KD, 3], F32)
            sT, d_sT = self.sb("sT", [128, KD, 3], F32)
            wring = self.ring("modw", 2, [128, KD, 512], F32)
            modrow, d_mr = self.sb("modrow", [3, 6 * D], F32)
            nT, d_nT = self.sb("nT", [128, 2, KD], F32)
            fw.dma(fw.sp, cT[:].rearrange("p k r -> p (k r)"), self.cvecT, writes=[d_cT])
            fw.op(fw.act, lambda: nc.scalar.activation(sT[:], cT[:], AF.Silu), reads=[d_cT], writes=[d_sT])
            for l in self.layers:
                p = f"l{l}_"
                mw = self.W[p + "mod_w"].rearrange("(k p) m -> p k m", p=128)
                for r in range(3):
                    fw.dma(fw.sp, modrow[r:r + 1, :], self.W[p + "mod_b"], writes=[d_mr])
                fw.dma(fw.sp, nT[:, 0, :], self.W[p + "n1T"], writes=[d_nT])
                fw.dma(fw.sp, nT[:, 1, :], self.W[p + "n2T"], writes=[d_nT])
                for j in range(24):
                    wt, wd = wring.next()
                    fw.dma(fw.sp, wt[:], mw[:, :, j * 512:(j + 1) * 512], writes=[wd])
                    bank, bd = self.bank_ring.next()
                    fw.group(fw.pe, [self.mm(bank[0:3, :], sT[:, k, :], wt[:, k, :], k == 0, k == KD - 1)
                                     for k in range(KD)], reads=[wd, d_sT], writes=[bd])
                    fw.op(fw.dve, lambda: nc.vector.tensor_tensor(
                        modrow[:, j * 512:(j + 1) * 512], modrow[:, j * 512:(j + 1) * 512], bank[0:3, :], ALU.add),
                        reads=[bd], writes=[d_mr])
                bank, bd = self.bank_ring.next()
                fw.group(fw.pe, [
                    (lambda c=c: nc.tensor.transpose(bank[:, c * 3:(c + 1) * 3], modrow[0:3, c * 128:(c + 1) * 128],
                                                     self.identF[0:3, 0:3]))
                    for c in range(96)], reads=[d_mr, self.d_ident], writes=[bd])
                cf, d_cf = self.coef[l]
                fw.op(fw.act, lambda: nc.scalar.copy(cf[:].rearrange("p c r -> p (c r)"), bank[:, 0:288]),
                      reads=[bd], writes=[d_cf])
                for w, c0 in ((0, 16), (1, 64)):
                    fw.op(fw.dve, lambda w=w, c0=c0: nc.vector.scalar_tensor_tensor(
                        cf[:, c0:c0 + 16, :], cf[:, c0:c0 + 16, :], 1.0,
                        nT[:, w, :].unsqueeze(2).to_broadcast([128, KD, 3]), ALU.add, ALU.mult),
                        reads=[d_nT, d_cf], writes=[d_cf])

    def block_tiles(self, blk):
        b, h = divmod(blk, 2)
        tiles = [(self.ctx_in[b, h * 128:(h + 1) * 128, :], self.y_out[b, 0:128, :], 0, True)]
        for t in range(8):
            r0 = h * 1024 + t * 128
            tiles.append((self.x_in[b, r0:r0 + 128, :], self.y_out[b, r0:r0 + 128, :], 128 + t * 128, False))
        return tiles

    def phase_in(self):
        nc, fw = self.nc, self.fw
        with self.phase("in"):
            xin = self.ring("xin", 3, [128, D], F32)
            xst = self.ring("xst", 3, [128, KD, 128], F32)
            sqt = self.ring("sqt", 3, [128, KD, 128], BF16)
            rs, d_rs = self.sb("rs_in", [128, TB], F32)
            ring5 = Ring(self.banks[0:5])
            sbank = Ring(self.banks[5:8])
            for blk in range(4):
                xv = self.xT[blk].rearrange("k p t -> p k t")
                pending = []
                for (src, _, c0, is_ctx) in self.block_tiles(blk):
                    xi, d_xi = xin.next()
                    fw.dma(fw.sp, xi[:], src, writes=[d_xi])
                    xs, d_xs = xst.next()
                    for g in range(4):
                        bank, bd = ring5.next()
                        fw.group(fw.pe, [
                            (lambda q=q: nc.tensor.transpose(bank[:, q * 128:(q + 1) * 128],
                                                             xi[:, (g * 4 + q) * 128:(g * 4 + q + 1) * 128],
                                                             self.identF[:]))
                            for q in range(4)], reads=[d_xi, self.d_ident], writes=[bd])
                        dst = xs[:, g * 4:(g + 1) * 4, :]
                        srcb = bank[:].rearrange("p (q t) -> p q t", q=4)
                        if g % 2 == 0:
                            fw.op(fw.act, lambda: nc.scalar.copy(dst, srcb), reads=[bd], writes=[d_xs])
                        else:
                            fw.op(fw.dve, lambda: nc.vector.tensor_copy(dst, srcb), reads=[bd], writes=[d_xs])
                    while pending:
                        pending.pop(0)()
                    fw.dma(fw.sp, xv[:, :, c0:c0 + 128], xs[:], reads=[d_xs], writes=[Dep()])
                    sq, d_sq = sqt.next()
                    fw.op(fw.act, lambda: nc.scalar.activation(sq[:], xs[:], AF.Square), reads=[d_xs], writes=[d_sq])

                    def stat(sq=sq, d_sq=d_sq, c0=c0):
                        sb_, d_sb = sbank.next()
                        fw.group(fw.pe, [self.mm(sb_[:, 0:128], self.onesB[:], sq[:, k, :], k == 0, k == KD - 1)
                                         for k in range(KD)], reads=[d_sq, self.d_ones], writes=[d_sb])
                        fw.op(fw.act, lambda: nc.scalar.activation(rs[:, c0:c0 + 128], sb_[:, 0:128], AF.Ln,
                                                                   bias=self.epsc[:, 0:1], scale=1.0 / D),
                              reads=[d_sb, self.d_epsc], writes=[d_rs])
                    pending.append(stat)
                while pending:
                    pending.pop(0)()
                fw.op(fw.act, lambda: nc.scalar.activation(rs[:], rs[:], AF.Exp, scale=-0.5), reads=[d_rs], writes=[d_rs])
                fw.dma(fw.sp, self.RS[0, blk], rs[:], reads=[d_rs], writes=[Dep()])

    def phase_out(self):
        nc, fw = self.nc, self.fw
        with self.phase("out"):
            xin = self.ring("xo_in", 2, [128, KD, 128], F32)
            xst = self.ring("xo_st", 2, [128, D], F32)
            outs = []
            for blk in range(4):
                xv = self.xT[blk].rearrange("k p t -> p k t")
                for (_, dst_d, c0, is_ctx) in self.block_tiles(blk):
                    if is_ctx:
                        continue
                    xi, d_xi = xin.next()
                    fw.dma(fw.sp, xi[:], xv[:, :, c0:c0 + 128], writes=[d_xi])
                    xs, d_xs = xst.next()
                    for g in range(4):
                        bank, bd = self.bank_ring.next()
                        fw.group(fw.pe, [
                            (lambda q=q: nc.tensor.transpose(bank[:, q * 128:(q + 1) * 128],
                                                             xi[:, g * 4 + q, :], self.identF[:]))
                            for q in range(4)], reads=[d_xi, self.d_ident], writes=[bd])
                        dst = xs[:, g * 512:(g + 1) * 512]
                        if g % 2 == 0:
                            fw.op(fw.act, lambda: nc.scalar.copy(dst, bank[:]), reads=[bd], writes=[d_xs])
                        else:
                            fw.op(fw.dve, lambda: nc.vector.tensor_copy(dst, bank[:]), reads=[bd], writes=[d_xs])
                    dd = Dep()
                    fw.dma(fw.sp, dst_d, xs[:], reads=[d_xs], writes=[dd])
                    outs.append(dd)

    @staticmethod
    def subs(ctx):
        return ([(0, 128)] if ctx else []) + [(128, 640), (640, TB)]

    def regions(self, blk, ctx):
        b = blk // 2
        return ([(0, 128, 2)] if ctx else []) + [(128, TB, b)]

    def alloc_norm(self, n_xc=3):
        self.xc_ring = self.ring("xc", n_xc, [128, TB], F32)
        self.tmp_ring = self.ring("tmpf", 2, [128, TB], F32)
        self.rstd, self.d_rstd = self.sb("rstd", [128, TB], F32)

    def norm_mod(self, l, blk, which, ctx, hT, d_hT):
        nc, fw = self.nc, self.fw
        cf, d_cf = self.coef[l]
        cB, cA = (0, 16) if which == 0 else (48, 64)
        subs = self.subs(ctx)
        a0 = subs[0][0]
        fw.dma(fw.sp, self.rstd[:, a0:TB], self.RS[which, blk, :, a0:TB], writes=[self.d_rstd])
        for k in range(KD):
            xc, d_xc = self.xc_ring.next()
            fw.dma(fw.sp, xc[:, a0:TB], self.xT[blk, k, :, a0:TB], writes=[d_xc])
            tm, d_tm = self.tmp_ring.next()
            fw.op(fw.dve, lambda: nc.vector.tensor_tensor(tm[:, a0:TB], xc[:, a0:TB], self.rstd[:, a0:TB], ALU.mult),
                  reads=[d_xc, self.d_rstd], writes=[d_tm])
            for (r0, r1, r) in self.regions(blk, ctx):
                fw.op(fw.act, lambda: nc.scalar.activation(hT[:, k, r0:r1], tm[:, r0:r1], AF.Identity,
                                                           bias=cf[:, cB + k, r:r + 1], scale=cf[:, cA + k, r:r + 1]),
                      reads=[d_tm, d_cf], writes=[d_hT])

    def stats_begin(self, subs):
        return {"subs": subs, "banks": [self.banks[5 + i] for i in range(len(subs))], "pending": []}

    def stats_push(self, st, src, d_src, k, si, sq, d_sq):
        nc, fw = self.nc, self.fw
        s0, s1 = st["subs"][si]
        fw.op(fw.act, lambda: nc.scalar.activation(sq[:, s0:s1], src[:, s0:s1], AF.Square), reads=[d_src], writes=[d_sq])
        bk, bd = st["banks"][si]
        st["pending"].append(lambda: fw.group(
            fw.pe, [self.mm(bk[:, 0:s1 - s0], self.onesB[:], sq[:, s0:s1], k == 0, k == KD - 1)],
            reads=[d_sq, self.d_ones], writes=[bd]))

    def stats_flush(self, st):
        while st["pending"]:
            st["pending"].pop(0)()

    def stats_end(self, st, which, blk, rs, d_rs):
        nc, fw = self.nc, self.fw
        self.stats_flush(st)
        a0 = st["subs"][0][0]
        for (bk, bd), (s0, s1) in zip(st["banks"], st["subs"]):
            fw.op(fw.act, lambda: nc.scalar.activation(rs[:, s0:s1], bk[:, 0:s1 - s0], AF.Ln,
                                                       bias=self.epsc[:, 0:1], scale=1.0 / D),
                  reads=[bd, self.d_epsc], writes=[d_rs])
        fw.op(fw.act, lambda: nc.scalar.activation(rs[:, a0:TB], rs[:, a0:TB], AF.Exp, scale=-0.5),
              reads=[d_rs], writes=[d_rs])
        fw.dma(fw.sp, self.RS[which, blk, :, a0:TB], rs[:, a0:TB], reads=[d_rs], writes=[Dep()])

    def pipeline(self, gens):
        def step(g):
            try:
                next(g)
                return True
            except StopIteration:
                return False
        active = []
        for g in gens:
            alive = step(g)
            active = [a for a in active if step(a)]
            if alive:
                active.append(g)
        while active:
            active = [a for a in active if step(a)]

    def linear(self, hT, d_hT, kd, wview, cols, subs, epilogue, wring, mw=128):
        nc, fw = self.nc, self.fw
        wts = {}
        PF = 2

        def load(fi):
            if fi < len(cols) and fi not in wts:
                wts[fi] = wring.next()
                fw.dma(fw.pool, wts[fi][0][:, 0:kd, 0:mw], wview[:, :, cols[fi]:cols[fi] + mw], writes=[wts[fi][1]])

        def item(fi, c0, si, s0, s1):
            if si == 0:
                for j in range(fi, fi + PF + 1):
                    load(j)
            wt, wd = wts[fi]
            bank, bd = self.bank_ring.next()
            fw.group(fw.pe, [self.mm(bank[0:mw, 0:s1 - s0], wt[:, k, 0:mw], hT[:, k, s0:s1], k == 0, k == kd - 1)
                             for k in range(kd)], reads=[wd, d_hT], writes=[bd])
            r = epilogue(fi, si, (s0, s1), bank, bd)
            if r is not None:
                yield from r

        self.pipeline(item(fi, c0, si, s0, s1) for fi, c0 in enumerate(cols) for si, (s0, s1) in enumerate(subs))

    def linear_tok(self, hT, d_hT, wview, col_groups, tok_tiles, epilogue, wtring):
        nc, fw = self.nc, self.fw
        for gi, c0 in enumerate(col_groups):
            wt, wd = wtring.next()
            fw.dma(fw.pool, wt[:], wview[:, :, c0:c0 + 512], writes=[wd])
            for ti, t0 in enumerate(tok_tiles):
                bank, bd = self.bank_ring.next()
                fw.group(fw.pe, [self.mm(bank[:, :], hT[:, k, t0:t0 + 128], wt[:, k, :], k == 0, k == KD - 1)
                                 for k in range(KD)], reads=[wd, d_hT], writes=[bd])
                epilogue(gi, ti, t0, bank, bd)

    @staticmethod
    def nat(blk, col):
        h = blk % 2
        if col < 128:
            return h * 128 + col
        return CTX + h * 1024 + (col - 128)

    def phase_C1(self, l, ctx):
        nc, fw = self.nc, self.fw
        cf, d_cf = self.coef[l]
        wv = self.W[f"l{l}_w_out"].rearrange("(k p) m -> p k m", p=128)
        with self.phase("C1"):
            oT, d_oT = self.sb("oT", [128, KD, TB], BF16)
            wring = self.ring("wC1", 4, [128, KD, 128], BF16)
            xc_ring = self.ring("xc1", 3, [128, TB], F32)
            xo_ring = self.ring("xo1", 3, [128, TB], F32)
            sq_ring = self.ring("sq1", 3, [128, TB], BF16)
            rs, d_rs = self.sb("rs1", [128, TB], F32)
            subs = self.subs(ctx)
            a0 = subs[0][0]
            saved_ring = self.bank_ring
            self.bank_ring = Ring(self.banks[0:5])
            for blk in range(4):
                b = blk // 2
                fw.dma(fw.sp, oT[:, :, a0:TB], self.OT[blk].rearrange("k p t -> p k t")[:, :, a0:TB], writes=[d_oT])
                state = {}
                st = self.stats_begin(subs)

                def epi(fi, si, rng, bank, bd):
                    s0, s1 = rng
                    if si == 0:
                        self.stats_flush(st)
                        state["xc"] = xc_ring.next()
                        state["xo"] = xo_ring.next()
                        state["sq"] = sq_ring.next()
                        fw.dma(fw.sp, state["xc"][0][:, a0:TB], self.xT[blk, fi, :, a0:TB], writes=[state["xc"][1]])
                    xc, d_xc = state["xc"]
                    xo, d_xo = state["xo"]
                    r = 2 if s1 <= 128 else b
                    fw.op(fw.dve, lambda: nc.vector.scalar_tensor_tensor(
                        xo[:, s0:s1], bank[:, 0:s1 - s0], cf[:, 32 + fi, r:r + 1], xc[:, s0:s1], ALU.mult, ALU.add),
                        reads=[bd, d_xc, d_cf], writes=[d_xo])
                    self.stats_push(st, xo, d_xo, fi, si, state["sq"][0], state["sq"][1])
                    if si == len(subs) - 1:
                        fw.dma(fw.sp, self.xT[blk, fi, :, a0:TB], xo[:, a0:TB], reads=[d_xo], writes=[Dep()])

                self.linear(oT, d_oT, KD, wv, [c * 128 for c in range(KD)], subs, epi, wring)
                self.stats_end(st, 1, blk, rs, d_rs)
            self.bank_ring = saved_ring

    def phase_C2(self, l, ctx):
        nc, fw = self.nc, self.fw
        cf, d_cf = self.coef[l]
        w_in = self.W[f"l{l}_ffn_w_in"].rearrange("(k p) m -> p k m", p=128)
        w_out = self.W[f"l{l}_ffn_w_out"].rearrange("(f p) m -> p f m", p=128)
        with self.phase("C2"):
            self.alloc_norm(n_xc=2)
            sq_ring = self.ring("sq2", 2, [128, TB], BF16)
            saved_ring = self.bank_ring
            self.bank_ring = Ring(self.banks[0:5])
            make_stats = l != self.layers[-1]
            hT, d_hT = self.sb("hT", [128, KD, TB], BF16)
            act, _ = self.sb("act", [128, NF, TB], BF16)
            wring = self.ring("wffn", 4, [128, KD, 128], BF16)
            woring = self.ring("wffo", 3, [128, NF // 2, 128], BF16)
            sg_ring = self.ring("sg", 2, [128, 512], F32)
            subs = self.subs(ctx)
            a0 = subs[0][0]
            self.norm_mod(l, 0, 1, ctx, hT, d_hT)
            for blk in range(4):
                b = blk // 2
                d_act = [Dep() for _ in range(NF)]
                for f in range(NF):
                    wg, d_wg = wring.next()
                    wu, d_wu = wring.next()
                    fw.dma(fw.pool, wg[:], w_in[:, :, f * 128:(f + 1) * 128], writes=[d_wg])
                    fw.dma(fw.pool, wu[:], w_in[:, :, FFN_H + f * 128:FFN_H + (f + 1) * 128], writes=[d_wu])
                    for (s0, s1) in subs:
                        n = s1 - s0
                        bg, d_bg = self.bank_ring.next()
                        bu, d_bu = self.bank_ring.next()
                        fw.group(fw.pe, [self.mm(bg[:, 0:n], wg[:, k, :], hT[:, k, s0:s1], k == 0, k == KD - 1)
                                         for k in range(KD)], reads=[d_wg, d_hT], writes=[d_bg])
                        fw.group(fw.pe, [self.mm(bu[:, 0:n], wu[:, k, :], hT[:, k, s0:s1], k == 0, k == KD - 1)
                                         for k in range(KD)], reads=[d_wu, d_hT], writes=[d_bu])
                        sg, d_sg = sg_ring.next()
                        fw.op(fw.act, lambda: nc.scalar.activation(sg[:, 0:n], bg[:, 0:n], AF.Silu),
                              reads=[d_bg], writes=[d_sg])
                        fw.op(fw.dve, lambda: nc.vector.tensor_tensor(act[:, f, s0:s1], sg[:, 0:n], bu[:, 0:n], ALU.mult),
                              reads=[d_sg, d_bu], writes=[d_act[f]])
                if blk < 3:
                    self.norm_mod(l, blk + 1, 1, ctx, hT, d_hT)
                H = NF // 2
                st = self.stats_begin(subs)
                for dch in range(KD):
                    wh = []
                    for hf in range(2):
                        wt, d_wt = woring.next()
                        fw.dma(fw.pool, wt[:], w_out[:, hf * H:(hf + 1) * H, dch * 128:(dch + 1) * 128], writes=[d_wt])
                        wh.append((wt, d_wt))
                    xc, d_xc = self.xc_ring.next()
                    fw.dma(fw.sp, xc[:, a0:TB], self.xT[blk, dch, :, a0:TB], writes=[d_xc])
                    xo, d_xo = self.tmp_ring.next()
                    obanks = [self.bank_ring.next() for _ in subs]
                    for hf in range(2):
                        wt, d_wt = wh[hf]
                        for (bank, bd), (s0, s1) in zip(obanks, subs):
                            n = s1 - s0
                            fw.group(fw.pe, [self.mm(bank[:, 0:n], wt[:, f, :], act[:, hf * H + f, s0:s1],
                                                     hf == 0 and f == 0, hf == 1 and f == H - 1) for f in range(H)],
                                     reads=[d_wt] + d_act[hf * H:(hf + 1) * H], writes=[bd])
                    self.stats_flush(st)
                    sq, d_sq = sq_ring.next()
                    for si, ((bank, bd), (s0, s1)) in enumerate(zip(obanks, subs)):
                        n = s1 - s0
                        r = 2 if s1 <= 128 else b
                        fw.op(fw.dve, lambda: nc.vector.scalar_tensor_tensor(
                            xo[:, s0:s1], bank[:, 0:n], cf[:, 80 + dch, r:r + 1], xc[:, s0:s1], ALU.mult, ALU.add),
                            reads=[bd, d_xc, d_cf], writes=[d_xo])
                        if make_stats:
                            self.stats_push(st, xo, d_xo, dch, si, sq, d_sq)
                    fw.dma(fw.sp, self.xT[blk, dch, :, a0:TB], xo[:, a0:TB], reads=[d_xo], writes=[Dep()])
                if make_stats:
                    rs, d_rs = self.tmp_ring.next()
                    self.stats_end(st, 0, blk, rs, d_rs)
            self.bank_ring = saved_ring

    def fnet_A(self, l):
        nc, fw = self.nc, self.fw
        with self.phase("fnetA"):
            self.alloc_norm()
            hT, d_hT = self.sb("hT", [128, KD, TB], BF16)
            dc, d_dc = self.sb("dftc", [128, 2, 4, 512], BF16)
            st_ring = self.ring("abst", 4, [128, 512], BF16)
            fw.dma(fw.sp, dc[:].rearrange("p a k m -> p (a k m)"), self.dftc_d, writes=[d_dc])
            for blk in range(4):
                b, h = divmod(blk, 2)
                self.norm_mod(l, blk, 0, False, hT, d_hT)
                for t in range(8):
                    c0 = 128 + t * 128
                    tt = h * 8 + t
                    for g in range(4):
                        for a in range(2):
                            bank, bd = self.bank_ring.next()
                            fw.group(fw.pe, [self.mm(bank[:, :], hT[:, 4 * g + kc, c0:c0 + 128], dc[:, a, kc, :],
                                                     kc == 0, kc == 3) for kc in range(4)],
                                     reads=[d_hT, d_dc], writes=[bd])
                            st, d_st = st_ring.next()
                            if a == 0:
                                fw.op(fw.act, lambda: nc.scalar.copy(st[:], bank[:]), reads=[bd], writes=[d_st])
                            else:
                                fw.op(fw.dve, lambda: nc.vector.tensor_scalar(st[:], bank[:], -1.0, None, ALU.mult),
                                      reads=[bd], writes=[d_st])
                            fw.dma(fw.sp, self.AB[b, g, a, tt], st[:], reads=[d_st], writes=[Dep()])

    def fnet_B(self, l):
        nc, fw = self.nc, self.fw
        with self.phase("fnetB"):
            dl, d_dl = self.sb("dftl", [128, 2, 16, SEQ], BF16)
            ab_ring = self.ring("ab", 2, [128, 2, 16, 512], BF16)
            st_ring = self.ring("yst", 3, [128, 512], BF16)
            for a in range(2):
                for q in range(4):
                    fw.dma(fw.sp, dl[:, a, q * 4:(q + 1) * 4, :].rearrange("p k m -> p (k m)"),
                           self.dftl_d[:, (a * 16 + q * 4) * SEQ:(a * 16 + q * 4 + 4) * SEQ], writes=[d_dl])
            for b in range(2):
                for g in range(4):
                    ab, d_ab = ab_ring.next()
                    for a in range(2):
                        fw.dma(fw.sp, ab[:, a], self.AB[b, g, a].rearrange("t p c -> p t c"), writes=[d_ab])
                    for cc in range(4):
                        for tb in range(4):
                            bank, bd = self.bank_ring.next()
                            fns = []
                            for a in range(2):
                                for tt in range(16):
                                    fns.append(self.mm(bank[:, :], ab[:, a, tt, cc * 128:(cc + 1) * 128],
                                                       dl[:, a, tt, tb * 512:(tb + 1) * 512],
                                                       a == 0 and tt == 0, a == 1 and tt == 15))
                            fw.group(fw.pe, fns, reads=[d_ab, d_dl], writes=[bd])
                            st, d_st = st_ring.next()
                            if (cc + tb) % 2 == 0:
                                fw.op(fw.act, lambda: nc.scalar.copy(st[:], bank[:]), reads=[bd], writes=[d_st])
                            else:
                                fw.op(fw.dve, lambda: nc.vector.tensor_copy(st[:], bank[:]), reads=[bd], writes=[d_st])
                            blk = b * 2 + tb // 2
                            col = 128 + (tb % 2) * 512
                            fw.dma(fw.sp, self.OT[blk, 4 * g + cc, :, col:col + 512], st[:], reads=[d_st], writes=[Dep()])

    def attn_A(self, l, diff):
        nc, fw = self.nc, self.fw
        p = f"l{l}_"
        wv = self.W[p + "w_in"].rearrange("(k p) m -> p k m", p=128)
        nq = 16
        nk = 16 if diff else 4
        v0 = (nq + nk) * 128
        nvg = 4 if diff else 1
        with self.phase("attnA"):
            self.alloc_norm()
            hT, d_hT = self.sb("hT", [128, KD, TB], BF16)
            rope, d_rope = self.sb("rope", [128, 2, SEQ], F32)
            RT, d_RT = self.sb("RT", [128, 128], BF16)
            qk, d_qk = self.sb("qk", [128, 2], F32)
            wring = self.ring("wA", 4, [128, KD, 128], BF16)
            wtring = self.ring("wtA", 2, [128, KD, 512], BF16)
            sq_r = self.ring("sqA", 3, [128, 512], BF16)
            raw_r = self.ring("rawA", 2, [128, 512], F32)
            t_r = self.ring("tA", 2, [128, 512], F32)
            qn_r = self.ring("qnA", 3, [128, 512], BF16)
            t1_r = self.ring("t1A", 2, [128, 512], F32)
            t2_r = self.ring("t2A", 2, [128, 512], F32)
            qf_r = self.ring("qfA", 3, [128, 512], BF16)
            vst_r = self.ring("vstA", 3, [128, 512], BF16)
            fw.dma(fw.sp, rope[:].rearrange("p a t -> p (a t)"), self.rope_d, writes=[d_rope])
            fw.dma(fw.sp, RT[:], self.RT_d, writes=[d_RT])
            fw.dma(fw.sp, qk[:], self.W[p + "qk"], writes=[d_qk])
            for blk in range(4):
                b, h = divmod(blk, 2)
                self.norm_mod(l, blk, 0, True, hT, d_hT)

                def epi(fi, si, rng, bank, bd):
                    s0, s1 = rng
                    n = s1 - s0
                    is_q = fi < nq
                    is_ctx = s1 <= 128
                    if diff and is_q and is_ctx:
                        return
                    g = qk[:, 0:1] if is_q else qk[:, 1:2]
                    sq, d_sq = sq_r.next()
                    fw.op(fw.act, lambda: nc.scalar.activation(sq[:, 0:n], bank[:, 0:n], AF.Square), reads=[bd], writes=[d_sq])
                    yield
                    ssb, d_ssb = self.bank_ring.next()
                    fw.group(fw.pe, [self.mm(ssb[:, 0:n], self.onesB[:], sq[:, 0:n], True, True)],
                             reads=[d_sq, self.d_ones], writes=[d_ssb])
                    t, d_t = t_r.next()
                    fw.op(fw.act, lambda: nc.scalar.activation(t[:, 0:n], ssb[:, 0:n], AF.Ln, bias=self.epsc[:, 0:1],
                                                               scale=1.0 / 128), reads=[d_ssb, self.d_epsc], writes=[d_t])
                    fw.op(fw.act, lambda: nc.scalar.activation(t[:, 0:n], t[:, 0:n], AF.Exp, scale=-0.5), reads=[d_t], writes=[d_t])
                    nat0 = self.nat(blk, s0)
                    if is_ctx:
                        qf, d_qf = qf_r.next()
                        fw.op(fw.dve, lambda: nc.vector.scalar_tensor_tensor(qf[:, 0:n], bank[:, 0:n], g, t[:, 0:n],
                                                                             ALU.mult, ALU.mult),
                              reads=[bd, d_t, d_qk], writes=[d_qf])
                        fw.dma(fw.sp, self.PA[b, fi, :, nat0:nat0 + n], qf[:, 0:n], reads=[d_qf], writes=[Dep()])
                        return
                    qn, d_qn = qn_r.next()
                    fw.op(fw.dve, lambda: nc.vector.scalar_tensor_tensor(qn[:, 0:n], bank[:, 0:n], g, t[:, 0:n],
                                                                         ALU.mult, ALU.mult),
                          reads=[bd, d_t, d_qk], writes=[d_qn])
                    yield
                    rb, d_rb = self.bank_ring.next()
                    fw.group(fw.pe, [self.mm(rb[:, 0:n], RT[:], qn[:, 0:n], True, True)], reads=[d_qn, d_RT], writes=[d_rb])
                    lt0 = nat0 - CTX
                    t1, d_t1 = t1_r.next()
                    fw.op(fw.pool, lambda: nc.gpsimd.tensor_tensor(t1[:, 0:n], qn[:, 0:n], rope[:, 0, lt0:lt0 + n], ALU.mult),
                          reads=[d_qn, d_rope], writes=[d_t1])
                    t2, d_t2 = t2_r.next()
                    fw.op(fw.dve, lambda: nc.vector.tensor_tensor(t2[:, 0:n], rb[:, 0:n], rope[:, 1, lt0:lt0 + n], ALU.mult),
                          reads=[d_rb, d_rope], writes=[d_t2])
                    qf, d_qf = qf_r.next()
                    fw.op(fw.pool, lambda: nc.gpsimd.tensor_tensor(qf[:, 0:n], t1[:, 0:n], t2[:, 0:n], ALU.add),
                          reads=[d_t1, d_t2], writes=[d_qf])
                    fw.dma(fw.sp, self.PA[b, fi, :, nat0:nat0 + n], qf[:, 0:n], reads=[d_qf], writes=[Dep()])

                self.linear(hT, d_hT, KD, wv, [c * 128 for c in range(nq + nk)], self.subs(True), epi, wring)

                def epi_v(gi, ti, t0, bank, bd):
                    vs, d_vs = vst_r.next()
                    if ti % 2 == 0:
                        fw.op(fw.act, lambda: nc.scalar.copy(vs[:], bank[:]), reads=[bd], writes=[d_vs])
                    else:
                        fw.op(fw.dve, lambda: nc.vector.tensor_copy(vs[:], bank[:]), reads=[bd], writes=[d_vs])
                    n0 = self.nat(blk, t0)
                    fw.dma(fw.sp, self.VT[b, n0:n0 + 128, gi * 512:(gi + 1) * 512], vs[:], reads=[d_vs], writes=[Dep()])

                self.linear_tok(hT, d_hT, wv, [v0 + g * 512 for g in range(nvg)], [t * 128 for t in range(9)], epi_v, wtring)

    def qblocks(self, b, n_lat, with_ctx):
        out = []
        if with_ctx:
            out.append((0, 256, [0, 1], [(b * 2, 0, 0, 128), (b * 2 + 1, 0, 128, 128)]))
        for j in range(SEQ // n_lat):
            lt = j * n_lat
            out.append((CTX + lt, n_lat, list(range(18)), [(b * 2 + lt // 1024, 128 + lt % 1024, 0, n_lat)]))
        return out

    def gqa_B(self, l):
        nc, fw = self.nc, self.fw
        scale = 128 ** -0.5
        with self.phase("gqaB"):
            k_r = self.ring("kB", 2, [128, NTOK], BF16)
            v_r = self.ring("vB", 2, [128, 18, 128], BF16)
            q_r = self.ring("qB", 3, [128, 512], BF16)
            e_r = self.ring("eB", 4, [128, 512], BF16)
            rd_r = self.ring("rdB", 2, [128, 512], F32)
            o_r = self.ring("oB", 2, [128, 512], BF16)
            st_r = Ring([self.banks[0], self.banks[1], self.banks[2], self.banks[7]])
            o_b = Ring([self.banks[3], self.banks[5]])
            d_b = Ring([self.banks[4], self.banks[6]])

            def qblock_items(b, kT, d_kT, Vg, d_V, hq, q0, n, kts, dsts):
                cx = {}
                last = len(kts) - 1

                def item(i):
                    kt = kts[i]
                    if i == 0:
                        cx["q"] = q_r.next()
                        fw.dma(fw.sp, cx["q"][0][:, 0:n], self.PA[b, hq, :, q0:q0 + n], writes=[cx["q"][1]])
                        cx["O"] = o_b.next()
                        cx["D"] = d_b.next()
                    q, d_q = cx["q"]
                    O, d_O = cx["O"]
                    Dn, d_D = cx["D"]
                    bk, bd = st_r.next()
                    fw.group(fw.pe, [self.mm(bk[:, 0:n], kT[:, kt * 128:(kt + 1) * 128], q[:, 0:n], True, True)],
                             reads=[d_kT, d_q], writes=[bd])
                    yield
                    E, d_E = e_r.next()
                    fw.op(fw.act, lambda: nc.scalar.activation(E[:, 0:n], bk[:, 0:n], AF.Exp, scale=scale),
                          reads=[bd], writes=[d_E])
                    yield
                    fw.group(fw.pe, [self.mm(O[:, 0:n], Vg[:, kt, :], E[:, 0:n], i == 0, i == last),
                                     self.mm(Dn[:, 0:n], self.onesB[:], E[:, 0:n], i == 0, i == last)],
                             reads=[d_E, d_V, self.d_ones], writes=[d_O, d_D])
                    if i == last:
                        rd, d_rd = rd_r.next()
                        fw.op(fw.dve, lambda: nc.vector.reciprocal(rd[:, 0:n], Dn[:, 0:n]), reads=[d_D], writes=[d_rd])
                        o, d_o = o_r.next()
                        fw.op(fw.dve, lambda: nc.vector.tensor_tensor(o[:, 0:n], O[:, 0:n], rd[:, 0:n], ALU.mult),
                              reads=[d_O, d_rd], writes=[d_o])
                        for (blk, col, off, ln) in dsts:
                            fw.dma(fw.sp, self.OT[blk, hq, :, col:col + ln], o[:, off:off + ln], reads=[d_o], writes=[Dep()])

                return [item(i) for i in range(len(kts))]

            def gens():
                for b in range(2):
                    for g in range(4):
                        kT, d_kT = k_r.next()
                        fw.dma(fw.sp, kT[:], self.PA[b, 16 + g], writes=[d_kT])
                        Vg, d_V = v_r.next()
                        fw.dma(fw.sp, Vg[:], self.VT[b, :, g * 128:(g + 1) * 128].rearrange("(t p) d -> p t d", p=128), writes=[d_V])
                        for j in range(4):
                            for (q0, n, kts, dsts) in self.qblocks(b, 512, True):
                                for it in qblock_items(b, kT, d_kT, Vg, d_V, 4 * g + j, q0, n, kts, dsts):
                                    yield it

            self.pipeline(gens())

    def diff_B(self, l):
        nc, fw = self.nc, self.fw
        scale = 128 ** -0.5
        lam_init = 0.8 - 0.6 * math.exp(-0.3 * l)
        p = f"l{l}_"
        with self.phase("diffB"):
            lv, d_lv = self.sb("lv", [128, 512], F32)
            lt, d_lt = self.sb("ltmp", [128, 2, 128], F32)
            ls, d_ls = self.sb("ls", [128, 4], F32)
            on, d_on = self.sb("on", [128, 2], F32)
            fw.dma(fw.sp, lv[:], self.W[p + "lvec"], writes=[d_lv])
            fw.dma(fw.sp, on[:], self.W[p + "onT"], writes=[d_on])
            for m in range(2):
                fw.op(fw.dve, lambda: nc.vector.tensor_tensor(lt[:, m, :], lv[:, m * 256:m * 256 + 128],
                                                              lv[:, m * 256 + 128:m * 256 + 256], ALU.mult),
                      reads=[d_lv], writes=[d_lt])
                fw.op(fw.dve, lambda: nc.vector.reduce_sum(ls[:, m:m + 1], lt[:, m, :], axis=AX.X), reads=[d_lt], writes=[d_ls])
            fw.op(fw.act, lambda: nc.scalar.activation(ls[:, 0:2], ls[:, 0:2], AF.Exp), reads=[d_ls], writes=[d_ls])
            fw.op(fw.dve, lambda: nc.vector.tensor_tensor(ls[:, 2:3], ls[:, 1:2], ls[:, 0:1], ALU.subtract), reads=[d_ls], writes=[d_ls])
            fw.op(fw.dve, lambda: nc.vector.tensor_scalar(ls[:, 3:4], ls[:, 2:3], -lam_init, None, ALU.add), reads=[d_ls], writes=[d_ls])
            fw.op(fw.dve, lambda: nc.vector.tensor_scalar(on[:], on[:], 1.0 - lam_init, None, ALU.mult), reads=[d_on], writes=[d_on])
            neglam = ls[:, 3:4]
            k_r = self.ring("kD", 2, [128, 2, NTOK], BF16)
            v_r = self.ring("vD", 2, [128, 18, 256], BF16)
            q_r = self.ring("qD", 3, [128, 2, 256], BF16)
            e_r = self.ring("eD", 4, [128, 512], BF16)
            rd_r = self.ring("rdD", 2, [128, 512], F32)
            t1_r = self.ring("t1D", 2, [128, 2, 256], F32)
            t2_r = self.ring("t2D", 2, [128, 2, 256], F32)
            sq_r = self.ring("sqD", 2, [128, 2, 256], BF16)
            rs_r = self.ring("rsD", 2, [128, 256], F32)
            o_r = self.ring("oD", 2, [128, 2, 256], BF16)
            st_r = Ring(self.banks[0:2])
            ssb, d_ssb = self.banks[2]
            Ob = [self.banks[3], self.banks[4], self.banks[5], self.banks[6]]
            Dn, d_D = self.banks[7]

            def qblock_items(b, h, kT, d_kT, Vh, d_V, q0, n, kts, dsts):
                cx = {}
                last = len(kts) - 1

                def item(i):
                    kt = kts[i]
                    if i == 0:
                        cx["q"] = q_r.next()
                        fw.dma(fw.sp, cx["q"][0][:], self.PA[b, 2 * h:2 * h + 2, :, q0:q0 + n].rearrange("c p t -> p c t"),
                               writes=[cx["q"][1]])
                    q, d_q = cx["q"]
                    bk, bd = st_r.next()
                    fw.group(fw.pe, [self.mm(bk[:, m * 256:(m + 1) * 256], kT[:, m, kt * 128:(kt + 1) * 128], q[:, m, :], True, True)
                                     for m in range(2)], reads=[d_kT, d_q], writes=[bd])
                    yield
                    E, d_E = e_r.next()
                    fw.op(fw.act, lambda: nc.scalar.activation(E[:], bk[:], AF.Exp, scale=scale), reads=[bd], writes=[d_E])
                    yield
                    fns = []
                    for m in range(2):
                        for dv in range(2):
                            fns.append(self.mm(Ob[m * 2 + dv][0][:, 0:256], Vh[:, kt, dv * 128:(dv + 1) * 128],
                                               E[:, m * 256:(m + 1) * 256], i == 0, i == last))
                    fns.append(self.mm(Dn[:, :], self.onesB[:], E[:], i == 0, i == last))
                    fw.group(fw.pe, fns, reads=[d_E, d_V, self.d_ones], writes=[x[1] for x in Ob] + [d_D])
                    if i != last:
                        return
                    rd, d_rd = rd_r.next()
                    fw.op(fw.dve, lambda: nc.vector.reciprocal(rd[:], Dn[:]), reads=[d_D], writes=[d_rd])
                    t1, d_t1 = t1_r.next()
                    t2, d_t2 = t2_r.next()
                    for dv in range(2):
                        fw.op(fw.dve, lambda: nc.vector.tensor_tensor(t1[:, dv, :], Ob[dv][0][:, 0:256], rd[:, 0:256], ALU.mult),
                              reads=[Ob[dv][1], d_rd], writes=[d_t1])
                        fw.op(fw.dve, lambda: nc.vector.tensor_tensor(t2[:, dv, :], Ob[2 + dv][0][:, 0:256], rd[:, 256:512], ALU.mult),
                              reads=[Ob[2 + dv][1], d_rd], writes=[d_t2])
                    fw.op(fw.dve, lambda: nc.vector.scalar_tensor_tensor(t1[:], t2[:], neglam, t1[:], ALU.mult, ALU.add),
                          reads=[d_t1, d_t2, d_ls], writes=[d_t1])
                    sq, d_sq = sq_r.next()
                    fw.op(fw.act, lambda: nc.scalar.activation(sq[:], t1[:], AF.Square), reads=[d_t1], writes=[d_sq])
                    yield
                    fw.group(fw.pe, [self.mm(ssb[:, 0:256], self.onesB[:], sq[:, dv, :], dv == 0, dv == 1) for dv in range(2)],
                             reads=[d_sq, self.d_ones], writes=[d_ssb])
                    rs, d_rs = rs_r.next()
                    fw.op(fw.act, lambda: nc.scalar.activation(rs[:], ssb[:, 0:256], AF.Ln, bias=self.epsc[:, 0:1], scale=1.0 / 256),
                          reads=[d_ssb, self.d_epsc], writes=[d_rs])
                    fw.op(fw.act, lambda: nc.scalar.activation(rs[:], rs[:], AF.Exp, scale=-0.5), reads=[d_rs], writes=[d_rs])
                    o, d_o = o_r.next()
                    for dv in range(2):
                        fw.op(fw.dve, lambda: nc.vector.scalar_tensor_tensor(o[:, dv, :], t1[:, dv, :], on[:, dv:dv + 1], rs[:],
                                                                             ALU.mult, ALU.mult),
                              reads=[d_t1, d_rs, d_on], writes=[d_o])
                    (blk, col, off, ln) = dsts[0]
                    fw.dma(fw.sp, self.OT[blk, 2 * h:2 * h + 2, :, col:col + ln].rearrange("c p t -> p c t"), o[:],
                           reads=[d_o], writes=[Dep()])

                return [item(i) for i in range(len(kts))]

            def gens():
                for b in range(2):
                    for h in range(8):
                        kT, d_kT = k_r.next()
                        fw.dma(fw.sp, kT[:], self.PA[b, 16 + 2 * h:16 + 2 * h + 2].rearrange("c p t -> p c t"), writes=[d_kT])
                        Vh, d_V = v_r.next()
                        fw.dma(fw.sp, Vh[:], self.VT[b, :, h * 256:(h + 1) * 256].rearrange("(t p) d -> p t d", p=128), writes=[d_V])
                        for (q0, n, kts, dsts) in self.qblocks(b, 256, False):
                            for it in qblock_items(b, h, kT, d_kT, Vh, d_V, q0, n, kts, dsts):
                                yield it

            self.pipeline(gens())

    def gla_A(self, l):
        nc, fw = self.nc, self.fw
        p = f"l{l}_"
        wv = self.W[p + "w_in"].rearrange("(k p) m -> p k m", p=128)
        with self.phase("glaA"):
            self.alloc_norm()
            hT, d_hT = self.sb("hT", [128, KD, TB], BF16)
            wring = self.ring("wG", 4, [128, KD, 128], BF16)
            wtring = self.ring("wtG", 2, [128, KD, 512], BF16)
            zT, d_zT = self.sb("zT", [33, TB], F32)
            wga, d_wga = self.sb("wga", [33, 2, 1024], F32)
            f_r = self.ring("fstG", 3, [128, 512], F32)
            b_r = self.ring("bstG", 3, [128, 512], BF16)
            e_r = self.ring("etG", 2, [128, 512], F32)
            l_r = self.ring("lstG", 2, [128, 512], F32)
            fw.dma(fw.sp, wga[:, 0, :], self.W[p + "wgaf"], writes=[d_wga])
            fw.dma(fw.sp, wga[:, 1, :], self.W[p + "wgab"], writes=[d_wga])
            fw.op(fw.dve, lambda: nc.vector.memset(zT[32:33, :], 1.0), writes=[d_zT])
            subs = self.subs(True)
            for blk in range(4):
                b, h = divmod(blk, 2)
                self.norm_mod(l, blk, 0, True, hT, d_hT)

                def epi_z(fi, si, rng, bank, bd):
                    s0, s1 = rng
                    fw.op(fw.act, lambda: nc.scalar.copy(zT[0:32, s0:s1], bank[0:32, 0:s1 - s0]), reads=[bd], writes=[d_zT])

                self.linear(hT, d_hT, KD, wv, [6144], subs, epi_z, wring, mw=32)
                for ti in range(9):
                    t0 = ti * 128
                    n0 = self.nat(blk, t0)
                    for dirn in range(2):
                        for half in range(2):
                            bank, bd = self.bank_ring.next()
                            fw.group(fw.pe, [self.mm(bank[:, :], zT[0:33, t0:t0 + 128],
                                                     wga[0:33, dirn, half * 512:(half + 1) * 512], True, True)],
                                     reads=[d_zT, d_wga], writes=[bd])
                            et, d_et = e_r.next()
                            fw.op(fw.act, lambda: nc.scalar.activation(et[:], bank[:], AF.Exp, scale=-1.0), reads=[bd], writes=[d_et])
                            ls, d_l = l_r.next()
                            fw.op(fw.act, lambda: nc.scalar.activation(ls[:], et[:], AF.Ln, bias=self.epsc[:, 1:2], scale=1.0),
                                  reads=[d_et, self.d_epsc], writes=[d_l])
                            c0 = dirn * 1024 + half * 512
                            fw.dma(fw.sp, self.LT[b, n0:n0 + 128, c0:c0 + 512], ls[:], reads=[d_l], writes=[Dep()])

                def epi_qk(fi, si, rng, bank, bd):
                    s0, s1 = rng
                    n = s1 - s0
                    st, d_st = f_r.next()
                    if (fi + si) % 2 == 0:
                        fw.op(fw.act, lambda: nc.scalar.copy(st[:, 0:n], bank[:, 0:n]), reads=[bd], writes=[d_st])
                    else:
                        fw.op(fw.dve, lambda: nc.vector.tensor_copy(st[:, 0:n], bank[:, 0:n]), reads=[bd], writes=[d_st])
                    n0 = self.nat(blk, s0)
                    fw.dma(fw.sp, self.PAF[b, fi, :, n0:n0 + n], st[:, 0:n], reads=[d_st], writes=[Dep()])

                self.linear(hT, d_hT, KD, wv, [c * 128 for c in range(16)], subs, epi_qk, wring)

                def epi_r(fi, si, rng, bank, bd):
                    s0, s1 = rng
                    n = s1 - s0
                    st, d_st = b_r.next()
                    fw.op(fw.act, lambda: nc.scalar.activation(st[:, 0:n], bank[:, 0:n], AF.Silu), reads=[bd], writes=[d_st])
                    n0 = self.nat(blk, s0)
                    fw.dma(fw.sp, self.SR[b, fi, :, n0:n0 + n], st[:, 0:n], reads=[d_st], writes=[Dep()])

                self.linear(hT, d_hT, KD, wv, [4096 + c * 128 for c in range(16)], subs, epi_r, wring)

                def epi_kv(gi, ti, t0, bank, bd):
                    st, d_st = b_r.next()
                    if ti % 2 == 0:
                        fw.op(fw.act, lambda: nc.scalar.copy(st[:], bank[:]), reads=[bd], writes=[d_st])
                    else:
                        fw.op(fw.dve, lambda: nc.vector.tensor_copy(st[:], bank[:]), reads=[bd], writes=[d_st])
                    n0 = self.nat(blk, t0)
                    fw.dma(fw.sp, self.VT[b, n0:n0 + 128, gi * 512:(gi + 1) * 512], st[:], reads=[d_st], writes=[Dep()])

                self.linear_tok(hT, d_hT, wv, [1024 + g * 512 for g in range(6)], [t * 128 for t in range(9)], epi_kv, wtring)

    def gla_B(self, l):
        nc, fw = self.nc, self.fw
        p = f"l{l}_"
        with self.phase("glaB"):
            gc, d_gc = self.sb("gc", [128, 1024], F32)
            on, d_on = self.sb("onG", [128, 4], F32)
            fw.dma(fw.sp, gc[:], self.gla_c, writes=[d_gc])
            fw.dma(fw.sp, on[:], self.W[p + "onT"], writes=[d_on])
            S, _ = self.sb("S", [128, 4, 2, 512], F32)
            Sb, _ = self.sb("Sb", [128, 4, 2, 512], BF16)
            d_S = [Dep() for _ in range(4)]
            d_Sb = [Dep() for _ in range(4)]
            qk_r = self.ring("qkG", 3, [128, 16, 128], F32)
            kv_r = self.ring("kvG", 3, [128, 3072], BF16)
            lt_r = self.ring("ltG", 3, [128, 1024], F32)
            of_r = self.ring("ofG", 4, [128, 16, 128], F32)
            sr_r = self.ring("srG", 4, [128, 16, 128], BF16)
            e13_r = self.ring("e13", 4, [128, 2, 256], F32)
            e2_r = self.ring("e2", 2, [128, 2, 128], F32)
            qt_r = self.ring("qtG", 3, [128, 2, 128], BF16)
            kt_r = self.ring("ktG", 3, [128, 2, 128], BF16)
            qi_r = self.ring("qiG", 5, [128, 2, 128], BF16)
            er_r = self.ring("erG", 2, [128, 256], F32)
            kc_r = self.ring("kcG", 5, [128, 256], BF16)
            am_r = self.ring("amG", 3, [128, 128], BF16)
            os_r = self.ring("osG", 3, [128, 4, 128], F32)
            sq_r = self.ring("sqG", 3, [128, 4, 128], BF16)
            rs_r = self.ring("rsG", 2, [128, 128], F32)
            tm_r = self.ring("tmG", 2, [128, 4, 128], F32)
            fin_r = self.ring("finG", 2, [128, 4, 128], BF16)
            xb_r = Ring(self.banks[0:2])
            rb_r = Ring(self.banks[2:4])
            ab_r = Ring(self.banks[4:5])
            ob_r = Ring(self.banks[5:6])
            sk_r = Ring(self.banks[6:7])
            ss_r = Ring(self.banks[7:8])

            for b in range(2):
                for dirn in range(2):
                    for hh in range(4):
                        fw.op(fw.dve, lambda: nc.vector.memset(S[:, hh], 0.0), writes=[d_S[hh]])
                        fw.op(fw.pool, lambda: nc.gpsimd.memset(Sb[:, hh], 0.0), writes=[d_Sb[hh]])
                    order = list(range(18)) if dirn == 0 else [1, 0] + list(range(17, 1, -1))
                    TT = gc[:, dirn * 256:(dirn + 1) * 256]
                    TriS = gc[:, 512 + dirn * 128:512 + (dirn + 1) * 128]
                    mask = gc[:, 768 + dirn * 128:768 + (dirn + 1) * 128]
                    dcol = 128 + (127 if dirn == 0 else 0)
                    tiles = {}

                    def load_tile(pos, b=b, dirn=dirn, order=order, tiles=tiles):
                        if pos >= len(order) or pos in tiles:
                            return
                        t0 = order[pos] * 128
                        d = {}
                        d["lt"] = lt_r.next()
                        fw.dma(fw.sp, d["lt"][0][:], self.LT[b, t0:t0 + 128, dirn * 1024:(dirn + 1) * 1024], writes=[d["lt"][1]])
                        d["qk"] = qk_r.next()
                        fw.dma(fw.sp, d["qk"][0][:], self.PAF[b, :, :, t0:t0 + 128].rearrange("c p t -> p c t"), writes=[d["qk"][1]])
                        d["kv"] = kv_r.next()
                        fw.dma(fw.sp, d["kv"][0][:], self.VT[b, t0:t0 + 128, :], writes=[d["kv"][1]])
                        if dirn == 1:
                            d["of"] = of_r.next()
                            fw.dma(fw.sp, d["of"][0][:], self.OF[b, :, :, t0:t0 + 128].rearrange("c p t -> p c t"), writes=[d["of"][1]])
                            d["sr"] = sr_r.next()
                            fw.dma(fw.sp, d["sr"][0][:], self.SR[b, :, :, t0:t0 + 128].rearrange("c p t -> p c t"), writes=[d["sr"][1]])
                        tiles[pos] = d

                    def item(pos, hh, b=b, dirn=dirn, order=order, tiles=tiles, TT=TT, TriS=TriS, mask=mask, dcol=dcol):
                        t = order[pos]
                        t0 = t * 128
                        if hh == 0:
                            load_tile(pos)
                            load_tile(pos + 1)
                        td = tiles[pos]
                        lt, d_lt = td["lt"]
                        qk, d_qk = td["qk"]
                        kv, d_kv = td["kv"]
                        if t < 2:
                            oblk, ocol = b * 2 + t, 0
                        else:
                            ltok = (t - 2) * 128
                            oblk, ocol = b * 2 + ltok // 1024, 128 + ltok % 1024
                        xb, d_xb = xb_r.next()
                        fw.group(fw.pe, [self.mm(xb[:, dc * 256:(dc + 1) * 256], lt[:, hh * 256 + dc * 128:hh * 256 + (dc + 1) * 128],
                                                 TT, True, True) for dc in range(2)], reads=[d_lt, d_gc], writes=[d_xb])
                        rbb, d_rb = rb_r.next()
                        fw.group(fw.pe, [self.mm(rbb[:, 0:256], TriS, lt[:, hh * 256:(hh + 1) * 256], True, True)],
                                 reads=[d_lt, d_gc], writes=[d_rb])
                        yield
                        xv = xb[:].rearrange("p (c i) -> p c i", c=2)
                        e13, d_e13 = e13_r.next()
                        fw.op(fw.act, lambda: nc.scalar.activation(e13[:], xv, AF.Exp), reads=[d_xb], writes=[d_e13])
                        e2, d_e2 = e2_r.next()
                        fw.op(fw.act, lambda: nc.scalar.activation(e2[:], xv[:, :, 0:128], AF.Exp, scale=-1.0), reads=[d_xb], writes=[d_e2])
                        er, d_er = er_r.next()
                        fw.op(fw.act, lambda: nc.scalar.activation(er[:], rbb[:, 0:256], AF.Exp), reads=[d_rb], writes=[d_er])
                        qv = qk[:, 2 * hh:2 * hh + 2, :]
                        kvv = qk[:, 8 + 2 * hh:8 + 2 * hh + 2, :]
                        qt, d_qt = qt_r.next()
                        fw.op(fw.dve, lambda: nc.vector.scalar_tensor_tensor(qt[:], qv, 0.0625, e13[:, :, 0:128], ALU.mult, ALU.mult),
                              reads=[d_qk, d_e13], writes=[d_qt])
                        kt_, d_kt = kt_r.next()
                        fw.op(fw.pool, lambda: nc.gpsimd.tensor_tensor(kt_[:], kvv, e2[:], ALU.mult), reads=[d_qk, d_e2], writes=[d_kt])
                        qi, d_qi = qi_r.next()
                        fw.op(fw.dve, lambda: nc.vector.scalar_tensor_tensor(qi[:], qv, 0.0625, e13[:, :, 128:256], ALU.mult, ALU.mult),
                              reads=[d_qk, d_e13], writes=[d_qi])
                        kc, d_kc = kc_r.next()
                        fw.op(fw.pool, lambda: nc.gpsimd.tensor_tensor(kc[:], kv[:, hh * 256:(hh + 1) * 256], er[:], ALU.mult),
                              reads=[d_kv, d_er], writes=[d_kc])
                        yield
                        abb, d_ab = ab_r.next()
                        fw.group(fw.pe, [self.mm(abb[:, 0:128], kt_[:, dc, :], qt[:, dc, :], dc == 0, dc == 1) for dc in range(2)],
                                 reads=[d_kt, d_qt], writes=[d_ab])
                        am, d_am = am_r.next()
                        fw.op(fw.dve, lambda: nc.vector.tensor_tensor(am[:], abb[:, 0:128], mask, ALU.mult), reads=[d_ab, d_gc], writes=[d_am])
                        yield
                        ob, d_ob = ob_r.next()
                        fns = []
                        for dvc in range(4):
                            oo = ob[:, dvc * 128:(dvc + 1) * 128]
                            vcol = 1024 + hh * 512 + dvc * 128
                            fns.append(self.mm(oo, kv[:, vcol:vcol + 128], am[:], True, False))
                            for dc in range(2):
                                fns.append(self.mm(oo, Sb[:, hh, dc, dvc * 128:(dvc + 1) * 128], qi[:, dc, :], False, dc == 1))
                        fw.group(fw.pe, fns, reads=[d_kv, d_am, d_Sb[hh], d_qi], writes=[d_ob])
                        for dc in range(2):
                            sbk, d_sbk = sk_r.next()
                            fw.group(fw.pe, [self.mm(sbk[:, :], kc[:, dc * 128:(dc + 1) * 128],
                                                     kv[:, 1024 + hh * 512:1024 + (hh + 1) * 512], True, True)],
                                     reads=[d_kc, d_kv], writes=[d_sbk])
                            fw.op(fw.dve, lambda: nc.vector.scalar_tensor_tensor(S[:, hh, dc, :], S[:, hh, dc, :],
                                                                                 e13[:, dc, dcol:dcol + 1], sbk[:, :],
                                                                                 ALU.mult, ALU.add),
                                  reads=[d_sbk, d_e13], writes=[d_S[hh]])
                            fw.op(fw.act, lambda: nc.scalar.copy(Sb[:, hh, dc, :], S[:, hh, dc, :]), reads=[d_S[hh]], writes=[d_Sb[hh]])
                        ov = ob[:].rearrange("p (c i) -> p c i", c=4)
                        os_, d_os = os_r.next()
                        if dirn == 0:
                            fw.op(fw.act, lambda: nc.scalar.copy(os_[:], ov), reads=[d_ob], writes=[d_os])
                            fw.dma(fw.sp, self.OF[b, 4 * hh:4 * hh + 4, :, t0:t0 + 128].rearrange("c p t -> p c t"), os_[:],
                                   reads=[d_os], writes=[Dep()])
                            return
                        of, d_of = td["of"]
                        sr, d_sr = td["sr"]
                        fw.op(fw.dve, lambda: nc.vector.tensor_tensor(os_[:], ov, of[:, 4 * hh:4 * hh + 4, :], ALU.add),
                              reads=[d_ob, d_of], writes=[d_os])
                        sq, d_sq = sq_r.next()
                        fw.op(fw.act, lambda: nc.scalar.activation(sq[:], os_[:], AF.Square), reads=[d_os], writes=[d_sq])
                        yield
                        ssb, d_ssb = ss_r.next()
                        fw.group(fw.pe, [self.mm(ssb[:, 0:128], self.onesB[:], sq[:, dvc, :], dvc == 0, dvc == 3) for dvc in range(4)],
                                 reads=[d_sq, self.d_ones], writes=[d_ssb])
                        rs, d_rs = rs_r.next()
                        fw.op(fw.act, lambda: nc.scalar.activation(rs[:], ssb[:, 0:128], AF.Ln, bias=self.epsc[:, 0:1], scale=1.0 / 512),
                              reads=[d_ssb, self.d_epsc], writes=[d_rs])
                        fw.op(fw.act, lambda: nc.scalar.activation(rs[:], rs[:], AF.Exp, scale=-0.5), reads=[d_rs], writes=[d_rs])
                        tm, d_tm = tm_r.next()
                        fw.op(fw.dve, lambda: nc.vector.tensor_tensor(tm[:], os_[:], rs[:].unsqueeze(1).to_broadcast([128, 4, 128]), ALU.mult),
                              reads=[d_os, d_rs], writes=[d_tm])
                        fw.op(fw.pool, lambda: nc.gpsimd.tensor_tensor(tm[:], tm[:], on[:].unsqueeze(2).to_broadcast([128, 4, 128]), ALU.mult),
                              reads=[d_tm, d_on], writes=[d_tm])
                        fin, d_fin = fin_r.next()
                        fw.op(fw.pool, lambda: nc.gpsimd.tensor_tensor(fin[:], tm[:], sr[:, 4 * hh:4 * hh + 4, :], ALU.mult),
                              reads=[d_tm, d_sr], writes=[d_fin])
                        fw.dma(fw.sp, self.OT[oblk, 4 * hh:4 * hh + 4, :, ocol:ocol + 128].rearrange("c p t -> p c t"), fin[:],
                               reads=[d_fin], writes=[Dep()])

                    self.pipeline(item(pos, hh) for pos in range(18) for hh in range(4))


def _vecT(v, k):
    return np.ascontiguousarray(np.asarray(v, np.float32).reshape(k, 128).T)


def _bf(a):
    return np.ascontiguousarray(np.asarray(a).astype(ml_dtypes.bfloat16))


def _pk(m):
    K = m.shape[0] // 128
    return np.ascontiguousarray(m.reshape(K, 128, m.shape[1]).transpose(1, 0, 2).reshape(128, K * m.shape[1]))


_CONST_CACHE = {}


def _constants(layers):
    key = tuple(layers)
    if key in _CONST_CACHE:
        return _CONST_CACHE[key]
    c = {}
    c["identF"] = np.eye(128, dtype=np.float32)
    c["onesB"] = _bf(np.ones((128, 128), np.float32))
    if 1 in layers or 2 in layers:
        t = np.arange(SEQ)
        row = (t // 64).astype(np.float32)
        col = (t % 64).astype(np.float32)
        half = 64
        inv_freq = (np.float32(10000.0) ** (-np.arange(0, half, 2, dtype=np.float32) / np.float32(half))).astype(np.float32)
        ang_r = row[:, None] * inv_freq[None, :]
        ang_c = col[:, None] * inv_freq[None, :]
        ang = np.concatenate([ang_r, ang_r, ang_c, ang_c], axis=-1).astype(np.float32)
        c["rope"] = np.ascontiguousarray(np.concatenate([np.cos(ang).T, np.sin(ang).T], axis=1).astype(np.float32))
        R = np.zeros((128, 128), np.float32)
        for d in range(128):
            q = d // 32
            if q % 2 == 0:
                R[d, d + 32] = -1.0
            else:
                R[d, d - 32] = 1.0
        c["RT"] = _bf(R.T)
    if 3 in layers:
        def dft(n):
            k = np.arange(n, dtype=np.float64)
            ang = 2.0 * np.pi * np.outer(k, k) / n
            return np.cos(ang) / np.sqrt(n), np.sin(ang) / np.sqrt(n)
        cc, sc = dft(512)
        cl, sl = dft(SEQ)
        c["dftc"] = _bf(np.concatenate([_pk(cc), _pk(sc)], axis=1))
        c["dftl"] = _bf(np.concatenate([_pk(cl), _pk(sl)], axis=1))
    if 0 in layers:
        j = np.arange(128)[:, None]
        i = np.arange(128)[None, :]
        s = -1.0 / 16.0
        tri_f = (j <= i).astype(np.float64)
        tri_b = (j >= i).astype(np.float64)
        m1_f = tri_f - (j <= 63)
        m1_b = tri_b - (j >= 64)
        tris_f = (j > i).astype(np.float64)
        tris_b = (j < i).astype(np.float64)
        c["gla_c"] = np.ascontiguousarray(np.concatenate(
            [s * m1_f, s * tri_f, s * m1_b, s * tri_b, s * tris_f, s * tris_b, tri_f, tri_b], axis=1).astype(np.float32))
    _CONST_CACHE[key] = c
    return c


def _prep_inputs(inputs, layers, x_over=None, ctx_over=None):
    f32 = lambda a: np.ascontiguousarray(np.asarray(a, np.float32))
    shared = dict(_constants(layers))
    for l in layers:
        p = f"l{l}_"
        shared[p + "mod_w"] = f32(inputs[p + "mod_w"])
        shared[p + "mod_b"] = f32(inputs[p + "mod_b"]).reshape(1, -1)
        shared[p + "n1T"] = _vecT(inputs[p + "norm1"], KD)
        shared[p + "n2T"] = _vecT(inputs[p + "norm2"], KD)
        shared[p + "ffn_w_in"] = f32(inputs[p + "ffn_w_in"])
        shared[p + "ffn_w_out"] = f32(inputs[p + "ffn_w_out"])
        if l == 0:
            shared[p + "w_in"] = f32(inputs[p + "gla_w_in"])
            z16 = np.zeros((16, 1024), np.float32)
            shared[p + "wgaf"] = np.ascontiguousarray(np.concatenate(
                [f32(inputs[p + "gla_wg_f"]), z16, f32(inputs[p + "gla_bg_f"])[None]], axis=0))
            shared[p + "wgab"] = np.ascontiguousarray(np.concatenate(
                [z16, f32(inputs[p + "gla_wg_b"]), f32(inputs[p + "gla_bg_b"])[None]], axis=0))
            shared[p + "onT"] = _vecT(inputs[p + "gla_out_norm"], 4)
            shared[p + "w_out"] = f32(inputs[p + "gla_w_out"])
        elif l == 1:
            shared[p + "w_in"] = f32(inputs[p + "gqa_w_in"])
            shared[p + "qk"] = np.ascontiguousarray(np.stack([f32(inputs[p + "gqa_q_norm"]), f32(inputs[p + "gqa_k_norm"])], axis=1))
            shared[p + "w_out"] = f32(inputs[p + "gqa_w_out"])
        elif l == 2:
            shared[p + "w_in"] = f32(inputs[p + "diff_w_in"])
            shared[p + "qk"] = np.ascontiguousarray(np.stack([f32(inputs[p + "diff_q_norm"]), f32(inputs[p + "diff_k_norm"])], axis=1))
            lv = np.concatenate([f32(inputs[p + "diff_lq1"]), f32(inputs[p + "diff_lk1"]),
                                 f32(inputs[p + "diff_lq2"]), f32(inputs[p + "diff_lk2"])])
            shared[p + "lvec"] = np.ascontiguousarray(np.broadcast_to(lv[None, :], (128, 512)))
            shared[p + "onT"] = _vecT(inputs[p + "diff_out_norm"], 2)
            shared[p + "w_out"] = f32(inputs[p + "diff_w_out"])
        else:
            shared[p + "w_out"] = f32(inputs[p + "fnet_w_out"])
    x = f32(inputs["x"]) if x_over is None else x_over
    ctx = f32(inputs["ctx"]) if ctx_over is None else ctx_over
    c = f32(inputs["c"])
    c_ctx = f32(inputs["c_ctx"])
    nb = x.shape[0] // 2
    maps = []
    for i in range(nb):
        m = dict(shared)
        m["x"] = np.ascontiguousarray(x[2 * i:2 * i + 2])
        m["ctx"] = np.ascontiguousarray(ctx[2 * i:2 * i + 2])
        cv = np.stack([c[2 * i], c[2 * i + 1], c_ctx], axis=0)
        m["cvecT"] = np.ascontiguousarray(cv.reshape(3, KD, 128).transpose(2, 1, 0).reshape(128, KD * 3))
        maps.append(m)
    return maps


_NC_CACHE = {}


def _get_nc(layers):
    key = tuple(layers)
    if key not in _NC_CACHE:
        _NC_CACHE[key] = Builder(layers).build()
    return _NC_CACHE[key]


def run_layers(inputs, layers, x_over=None, ctx_over=None, trace=False):
    maps = _prep_inputs(inputs, layers, x_over, ctx_over)
    nc = _get_nc(layers)
    res = run_bass_kernel_spmd(nc, maps, core_ids=list(range(len(maps))), trace=trace)
    out = np.concatenate([r["y"] for r in res.results], axis=0)
    return out, res


def kernel(**inputs):
    out, _ = run_layers(inputs, (0, 1, 2, 3))
    return out.astype(np.float32, copy=False)
```

```python
import math
from contextlib import ExitStack, contextmanager

import numpy as np
import ml_dtypes

import concourse.bass as bass
import concourse.mybir as mybir
from concourse.bass_utils import run_bass_kernel_spmd

F32 = mybir.dt.float32
BF16 = mybir.dt.bfloat16
AF = mybir.ActivationFunctionType
ALU = mybir.AluOpType
AX = mybir.AxisListType

N_CORES = 8
D = 2048
KD = 16
SEQ = 2048
CTX = 256
NTOK = CTX + SEQ
TB = 1152
FFN_H = 5632
NF = FFN_H // 128
EPS = 1e-6
SEM_LIMIT = 30000


class Dep:
    __slots__ = ("w", "r", "ep")

    def __init__(self):
        self.w = {}
        self.r = {}
        self.ep = -1


class Eng:
    def __init__(self, fw, name, eng):
        self.fw = fw
        self.name = name
        self.eng = eng
        self.sem = None
        self.count = 0
        self.known = {}
        self.nsem = 0
        self.n_inst = 0
        self.n_wait = 0

    def rotate(self):
        self.sem = self.fw.new_sem(f"{self.name}_s{self.nsem}")
        self.nsem += 1
        self.count = 0

    def wait(self, toks):
        for s, v in toks.items():
            if self.known.get(s, 0) < v:
                self.eng.wait_ge(s, v)
                self.known[s] = v
                self.n_wait += 1


class FW:
    def __init__(self, nc, n_dma_slots=24):
        self.nc = nc
        self.stack = ExitStack()
        self.sem_count = 0
        self.epoch = 0
        self.pe = Eng(self, "pe", nc.tensor)
        self.act = Eng(self, "act", nc.scalar)
        self.dve = Eng(self, "dve", nc.vector)
        self.pool = Eng(self, "pool", nc.gpsimd)
        self.sp = Eng(self, "sp", nc.sync)
        self.engs = [self.pe, self.act, self.dve, self.pool, self.sp]
        for e in self.engs:
            e.rotate()
        self.slots = [[self.new_sem(f"dma{i}"), 0] for i in range(n_dma_slots)]
        self.slot_i = 0
        self.n_dma = 0

    def new_sem(self, name):
        self.sem_count += 1
        return self.stack.enter_context(self.nc.semaphore(name))

    def _sync(self, d):
        if d.ep != self.epoch:
            d.w = {}
            d.r = {}
            d.ep = self.epoch

    def _collect(self, reads, writes):
        toks = {}
        for d in reads:
            self._sync(d)
            for s, v in d.w.items():
                if toks.get(s, 0) < v:
                    toks[s] = v
        for d in writes:
            self._sync(d)
            for s, v in d.w.items():
                if toks.get(s, 0) < v:
                    toks[s] = v
            for s, v in d.r.items():
                if toks.get(s, 0) < v:
                    toks[s] = v
        return toks

    def _record(self, tok, reads, writes):
        s, v = tok
        for d in reads:
            if d.r.get(s, 0) < v:
                d.r[s] = v
        for d in writes:
            d.w = {s: v}
            d.r = {}

    def op(self, E, fn, reads=(), writes=()):
        return self.group(E, [fn], reads, writes)

    def group(self, E, fns, reads=(), writes=()):
        toks = self._collect(reads, writes)
        if E is self.pe:
            toks.pop(E.sem, None)
        E.wait(toks)
        ins = None
        for fn in fns:
            ins = fn()
            E.n_inst += 1
        if E.count >= SEM_LIMIT:
            E.rotate()
        ins.then_inc(E.sem, 1)
        E.count += 1
        tok = (E.sem, E.count)
        self._record(tok, reads, writes)
        return tok

    def dma(self, Q, out, in_, reads=(), writes=()):
        toks = self._collect(reads, writes)
        slot = self.slots[self.slot_i]
        self.slot_i = (self.slot_i + 1) % len(self.slots)
        if slot[1] > 0:
            toks[slot[0]] = max(toks.get(slot[0], 0), slot[1])
        Q.wait(toks)
        if slot[1] + 16 > SEM_LIMIT:
            slot[0] = self.new_sem(f"dmar{self.sem_count}")
            slot[1] = 0
        self.nc_dma(Q, out, in_).then_inc(slot[0], 16)
        slot[1] += 16
        self.n_dma += 1
        tok = (slot[0], slot[1])
        self._record(tok, reads, writes)
        return tok

    def nc_dma(self, Q, out, in_):
        return Q.eng.dma_start(out=out, in_=in_)

    def barrier(self):
        toks = {}
        for E in self.engs:
            if E.count > 0:
                toks[E.sem] = E.count
        for s, v in self.slots:
            if v > 0:
                toks[s] = v
        for E in self.engs:
            E.wait(dict(toks))
        self.epoch += 1

    def close(self):
        self.stack.close()


class Ring:
    def __init__(self, items):
        self.items = items
        self.i = 0

    def next(self):
        it = self.items[self.i]
        self.i = (self.i + 1) % len(self.items)
        return it


class Builder:
    def __init__(self, layers=(0, 1, 2, 3), debug_out=None):
        self.layers = list(layers)
        self.nc = bass.Bass("TRN2", target_bir_lowering=False)
        Builder.last = self
        self.fw = FW(self.nc)
        self.gs = ExitStack()
        self.ps = None
        self.inputs = {}
        self.consts = {}

    def din(self, name, shape, dt=F32):
        t = self.nc.dram_tensor(name, list(shape), dt, kind="ExternalInput").ap()
        self.inputs[name] = t
        return t

    def dscr(self, name, shape, dt):
        return self.nc.dram_tensor(name, list(shape), dt, kind="Internal").ap()

    def _uname(self, name):
        self.ucount = getattr(self, "ucount", 0) + 1
        return f"s{self.ucount}_{name}"

    def gsb(self, name, shape, dt):
        return self.gs.enter_context(self.nc.sbuf_tensor(self._uname(name), list(shape), dt)), Dep()

    def sb(self, name, shape, dt):
        return self.ps.enter_context(self.nc.sbuf_tensor(self._uname(name), list(shape), dt)), Dep()

    def ring(self, name, n, shape, dt):
        return Ring([self.sb(f"{name}{i}", shape, dt) for i in range(n)])

    @contextmanager
    def phase(self, name):
        self.fw.barrier()
        with ExitStack() as ps:
            self.ps = ps
            yield
            self.fw.barrier()
        self.ps = None

    def mm(self, out, lhsT, rhs, start, stop):
        nc = self.nc
        return lambda: nc.tensor.matmul(out, lhsT, rhs, start=start, stop=stop)

    def build(self):
        nc, fw = self.nc, self.fw
        L = self.layers
        self.x_in = self.din("x", [2, SEQ, D])
        self.ctx_in = self.din("ctx", [2, CTX, D])
        self.cvecT = self.din("cvecT", [128, KD * 3])
        self.identF_d = self.din("identF", [128, 128])
        self.onesB_d = self.din("onesB", [128, 128], BF16)
        self.W = {}
        for l in L:
            p = f"l{l}_"
            self.W[p + "mod_w"] = self.din(p + "mod_w", [D, 6 * D])
            self.W[p + "mod_bT"] = self.din(p + "mod_bT", [128, 96])
            self.W[p + "n1T"] = self.din(p + "n1T", [128, KD])
            self.W[p + "n2T"] = self.din(p + "n2T", [128, KD])
            self.W[p + "ffn_w_in"] = self.din(p + "ffn_w_in", [D, 2 * FFN_H])
            self.W[p + "ffn_w_out"] = self.din(p + "ffn_w_out", [FFN_H, D])
        if 0 in L:
            p = "l0_"
            self.W[p + "w_in"] = self.din(p + "w_in", [D, 6176])
            self.W[p + "wgaf"] = self.din(p + "wgaf", [33, 1024])
            self.W[p + "wgab"] = self.din(p + "wgab", [33, 1024])
            self.W[p + "onT"] = self.din(p + "onT", [128, 4])
            self.W[p + "w_out"] = self.din(p + "w_out", [D, D])
            self.gla_c = self.din("gla_c", [128, 1024])
        if 1 in L:
            p = "l1_"
            self.W[p + "w_in"] = self.din(p + "w_in", [D, 3072])
            self.W[p + "qk"] = self.din(p + "qk", [128, 2])
            self.W[p + "w_out"] = self.din(p + "w_out", [D, D])
        if 2 in L:
            p = "l2_"
            self.W[p + "w_in"] = self.din(p + "w_in", [D, 6144])
            self.W[p + "qk"] = self.din(p + "qk", [128, 2])
            self.W[p + "lvec"] = self.din(p + "lvec", [128, 512])
            self.W[p + "onT"] = self.din(p + "onT", [128, 2])
            self.W[p + "w_out"] = self.din(p + "w_out", [D, D])
        if 1 in L or 2 in L:
            self.rope_d = self.din("rope", [128, 2 * SEQ])
            self.RT_d = self.din("RT", [128, 128], BF16)
        if 3 in L:
            p = "l3_"
            self.W[p + "w_out"] = self.din(p + "w_out", [D, D])
            self.dftc_d = self.din("dftc", [128, 2 * 4 * 512], BF16)
            self.dftl_d = self.din("dftl", [128, 2 * 16 * SEQ], BF16)
        self.y_out = nc.dram_tensor("y", [2, SEQ, D], F32, kind="ExternalOutput").ap()

        self.xT = self.dscr("xT", [4, KD, 128, TB], F32)
        self.OT = self.dscr("OT", [4, KD, 128, TB], BF16)
        self.RS = self.dscr("RS", [2, 4, 128, TB], F32)
        self.PA = self.dscr("PA", [2, 32, 128, NTOK], BF16)
        self.VT = self.dscr("VT", [2, NTOK, 3072], BF16)
        if 0 in L:
            self.PAF = self.dscr("PAF", [2, 16, 128, NTOK], F32)
            self.SR = self.dscr("SR", [2, 16, 128, NTOK], BF16)
            self.LT = self.dscr("LT", [2, NTOK, 2048], F32)
            self.OF = self.dscr("OF", [2, 16, 128, NTOK], F32)
        if 3 in L:
            self.AB = self.dscr("AB", [2, 4, 2, 16, 128, 512], BF16)

        self.banks = []
        for i in range(8):
            t = self.gs.enter_context(nc.psum_tensor(f"bank{i}", [128, 512], F32))
            self.banks.append((t, Dep()))
        self.bank_ring = Ring(self.banks)
        self.identF, self.d_ident = self.gsb("identF_s", [128, 128], F32)
        self.onesB, self.d_ones = self.gsb("onesB_s", [128, 128], BF16)
        self.epsc, self.d_epsc = self.gsb("epsc", [128, 2], F32)
        self.coef = {}
        for l in L:
            self.coef[l] = self.gsb(f"coef{l}", [128, 96, 3], F32)
        self.sTb, self.d_sTb = self.gsb("sTb", [128, KD, 3], BF16)
        self.side = None
        self.side_every = 4
        self.side_count = 0

        with nc.Block() as block:
            @block.sync
            def _(sync):
                self.emit_all()
        self.gs.close()
        fw.close()
        return nc

    def emit_all(self):
        nc, fw = self.nc, self.fw
        fw.dma(fw.sp, self.identF[:], self.identF_d, writes=[self.d_ident])
        fw.dma(fw.sp, self.onesB[:], self.onesB_d, writes=[self.d_ones])
        fw.op(fw.dve, lambda: nc.vector.memset(self.epsc[:, 0:1], EPS), writes=[self.d_epsc])
        fw.op(fw.dve, lambda: nc.vector.memset(self.epsc[:, 1:2], 1.0), writes=[self.d_epsc])
        self.phase_mod()
        self.phase_in()
        for l in self.layers:
            kind = l % 4
            ctx_out = l < 2
            if kind == 0:
                self.gla_A(l)
                self.gla_B(l)
            elif kind == 1:
                self.attn_A(l, diff=False)
                self.gqa_B(l)
            elif kind == 2:
                self.attn_A(l, diff=True)
                self.diff_B(l)
            else:
                self.fnet_A(l)
                self.fnet_B(l)
            self.phase_C1(l, ctx_out)
            self.phase_C2(l, ctx_out)
        self.phase_out()

    def phase_mod(self):
        nc, fw = self.nc, self.fw
        with self.phase("mod"):
            cT, d_cT = self.sb("cT", [128, KD, 3], F32)
            fw.dma(fw.sp, cT[:].rearrange("p k r -> p (k r)"), self.cvecT, writes=[d_cT])
            fw.op(fw.act, lambda: nc.scalar.activation(self.sTb[:], cT[:], AF.Silu), reads=[d_cT], writes=[self.d_sTb])
            for _ in self.mod_gen(self.layers[0]):
                pass

    def mod_gen(self, l):
        nc, fw = self.nc, self.fw
        p = f"l{l}_"
        mw = self.W[p + "mod_w"].rearrange("(k p) m -> p k m", p=128)
        cf, d_cf = self.coef[l]
        wring = self.ring("modw", 2, [128, KD, 512], BF16)
        row_r = self.ring("modrow", 2, [3, 512], F32)
        mbT, d_mb = self.sb("mbT", [128, 96], F32)
        nT, d_nT = self.sb("nT", [128, 2, KD], F32)
        fw.dma(fw.sp, mbT[:], self.W[p + "mod_bT"], writes=[d_mb])
        fw.dma(fw.sp, nT[:, 0, :], self.W[p + "n1T"], writes=[d_nT])
        fw.dma(fw.sp, nT[:, 1, :], self.W[p + "n2T"], writes=[d_nT])
        tiles = {}

        def load(j):
            if j < 24 and j not in tiles:
                tiles[j] = wring.next()
                fw.dma(fw.pool, tiles[j][0][:], mw[:, :, j * 512:(j + 1) * 512], writes=[tiles[j][1]])

        d_parts = [Dep() for _ in range(24)]
        for j in range(24):
            load(j)
            load(j + 1)
            wt, wd = tiles.pop(j)
            bank, bd = self.bank_ring.next()
            fw.group(fw.pe, [self.mm(bank[0:3, :], self.sTb[:, k, :], wt[:, k, :], k == 0, k == KD - 1)
                             for k in range(KD)], reads=[wd, self.d_sTb], writes=[bd])
            row, d_row = row_r.next()
            fw.op(fw.act, lambda: nc.scalar.copy(row[:], bank[0:3, :]), reads=[bd], writes=[d_row])
            yield
            bank2, bd2 = self.bank_ring.next()
            fw.group(fw.pe, [
                (lambda q=q: nc.tensor.transpose(bank2[:, q * 3:(q + 1) * 3], row[0:3, q * 128:(q + 1) * 128],
                                                 self.identF[0:3, 0:3]))
                for q in range(4)], reads=[d_row, self.d_ident], writes=[bd2])
            fw.op(fw.dve, lambda: nc.vector.tensor_tensor(
                cf[:, 4 * j:4 * j + 4, :], bank2[:, 0:12].rearrange("p (c r) -> p c r", r=3),
                mbT[:, 4 * j:4 * j + 4].unsqueeze(2).to_broadcast([128, 4, 3]), ALU.add),
                reads=[bd2, d_mb], writes=[d_parts[j]])
            yield
        for w, c0 in ((0, 16), (1, 64)):
            fw.op(fw.dve, lambda w=w, c0=c0: nc.vector.scalar_tensor_tensor(
                cf[:, c0:c0 + 16, :], cf[:, c0:c0 + 16, :], 1.0,
                nT[:, w, :].unsqueeze(2).to_broadcast([128, KD, 3]), ALU.add, ALU.mult),
                reads=[d_nT] + d_parts, writes=[d_cf])

    def host_side(self, l):
        nxt = [x for x in self.layers if x > l]
        if nxt:
            self.side = self.mod_gen(nxt[0])
            self.side_count = 0

    def side_tick(self):
        if self.side is not None:
            self.side_count += 1
            if self.side_count % self.side_every == 0:
                try:
                    next(self.side)
                except StopIteration:
                    self.side = None

    def side_drain(self):
        if self.side is not None:
            for _ in self.side:
                pass
            self.side = None

    def block_tiles(self, blk):
        b, h = divmod(blk, 2)
        tiles = [(self.ctx_in[b, h * 128:(h + 1) * 128, :], self.y_out[b, 0:128, :], 0, True)]
        for t in range(8):
            r0 = h * 1024 + t * 128
            tiles.append((self.x_in[b, r0:r0 + 128, :], self.y_out[b, r0:r0 + 128, :], 128 + t * 128, False))
        return tiles

    def phase_in(self):
        nc, fw = self.nc, self.fw
        with self.phase("in"):
            xin = self.ring("xin", 3, [128, D], F32)
            xst = self.ring("xst", 3, [128, KD, 128], F32)
            sqt = self.ring("sqt", 3, [128, KD, 128], BF16)
            rs, d_rs = self.sb("rs_in", [128, TB], F32)
            ring5 = Ring(self.banks[0:5])
            sbank = Ring(self.banks[5:8])
            for blk in range(4):
                xv = self.xT[blk].rearrange("k p t -> p k t")
                pending = []
                for (src, _, c0, is_ctx) in self.block_tiles(blk):
                    xi, d_xi = xin.next()
                    fw.dma(fw.sp, xi[:], src, writes=[d_xi])
                    xs, d_xs = xst.next()
                    for g in range(4):
                        bank, bd = ring5.next()
                        fw.group(fw.pe, [
                            (lambda q=q: nc.tensor.transpose(bank[:, q * 128:(q + 1) * 128],
                                                             xi[:, (g * 4 + q) * 128:(g * 4 + q + 1) * 128],
                                                             self.identF[:]))
                            for q in range(4)], reads=[d_xi, self.d_ident], writes=[bd])
                        dst = xs[:, g * 4:(g + 1) * 4, :]
                        srcb = bank[:].rearrange("p (q t) -> p q t", q=4)
                        if g % 2 == 0:
                            fw.op(fw.act, lambda: nc.scalar.copy(dst, srcb), reads=[bd], writes=[d_xs])
                        else:
                            fw.op(fw.dve, lambda: nc.vector.tensor_copy(dst, srcb), reads=[bd], writes=[d_xs])
                    while pending:
                        pending.pop(0)()
                    fw.dma(fw.sp, xv[:, :, c0:c0 + 128], xs[:], reads=[d_xs], writes=[Dep()])
                    sq, d_sq = sqt.next()
                    fw.op(fw.act, lambda: nc.scalar.activation(sq[:], xs[:], AF.Square), reads=[d_xs], writes=[d_sq])

                    def stat(sq=sq, d_sq=d_sq, c0=c0):
                        sb_, d_sb = sbank.next()
                        fw.group(fw.pe, [self.mm(sb_[:, 0:128], self.onesB[:], sq[:, k, :], k == 0, k == KD - 1)
                                         for k in range(KD)], reads=[d_sq, self.d_ones], writes=[d_sb])
                        fw.op(fw.act, lambda: nc.scalar.activation(rs[:, c0:c0 + 128], sb_[:, 0:128], AF.Ln,
                                                                   bias=self.epsc[:, 0:1], scale=1.0 / D),
                              reads=[d_sb, self.d_epsc], writes=[d_rs])
                    pending.append(stat)
                while pending:
                    pending.pop(0)()
                fw.op(fw.act, lambda: nc.scalar.activation(rs[:], rs[:], AF.Exp, scale=-0.5), reads=[d_rs], writes=[d_rs])
                fw.dma(fw.sp, self.RS[0, blk], rs[:], reads=[d_rs], writes=[Dep()])

    def phase_out(self):
        nc, fw = self.nc, self.fw
        with self.phase("out"):
            xin = self.ring("xo_in", 2, [128, KD, 128], F32)
            xst = self.ring("xo_st", 2, [128, D], F32)
            outs = []
            for blk in range(4):
                xv = self.xT[blk].rearrange("k p t -> p k t")
                for (_, dst_d, c0, is_ctx) in self.block_tiles(blk):
                    if is_ctx:
                        continue
                    xi, d_xi = xin.next()
                    fw.dma(fw.sp, xi[:], xv[:, :, c0:c0 + 128], writes=[d_xi])
                    xs, d_xs = xst.next()
                    for g in range(4):
                        bank, bd = self.bank_ring.next()
                        fw.group(fw.pe, [
                            (lambda q=q: nc.tensor.transpose(bank[:, q * 128:(q + 1) * 128],
                                                             xi[:, g * 4 + q, :], self.identF[:]))
                            for q in range(4)], reads=[d_xi, self.d_ident], writes=[bd])
                        dst = xs[:, g * 512:(g + 1) * 512]
                        if g % 2 == 0:
                            fw.op(fw.act, lambda: nc.scalar.copy(dst, bank[:]), reads=[bd], writes=[d_xs])
                        else:
                            fw.op(fw.dve, lambda: nc.vector.tensor_copy(dst, bank[:]), reads=[bd], writes=[d_xs])
                    dd = Dep()
                    fw.dma(fw.sp, dst_d, xs[:], reads=[d_xs], writes=[dd])
                    outs.append(dd)

    @staticmethod
    def subs(ctx):
        return ([(0, 128)] if ctx else []) + [(128, 640), (640, TB)]

    def regions(self, blk, ctx):
        b = blk // 2
        return ([(0, 128, 2)] if ctx else []) + [(128, TB, b)]

    def alloc_norm(self, n_xc=3):
        self.xc_ring = self.ring("xc", n_xc, [128, TB], F32)
        self.tmp_ring = self.ring("tmpf", 2, [128, TB], F32)
        self.rstd, self.d_rstd = self.sb("rstd", [128, TB], F32)

    def norm_mod(self, l, blk, which, ctx, hT, d_hT):
        nc, fw = self.nc, self.fw
        cf, d_cf = self.coef[l]
        cB, cA = (0, 16) if which == 0 else (48, 64)
        subs = self.subs(ctx)
        a0 = subs[0][0]
        fw.dma(fw.sp, self.rstd[:, a0:TB], self.RS[which, blk, :, a0:TB], writes=[self.d_rstd])
        for k in range(KD):
            xc, d_xc = self.xc_ring.next()
            fw.dma(fw.sp, xc[:, a0:TB], self.xT[blk, k, :, a0:TB], writes=[d_xc])
            tm, d_tm = self.tmp_ring.next()
            fw.op(fw.dve, lambda: nc.vector.tensor_tensor(tm[:, a0:TB], xc[:, a0:TB], self.rstd[:, a0:TB], ALU.mult),
                  reads=[d_xc, self.d_rstd], writes=[d_tm])
            for (r0, r1, r) in self.regions(blk, ctx):
                fw.op(fw.act, lambda: nc.scalar.activation(hT[:, k, r0:r1], tm[:, r0:r1], AF.Identity,
                                                           bias=cf[:, cB + k, r:r + 1], scale=cf[:, cA + k, r:r + 1]),
                      reads=[d_tm, d_cf], writes=[d_hT])

    def stats_begin(self, subs):
        return {"subs": subs, "banks": [self.banks[5 + i] for i in range(len(subs))], "pending": []}

    def stats_push(self, st, src, d_src, k, si, sq, d_sq):
        nc, fw = self.nc, self.fw
        s0, s1 = st["subs"][si]
        fw.op(fw.act, lambda: nc.scalar.activation(sq[:, s0:s1], src[:, s0:s1], AF.Square), reads=[d_src], writes=[d_sq])
        bk, bd = st["banks"][si]
        st["pending"].append(lambda: fw.group(
            fw.pe, [self.mm(bk[:, 0:s1 - s0], self.onesB[:], sq[:, s0:s1], k == 0, k == KD - 1)],
            reads=[d_sq, self.d_ones], writes=[bd]))

    def stats_flush(self, st):
        while st["pending"]:
            st["pending"].pop(0)()

    def stats_end(self, st, which, blk, rs, d_rs):
        nc, fw = self.nc, self.fw
        self.stats_flush(st)
        a0 = st["subs"][0][0]
        for (bk, bd), (s0, s1) in zip(st["banks"], st["subs"]):
            fw.op(fw.act, lambda: nc.scalar.activation(rs[:, s0:s1], bk[:, 0:s1 - s0], AF.Ln,
                                                       bias=self.epsc[:, 0:1], scale=1.0 / D),
                  reads=[bd, self.d_epsc], writes=[d_rs])
        fw.op(fw.act, lambda: nc.scalar.activation(rs[:, a0:TB], rs[:, a0:TB], AF.Exp, scale=-0.5),
              reads=[d_rs], writes=[d_rs])
        fw.dma(fw.sp, self.RS[which, blk, :, a0:TB], rs[:, a0:TB], reads=[d_rs], writes=[Dep()])

    def pipeline(self, gens):
        def step(g):
            try:
                next(g)
                return True
            except StopIteration:
                return False
        active = []
        for g in gens:
            alive = step(g)
            active = [a for a in active if step(a)]
            if alive:
                active.append(g)
        while active:
            active = [a for a in active if step(a)]

    def linear(self, hT, d_hT, kd, wview, cols, subs, epilogue, wring, mw=128):
        nc, fw = self.nc, self.fw
        wts = {}
        PF = 2

        def load(fi):
            if fi < len(cols) and fi not in wts:
                wts[fi] = wring.next()
                fw.dma(fw.pool, wts[fi][0][:, 0:kd, 0:mw], wview[:, :, cols[fi]:cols[fi] + mw], writes=[wts[fi][1]])

        def item(fi, c0, si, s0, s1):
            if si == 0:
                for j in range(fi, fi + PF + 1):
                    load(j)
            wt, wd = wts[fi]
            bank, bd = self.bank_ring.next()
            fw.group(fw.pe, [self.mm(bank[0:mw, 0:s1 - s0], wt[:, k, 0:mw], hT[:, k, s0:s1], k == 0, k == kd - 1)
                             for k in range(kd)], reads=[wd, d_hT], writes=[bd])
            self.side_tick()
            r = epilogue(fi, si, (s0, s1), bank, bd)
            if r is not None:
                yield from r

        self.pipeline(item(fi, c0, si, s0, s1) for fi, c0 in enumerate(cols) for si, (s0, s1) in enumerate(subs))

    def linear_tok(self, hT, d_hT, wview, col_groups, tok_tiles, epilogue, wtring):
        nc, fw = self.nc, self.fw
        for gi, c0 in enumerate(col_groups):
            wt, wd = wtring.next()
            fw.dma(fw.pool, wt[:], wview[:, :, c0:c0 + 512], writes=[wd])
            for ti, t0 in enumerate(tok_tiles):
                bank, bd = self.bank_ring.next()
                fw.group(fw.pe, [self.mm(bank[:, :], hT[:, k, t0:t0 + 128], wt[:, k, :], k == 0, k == KD - 1)
                                 for k in range(KD)], reads=[wd, d_hT], writes=[bd])
                self.side_tick()
                epilogue(gi, ti, t0, bank, bd)

    @staticmethod
    def nat(blk, col):
        h = blk % 2
        if col < 128:
            return h * 128 + col
        return CTX + h * 1024 + (col - 128)

    def phase_C1(self, l, ctx):
        nc, fw = self.nc, self.fw
        cf, d_cf = self.coef[l]
        wv = self.W[f"l{l}_w_out"].rearrange("(k p) m -> p k m", p=128)
        with self.phase("C1"):
            oT_ring = self.ring("oT", 2, [128, KD, TB], BF16)
            wring = self.ring("wC1", 4, [128, KD, 128], BF16)
            xc_ring = self.ring("xc1", 3, [128, TB], F32)
            xo_ring = self.ring("xo1", 3, [128, TB], F32)
            sq_ring = self.ring("sq1", 3, [128, TB], BF16)
            rs, d_rs = self.sb("rs1", [128, TB], F32)
            subs = self.subs(ctx)
            a0 = subs[0][0]
            saved_ring = self.bank_ring
            self.bank_ring = Ring(self.banks[0:5])
            oTs = {}

            for blk in range(4):
                b = blk // 2
                if blk not in oTs:
                    oTs[blk] = oT_ring.next()
                    fw.dma(fw.sp, oTs[blk][0][:, :, a0:TB], self.OT[blk].rearrange("k p t -> p k t")[:, :, a0:TB],
                           writes=[oTs[blk][1]])
                if blk + 1 < 4:
                    oTs[blk + 1] = oT_ring.next()
                    fw.dma(fw.sp, oTs[blk + 1][0][:, :, a0:TB], self.OT[blk + 1].rearrange("k p t -> p k t")[:, :, a0:TB],
                           writes=[oTs[blk + 1][1]])
                oT, d_oT = oTs[blk]
                state = {}
                st = self.stats_begin(subs)

                def epi(fi, si, rng, bank, bd):
                    s0, s1 = rng
                    if si == 0:
                        self.stats_flush(st)
                        state["xc"] = xc_ring.next()
                        state["xo"] = xo_ring.next()
                        state["sq"] = sq_ring.next()
                        fw.dma(fw.sp, state["xc"][0][:, a0:TB], self.xT[blk, fi, :, a0:TB], writes=[state["xc"][1]])
                    xc, d_xc = state["xc"]
                    xo, d_xo = state["xo"]
                    r = 2 if s1 <= 128 else b
                    fw.op(fw.dve, lambda: nc.vector.scalar_tensor_tensor(
                        xo[:, s0:s1], bank[:, 0:s1 - s0], cf[:, 32 + fi, r:r + 1], xc[:, s0:s1], ALU.mult, ALU.add),
                        reads=[bd, d_xc, d_cf], writes=[d_xo])
                    self.stats_push(st, xo, d_xo, fi, si, state["sq"][0], state["sq"][1])
                    if si == len(subs) - 1:
                        fw.dma(fw.sp, self.xT[blk, fi, :, a0:TB], xo[:, a0:TB], reads=[d_xo], writes=[Dep()])

                self.linear(oT, d_oT, KD, wv, [c * 128 for c in range(KD)], subs, epi, wring)
                self.stats_end(st, 1, blk, rs, d_rs)
            self.bank_ring = saved_ring

    def phase_C2(self, l, ctx):
        nc, fw = self.nc, self.fw
        cf, d_cf = self.coef[l]
        w_in = self.W[f"l{l}_ffn_w_in"].rearrange("(k p) m -> p k m", p=128)
        w_out = self.W[f"l{l}_ffn_w_out"].rearrange("(f p) m -> p f m", p=128)
        with self.phase("C2"):
            self.alloc_norm(n_xc=2)
            sq_ring = self.ring("sq2", 2, [128, TB], BF16)
            saved_ring = self.bank_ring
            self.bank_ring = Ring(self.banks[0:5])
            make_stats = l != self.layers[-1]
            hT, d_hT = self.sb("hT", [128, KD, TB], BF16)
            act, _ = self.sb("act", [128, NF, TB], BF16)
            wring = self.ring("wffn", 4, [128, KD, 128], BF16)
            woring = self.ring("wffo", 3, [128, NF // 2, 128], BF16)
            sg_ring = self.ring("sg", 2, [128, 512], F32)
            subs = self.subs(ctx)
            a0 = subs[0][0]
            self.norm_mod(l, 0, 1, ctx, hT, d_hT)
            for blk in range(4):
                b = blk // 2
                d_act = [Dep() for _ in range(NF)]
                for f in range(NF):
                    wg, d_wg = wring.next()
                    wu, d_wu = wring.next()
                    fw.dma(fw.pool, wg[:], w_in[:, :, f * 128:(f + 1) * 128], writes=[d_wg])
                    fw.dma(fw.pool, wu[:], w_in[:, :, FFN_H + f * 128:FFN_H + (f + 1) * 128], writes=[d_wu])
                    for (s0, s1) in subs:
                        n = s1 - s0
                        bg, d_bg = self.bank_ring.next()
                        bu, d_bu = self.bank_ring.next()
                        fw.group(fw.pe, [self.mm(bg[:, 0:n], wg[:, k, :], hT[:, k, s0:s1], k == 0, k == KD - 1)
                                         for k in range(KD)], reads=[d_wg, d_hT], writes=[d_bg])
                        fw.group(fw.pe, [self.mm(bu[:, 0:n], wu[:, k, :], hT[:, k, s0:s1], k == 0, k == KD - 1)
                                         for k in range(KD)], reads=[d_wu, d_hT], writes=[d_bu])
                        sg, d_sg = sg_ring.next()
                        fw.op(fw.act, lambda: nc.scalar.activation(sg[:, 0:n], bg[:, 0:n], AF.Silu),
                              reads=[d_bg], writes=[d_sg])
                        fw.op(fw.dve, lambda: nc.vector.tensor_tensor(act[:, f, s0:s1], sg[:, 0:n], bu[:, 0:n], ALU.mult),
                              reads=[d_sg, d_bu], writes=[d_act[f]])
                if blk < 3:
                    self.norm_mod(l, blk + 1, 1, ctx, hT, d_hT)
                H = NF // 2
                st = self.stats_begin(subs)
                for dch in range(KD):
                    wh = []
                    for hf in range(2):
                        wt, d_wt = woring.next()
                        fw.dma(fw.pool, wt[:], w_out[:, hf * H:(hf + 1) * H, dch * 128:(dch + 1) * 128], writes=[d_wt])
                        wh.append((wt, d_wt))
                    xc, d_xc = self.xc_ring.next()
                    fw.dma(fw.sp, xc[:, a0:TB], self.xT[blk, dch, :, a0:TB], writes=[d_xc])
                    xo, d_xo = self.tmp_ring.next()
                    obanks = [self.bank_ring.next() for _ in subs]
                    for hf in range(2):
                        wt, d_wt = wh[hf]
                        for (bank, bd), (s0, s1) in zip(obanks, subs):
                            n = s1 - s0
                            fw.group(fw.pe, [self.mm(bank[:, 0:n], wt[:, f, :], act[:, hf * H + f, s0:s1],
                                                     hf == 0 and f == 0, hf == 1 and f == H - 1) for f in range(H)],
                                     reads=[d_wt] + d_act[hf * H:(hf + 1) * H], writes=[bd])
                    self.stats_flush(st)
                    sq, d_sq = sq_ring.next()
                    for si, ((bank, bd), (s0, s1)) in enumerate(zip(obanks, subs)):
                        n = s1 - s0
                        r = 2 if s1 <= 128 else b
                        fw.op(fw.dve, lambda: nc.vector.scalar_tensor_tensor(
                            xo[:, s0:s1], bank[:, 0:n], cf[:, 80 + dch, r:r + 1], xc[:, s0:s1], ALU.mult, ALU.add),
                            reads=[bd, d_xc, d_cf], writes=[d_xo])
                        if make_stats:
                            self.stats_push(st, xo, d_xo, dch, si, sq, d_sq)
                    fw.dma(fw.sp, self.xT[blk, dch, :, a0:TB], xo[:, a0:TB], reads=[d_xo], writes=[Dep()])
                if make_stats:
                    rs, d_rs = self.tmp_ring.next()
                    self.stats_end(st, 0, blk, rs, d_rs)
            self.bank_ring = saved_ring

    def fnet_A(self, l):
        nc, fw = self.nc, self.fw
        with self.phase("fnetA"):
            self.alloc_norm()
            hT, d_hT = self.sb("hT", [128, KD, TB], BF16)
            dc, d_dc = self.sb("dftc", [128, 2, 4, 512], BF16)
            st_ring = self.ring("abst", 4, [128, 512], BF16)
            fw.dma(fw.sp, dc[:].rearrange("p a k m -> p (a k m)"), self.dftc_d, writes=[d_dc])
            for blk in range(4):
                b, h = divmod(blk, 2)
                self.norm_mod(l, blk, 0, False, hT, d_hT)
                for t in range(8):
                    c0 = 128 + t * 128
                    tt = h * 8 + t
                    for g in range(4):
                        for a in range(2):
                            bank, bd = self.bank_ring.next()
                            fw.group(fw.pe, [self.mm(bank[:, :], hT[:, 4 * g + kc, c0:c0 + 128], dc[:, a, kc, :],
                                                     kc == 0, kc == 3) for kc in range(4)],
                                     reads=[d_hT, d_dc], writes=[bd])
                            st, d_st = st_ring.next()
                            if a == 0:
                                fw.op(fw.act, lambda: nc.scalar.copy(st[:], bank[:]), reads=[bd], writes=[d_st])
                            else:
                                fw.op(fw.dve, lambda: nc.vector.tensor_scalar(st[:], bank[:], -1.0, None, ALU.mult),
                                      reads=[bd], writes=[d_st])
                            fw.dma(fw.sp, self.AB[b, g, a, tt], st[:], reads=[d_st], writes=[Dep()])

    def fnet_B(self, l):
        nc, fw = self.nc, self.fw
        with self.phase("fnetB"):
            dl, d_dl = self.sb("dftl", [128, 2, 16, SEQ], BF16)
            ab_ring = self.ring("ab", 2, [128, 2, 16, 512], BF16)
            st_ring = self.ring("yst", 3, [128, 512], BF16)
            for a in range(2):
                for q in range(4):
                    fw.dma(fw.sp, dl[:, a, q * 4:(q + 1) * 4, :].rearrange("p k m -> p (k m)"),
                           self.dftl_d[:, (a * 16 + q * 4) * SEQ:(a * 16 + q * 4 + 4) * SEQ], writes=[d_dl])
            for b in range(2):
                for g in range(4):
                    ab, d_ab = ab_ring.next()
                    for a in range(2):
                        fw.dma(fw.sp, ab[:, a], self.AB[b, g, a].rearrange("t p c -> p t c"), writes=[d_ab])
                    for cc in range(4):
                        for tb in range(4):
                            bank, bd = self.bank_ring.next()
                            fns = []
                            for a in range(2):
                                for tt in range(16):
                                    fns.append(self.mm(bank[:, :], ab[:, a, tt, cc * 128:(cc + 1) * 128],
                                                       dl[:, a, tt, tb * 512:(tb + 1) * 512],
                                                       a == 0 and tt == 0, a == 1 and tt == 15))
                            fw.group(fw.pe, fns, reads=[d_ab, d_dl], writes=[bd])
                            st, d_st = st_ring.next()
                            if (cc + tb) % 2 == 0:
                                fw.op(fw.act, lambda: nc.scalar.copy(st[:], bank[:]), reads=[bd], writes=[d_st])
                            else:
                                fw.op(fw.dve, lambda: nc.vector.tensor_copy(st[:], bank[:]), reads=[bd], writes=[d_st])
                            blk = b * 2 + tb // 2
                            col = 128 + (tb % 2) * 512
                            fw.dma(fw.sp, self.OT[blk, 4 * g + cc, :, col:col + 512], st[:], reads=[d_st], writes=[Dep()])

    def attn_A(self, l, diff):
        nc, fw = self.nc, self.fw
        p = f"l{l}_"
        wv = self.W[p + "w_in"].rearrange("(k p) m -> p k m", p=128)
        nq = 16
        nk = 16 if diff else 4
        v0 = (nq + nk) * 128
        nvg = 4 if diff else 1
        with self.phase("attnA"):
            self.alloc_norm()
            hT, d_hT = self.sb("hT", [128, KD, TB], BF16)
            rope, d_rope = self.sb("rope", [128, 2, SEQ], F32)
            RT, d_RT = self.sb("RT", [128, 128], BF16)
            qk, d_qk = self.sb("qk", [128, 2], F32)
            wring = self.ring("wA", 4, [128, KD, 128], BF16)
            wtring = self.ring("wtA", 2, [128, KD, 512], BF16)
            sq_r = self.ring("sqA", 3, [128, 512], BF16)
            raw_r = self.ring("rawA", 2, [128, 512], F32)
            t_r = self.ring("tA", 2, [128, 512], F32)
            qn_r = self.ring("qnA", 3, [128, 512], BF16)
            t1_r = self.ring("t1A", 2, [128, 512], F32)
            t2_r = self.ring("t2A", 2, [128, 512], F32)
            qf_r = self.ring("qfA", 3, [128, 512], BF16)
            vst_r = self.ring("vstA", 3, [128, 512], BF16)
            fw.dma(fw.sp, rope[:].rearrange("p a t -> p (a t)"), self.rope_d, writes=[d_rope])
            fw.dma(fw.sp, RT[:], self.RT_d, writes=[d_RT])
            fw.dma(fw.sp, qk[:], self.W[p + "qk"], writes=[d_qk])
            self.host_side(l)
            for blk in range(4):
                b, h = divmod(blk, 2)
                self.norm_mod(l, blk, 0, True, hT, d_hT)

                def epi(fi, si, rng, bank, bd):
                    s0, s1 = rng
                    n = s1 - s0
                    is_q = fi < nq
                    is_ctx = s1 <= 128
                    if diff and is_q and is_ctx:
                        return
                    g = qk[:, 0:1] if is_q else qk[:, 1:2]
                    sq, d_sq = sq_r.next()
                    fw.op(fw.act, lambda: nc.scalar.activation(sq[:, 0:n], bank[:, 0:n], AF.Square), reads=[bd], writes=[d_sq])
                    yield
                    ssb, d_ssb = self.bank_ring.next()
                    fw.group(fw.pe, [self.mm(ssb[:, 0:n], self.onesB[:], sq[:, 0:n], True, True)],
                             reads=[d_sq, self.d_ones], writes=[d_ssb])
                    t, d_t = t_r.next()
                    fw.op(fw.act, lambda: nc.scalar.activation(t[:, 0:n], ssb[:, 0:n], AF.Ln, bias=self.epsc[:, 0:1],
                                                               scale=1.0 / 128), reads=[d_ssb, self.d_epsc], writes=[d_t])
                    fw.op(fw.act, lambda: nc.scalar.activation(t[:, 0:n], t[:, 0:n], AF.Exp, scale=-0.5), reads=[d_t], writes=[d_t])
                    nat0 = self.nat(blk, s0)
                    if is_ctx:
                        qf, d_qf = qf_r.next()
                        fw.op(fw.dve, lambda: nc.vector.scalar_tensor_tensor(qf[:, 0:n], bank[:, 0:n], g, t[:, 0:n],
                                                                             ALU.mult, ALU.mult),
                              reads=[bd, d_t, d_qk], writes=[d_qf])
                        fw.dma(fw.sp, self.PA[b, fi, :, nat0:nat0 + n], qf[:, 0:n], reads=[d_qf], writes=[Dep()])
                        return
                    qn, d_qn = qn_r.next()
                    fw.op(fw.dve, lambda: nc.vector.scalar_tensor_tensor(qn[:, 0:n], bank[:, 0:n], g, t[:, 0:n],
                                                                         ALU.mult, ALU.mult),
                          reads=[bd, d_t, d_qk], writes=[d_qn])
                    yield
                    rb, d_rb = self.bank_ring.next()
                    fw.group(fw.pe, [self.mm(rb[:, 0:n], RT[:], qn[:, 0:n], True, True)], reads=[d_qn, d_RT], writes=[d_rb])
                    lt0 = nat0 - CTX
                    t1, d_t1 = t1_r.next()
                    fw.op(fw.pool, lambda: nc.gpsimd.tensor_tensor(t1[:, 0:n], qn[:, 0:n], rope[:, 0, lt0:lt0 + n], ALU.mult),
                          reads=[d_qn, d_rope], writes=[d_t1])
                    t2, d_t2 = t2_r.next()
                    fw.op(fw.dve, lambda: nc.vector.tensor_tensor(t2[:, 0:n], rb[:, 0:n], rope[:, 1, lt0:lt0 + n], ALU.mult),
                          reads=[d_rb, d_rope], writes=[d_t2])
                    qf, d_qf = qf_r.next()
                    fw.op(fw.pool, lambda: nc.gpsimd.tensor_tensor(qf[:, 0:n], t1[:, 0:n], t2[:, 0:n], ALU.add),
                          reads=[d_t1, d_t2], writes=[d_qf])
                    fw.dma(fw.sp, self.PA[b, fi, :, nat0:nat0 + n], qf[:, 0:n], reads=[d_qf], writes=[Dep()])

                self.linear(hT, d_hT, KD, wv, [c * 128 for c in range(nq + nk)], self.subs(True), epi, wring)

                def epi_v(gi, ti, t0, bank, bd):
                    vs, d_vs = vst_r.next()
                    if ti % 2 == 0:
                        fw.op(fw.act, lambda: nc.scalar.copy(vs[:], bank[:]), reads=[bd], writes=[d_vs])
                    else:
                        fw.op(fw.dve, lambda: nc.vector.tensor_copy(vs[:], bank[:]), reads=[bd], writes=[d_vs])
                    n0 = self.nat(blk, t0)
                    fw.dma(fw.sp, self.VT[b, n0:n0 + 128, gi * 512:(gi + 1) * 512], vs[:], reads=[d_vs], writes=[Dep()])

                self.linear_tok(hT, d_hT, wv, [v0 + g * 512 for g in range(nvg)], [t * 128 for t in range(9)], epi_v, wtring)
            self.side_drain()

    def qblocks(self, b, n_lat, with_ctx):
        out = []
        if with_ctx:
            out.append((0, 256, [0, 1], [(b * 2, 0, 0, 128), (b * 2 + 1, 0, 128, 128)]))
        for j in range(SEQ // n_lat):
            lt = j * n_lat
            out.append((CTX + lt, n_lat, list(range(18)), [(b * 2 + lt // 1024, 128 + lt % 1024, 0, n_lat)]))
        return out

    def gqa_B(self, l):
        nc, fw = self.nc, self.fw
        scale = 128 ** -0.5
        with self.phase("gqaB"):
            k_r = self.ring("kB", 2, [128, NTOK], BF16)
            v_r = self.ring("vB", 2, [128, 18, 128], BF16)
            q_r = self.ring("qB", 3, [128, 512], BF16)
            e_r = self.ring("eB", 4, [128, 512], BF16)
            rd_r = self.ring("rdB", 2, [128, 512], F32)
            o_r = self.ring("oB", 2, [128, 512], BF16)
            st_r = Ring([self.banks[0], self.banks[1], self.banks[2], self.banks[7]])
            o_b = Ring([self.banks[3], self.banks[5]])
            d_b = Ring([self.banks[4], self.banks[6]])

            def qblock_items(b, kT, d_kT, Vg, d_V, hq, q0, n, kts, dsts):
                cx = {}
                last = len(kts) - 1

                def item(i):
                    kt = kts[i]
                    if i == 0:
                        cx["q"] = q_r.next()
                        fw.dma(fw.sp, cx["q"][0][:, 0:n], self.PA[b, hq, :, q0:q0 + n], writes=[cx["q"][1]])
                        cx["O"] = o_b.next()
                        cx["D"] = d_b.next()
                    q, d_q = cx["q"]
                    O, d_O = cx["O"]
                    Dn, d_D = cx["D"]
                    bk, bd = st_r.next()
                    fw.group(fw.pe, [self.mm(bk[:, 0:n], kT[:, kt * 128:(kt + 1) * 128], q[:, 0:n], True, True)],
                             reads=[d_kT, d_q], writes=[bd])
                    yield
                    E, d_E = e_r.next()
                    fw.op(fw.act, lambda: nc.scalar.activation(E[:, 0:n], bk[:, 0:n], AF.Exp, scale=scale),
                          reads=[bd], writes=[d_E])
                    yield
                    fw.group(fw.pe, [self.mm(O[:, 0:n], Vg[:, kt, :], E[:, 0:n], i == 0, i == last),
                                     self.mm(Dn[:, 0:n], self.onesB[:], E[:, 0:n], i == 0, i == last)],
                             reads=[d_E, d_V, self.d_ones], writes=[d_O, d_D])
                    if i == last:
                        rd, d_rd = rd_r.next()
                        fw.op(fw.dve, lambda: nc.vector.reciprocal(rd[:, 0:n], Dn[:, 0:n]), reads=[d_D], writes=[d_rd])
                        o, d_o = o_r.next()
                        fw.op(fw.dve, lambda: nc.vector.tensor_tensor(o[:, 0:n], O[:, 0:n], rd[:, 0:n], ALU.mult),
                              reads=[d_O, d_rd], writes=[d_o])
                        for (blk, col, off, ln) in dsts:
                            fw.dma(fw.sp, self.OT[blk, hq, :, col:col + ln], o[:, off:off + ln], reads=[d_o], writes=[Dep()])

                return [item(i) for i in range(len(kts))]

            def gens():
                for b in range(2):
                    for g in range(4):
                        kT, d_kT = k_r.next()
                        fw.dma(fw.sp, kT[:], self.PA[b, 16 + g], writes=[d_kT])
                        Vg, d_V = v_r.next()
                        fw.dma(fw.sp, Vg[:], self.VT[b, :, g * 128:(g + 1) * 128].rearrange("(t p) d -> p t d", p=128), writes=[d_V])
                        for j in range(4):
                            for (q0, n, kts, dsts) in self.qblocks(b, 512, True):
                                for it in qblock_items(b, kT, d_kT, Vg, d_V, 4 * g + j, q0, n, kts, dsts):
                                    yield it

            self.pipeline(gens())

    def diff_B(self, l):
        nc, fw = self.nc, self.fw
        scale = 128 ** -0.5
        lam_init = 0.8 - 0.6 * math.exp(-0.3 * l)
        p = f"l{l}_"
        with self.phase("diffB"):
            lv, d_lv = self.sb("lv", [128, 512], F32)
            lt, d_lt = self.sb("ltmp", [128, 2, 128], F32)
            ls, d_ls = self.sb("ls", [128, 4], F32)
            on, d_on = self.sb("on", [128, 2], F32)
            fw.dma(fw.sp, lv[:], self.W[p + "lvec"], writes=[d_lv])
            fw.dma(fw.sp, on[:], self.W[p + "onT"], writes=[d_on])
            for m in range(2):
                fw.op(fw.dve, lambda: nc.vector.tensor_tensor(lt[:, m, :], lv[:, m * 256:m * 256 + 128],
                                                              lv[:, m * 256 + 128:m * 256 + 256], ALU.mult),
                      reads=[d_lv], writes=[d_lt])
                fw.op(fw.dve, lambda: nc.vector.reduce_sum(ls[:, m:m + 1], lt[:, m, :], axis=AX.X), reads=[d_lt], writes=[d_ls])
            fw.op(fw.act, lambda: nc.scalar.activation(ls[:, 0:2], ls[:, 0:2], AF.Exp), reads=[d_ls], writes=[d_ls])
            fw.op(fw.dve, lambda: nc.vector.tensor_tensor(ls[:, 2:3], ls[:, 1:2], ls[:, 0:1], ALU.subtract), reads=[d_ls], writes=[d_ls])
            fw.op(fw.dve, lambda: nc.vector.tensor_scalar(ls[:, 3:4], ls[:, 2:3], -lam_init, None, ALU.add), reads=[d_ls], writes=[d_ls])
            fw.op(fw.dve, lambda: nc.vector.tensor_scalar(on[:], on[:], 1.0 - lam_init, None, ALU.mult), reads=[d_on], writes=[d_on])
            neglam = ls[:, 3:4]
            k_r = self.ring("kD", 2, [128, 2, NTOK], BF16)
            v_r = self.ring("vD", 2, [128, 18, 256], BF16)
            q_r = self.ring("qD", 3, [128, 2, 256], BF16)
            e_r = self.ring("eD", 4, [128, 512], BF16)
            rd_r = self.ring("rdD", 2, [128, 512], F32)
            t1_r = self.ring("t1D", 2, [128, 2, 256], F32)
            t2_r = self.ring("t2D", 2, [128, 2, 256], F32)
            sq_r = self.ring("sqD", 2, [128, 2, 256], BF16)
            rs_r = self.ring("rsD", 2, [128, 256], F32)
            o_r = self.ring("oD", 2, [128, 2, 256], BF16)
            st_r = Ring(self.banks[0:2])
            ssb, d_ssb = self.banks[2]
            Ob = [self.banks[3], self.banks[4], self.banks[5], self.banks[6]]
            Dn, d_D = self.banks[7]

            def qblock_items(b, h, kT, d_kT, Vh, d_V, q0, n, kts, dsts):
                cx = {}
                last = len(kts) - 1

                def item(i):
                    kt = kts[i]
                    if i == 0:
                        cx["q"] = q_r.next()
                        fw.dma(fw.sp, cx["q"][0][:], self.PA[b, 2 * h:2 * h + 2, :, q0:q0 + n].rearrange("c p t -> p c t"),
                               writes=[cx["q"][1]])
                    q, d_q = cx["q"]
                    bk, bd = st_r.next()
                    fw.group(fw.pe, [self.mm(bk[:, m * 256:(m + 1) * 256], kT[:, m, kt * 128:(kt + 1) * 128], q[:, m, :], True, True)
                                     for m in range(2)], reads=[d_kT, d_q], writes=[bd])
                    yield
                    E, d_E = e_r.next()
                    fw.op(fw.act, lambda: nc.scalar.activation(E[:], bk[:], AF.Exp, scale=scale), reads=[bd], writes=[d_E])
                    yield
                    fns = []
                    for m in range(2):
                        for dv in range(2):
                            fns.append(self.mm(Ob[m * 2 + dv][0][:, 0:256], Vh[:, kt, dv * 128:(dv + 1) * 128],
                                               E[:, m * 256:(m + 1) * 256], i == 0, i == last))
                    fns.append(self.mm(Dn[:, :], self.onesB[:], E[:], i == 0, i == last))
                    fw.group(fw.pe, fns, reads=[d_E, d_V, self.d_ones], writes=[x[1] for x in Ob] + [d_D])
                    if i != last:
                        return
                    t1, d_t1 = t1_r.next()
                    t2, d_t2 = t2_r.next()
                    for dv in range(2):
                        fw.op(fw.dve, lambda: nc.vector.tensor_copy(t1[:, dv, :], Ob[dv][0][:, 0:256]), reads=[Ob[dv][1]], writes=[d_t1])
                        fw.op(fw.act, lambda: nc.scalar.copy(t2[:, dv, :], Ob[2 + dv][0][:, 0:256]), reads=[Ob[2 + dv][1]], writes=[d_t2])
                    rd, d_rd = rd_r.next()
                    fw.op(fw.act, lambda: nc.scalar.activation(rd[:], Dn[:], AF.Ln), reads=[d_D], writes=[d_rd])
                    fw.op(fw.act, lambda: nc.scalar.activation(rd[:], rd[:], AF.Exp, scale=-1.0), reads=[d_rd], writes=[d_rd])
                    fw.op(fw.dve, lambda: nc.vector.tensor_tensor(t1[:], t1[:], rd[:, 0:256].unsqueeze(1).to_broadcast([128, 2, 256]), ALU.mult),
                          reads=[d_rd], writes=[d_t1])
                    fw.op(fw.dve, lambda: nc.vector.tensor_tensor(t2[:], t2[:], rd[:, 256:512].unsqueeze(1).to_broadcast([128, 2, 256]), ALU.mult),
                          reads=[d_rd], writes=[d_t2])
                    fw.op(fw.dve, lambda: nc.vector.scalar_tensor_tensor(t1[:], t2[:], neglam, t1[:], ALU.mult, ALU.add),
                          reads=[d_t1, d_t2, d_ls], writes=[d_t1])
                    sq, d_sq = sq_r.next()
                    fw.op(fw.act, lambda: nc.scalar.activation(sq[:], t1[:], AF.Square), reads=[d_t1], writes=[d_sq])
                    yield
                    fw.group(fw.pe, [self.mm(ssb[:, 0:256], self.onesB[:], sq[:, dv, :], dv == 0, dv == 1) for dv in range(2)],
                             reads=[d_sq, self.d_ones], writes=[d_ssb])
                    rs, d_rs = rs_r.next()
                    fw.op(fw.act, lambda: nc.scalar.activation(rs[:], ssb[:, 0:256], AF.Ln, bias=self.epsc[:, 0:1], scale=1.0 / 256),
                          reads=[d_ssb, self.d_epsc], writes=[d_rs])
                    fw.op(fw.act, lambda: nc.scalar.activation(rs[:], rs[:], AF.Exp, scale=-0.5), reads=[d_rs], writes=[d_rs])
                    o, d_o = o_r.next()
                    for dv in range(2):
                        fw.op(fw.dve, lambda: nc.vector.scalar_tensor_tensor(o[:, dv, :], t1[:, dv, :], on[:, dv:dv + 1], rs[:],
                                                                             ALU.mult, ALU.mult),
                              reads=[d_t1, d_rs, d_on], writes=[d_o])
                    (blk, col, off, ln) = dsts[0]
                    fw.dma(fw.sp, self.OT[blk, 2 * h:2 * h + 2, :, col:col + ln].rearrange("c p t -> p c t"), o[:],
                           reads=[d_o], writes=[Dep()])

                return [item(i) for i in range(len(kts))]

            def gens():
                for b in range(2):
                    for h in range(8):
                        kT, d_kT = k_r.next()
                        fw.dma(fw.sp, kT[:], self.PA[b, 16 + 2 * h:16 + 2 * h + 2].rearrange("c p t -> p c t"), writes=[d_kT])
                        Vh, d_V = v_r.next()
                        fw.dma(fw.sp, Vh[:], self.VT[b, :, h * 256:(h + 1) * 256].rearrange("(t p) d -> p t d", p=128), writes=[d_V])
                        for (q0, n, kts, dsts) in self.qblocks(b, 256, False):
                            for it in qblock_items(b, h, kT, d_kT, Vh, d_V, q0, n, kts, dsts):
                                yield it

            self.pipeline(gens())

    def gla_A(self, l):
        nc, fw = self.nc, self.fw
        p = f"l{l}_"
        wv = self.W[p + "w_in"].rearrange("(k p) m -> p k m", p=128)
        with self.phase("glaA"):
            self.alloc_norm()
            hT, d_hT = self.sb("hT", [128, KD, TB], BF16)
            wring = self.ring("wG", 4, [128, KD, 128], BF16)
            wtring = self.ring("wtG", 2, [128, KD, 512], BF16)
            zT, d_zT = self.sb("zT", [33, TB], F32)
            wga, d_wga = self.sb("wga", [33, 2, 1024], F32)
            f_r = self.ring("fstG", 3, [128, 512], F32)
            b_r = self.ring("bstG", 3, [128, 512], BF16)
            e_r = self.ring("etG", 2, [128, 512], F32)
            l_r = self.ring("lstG", 2, [128, 512], F32)
            fw.dma(fw.sp, wga[:, 0, :], self.W[p + "wgaf"], writes=[d_wga])
            fw.dma(fw.sp, wga[:, 1, :], self.W[p + "wgab"], writes=[d_wga])
            fw.op(fw.dve, lambda: nc.vector.memset(zT[32:33, :], 1.0), writes=[d_zT])
            subs = self.subs(True)
            self.host_side(l)
            for blk in range(4):
                b, h = divmod(blk, 2)
                self.norm_mod(l, blk, 0, True, hT, d_hT)

                def epi_z(fi, si, rng, bank, bd):
                    s0, s1 = rng
                    fw.op(fw.act, lambda: nc.scalar.copy(zT[0:32, s0:s1], bank[0:32, 0:s1 - s0]), reads=[bd], writes=[d_zT])

                self.linear(hT, d_hT, KD, wv, [6144], subs, epi_z, wring, mw=32)
                for ti in range(9):
                    t0 = ti * 128
                    n0 = self.nat(blk, t0)
                    for dirn in range(2):
                        for half in range(2):
                            bank, bd = self.bank_ring.next()
                            fw.group(fw.pe, [self.mm(bank[:, :], zT[0:33, t0:t0 + 128],
                                                     wga[0:33, dirn, half * 512:(half + 1) * 512], True, True)],
                                     reads=[d_zT, d_wga], writes=[bd])
                            et, d_et = e_r.next()
                            fw.op(fw.act, lambda: nc.scalar.activation(et[:], bank[:], AF.Exp, scale=-1.0), reads=[bd], writes=[d_et])
                            ls, d_l = l_r.next()
                            fw.op(fw.act, lambda: nc.scalar.activation(ls[:], et[:], AF.Ln, bias=self.epsc[:, 1:2], scale=1.0),
                                  reads=[d_et, self.d_epsc], writes=[d_l])
                            c0 = dirn * 1024 + half * 512
                            fw.dma(fw.sp, self.LT[b, n0:n0 + 128, c0:c0 + 512], ls[:], reads=[d_l], writes=[Dep()])

                def epi_qk(fi, si, rng, bank, bd):
                    s0, s1 = rng
                    n = s1 - s0
                    st, d_st = f_r.next()
                    if (fi + si) % 2 == 0:
                        fw.op(fw.act, lambda: nc.scalar.copy(st[:, 0:n], bank[:, 0:n]), reads=[bd], writes=[d_st])
                    else:
                        fw.op(fw.dve, lambda: nc.vector.tensor_copy(st[:, 0:n], bank[:, 0:n]), reads=[bd], writes=[d_st])
                    n0 = self.nat(blk, s0)
                    fw.dma(fw.sp, self.PAF[b, fi, :, n0:n0 + n], st[:, 0:n], reads=[d_st], writes=[Dep()])

                self.linear(hT, d_hT, KD, wv, [c * 128 for c in range(16)], subs, epi_qk, wring)

                def epi_r(fi, si, rng, bank, bd):
                    s0, s1 = rng
                    n = s1 - s0
                    st, d_st = b_r.next()
                    fw.op(fw.act, lambda: nc.scalar.activation(st[:, 0:n], bank[:, 0:n], AF.Silu), reads=[bd], writes=[d_st])
                    n0 = self.nat(blk, s0)
                    fw.dma(fw.sp, self.SR[b, fi, :, n0:n0 + n], st[:, 0:n], reads=[d_st], writes=[Dep()])

                self.linear(hT, d_hT, KD, wv, [4096 + c * 128 for c in range(16)], subs, epi_r, wring)

                def epi_kv(gi, ti, t0, bank, bd):
                    st, d_st = b_r.next()
                    if ti % 2 == 0:
                        fw.op(fw.act, lambda: nc.scalar.copy(st[:], bank[:]), reads=[bd], writes=[d_st])
                    else:
                        fw.op(fw.dve, lambda: nc.vector.tensor_copy(st[:], bank[:]), reads=[bd], writes=[d_st])
                    n0 = self.nat(blk, t0)
                    fw.dma(fw.sp, self.VT[b, n0:n0 + 128, gi * 512:(gi + 1) * 512], st[:], reads=[d_st], writes=[Dep()])

                self.linear_tok(hT, d_hT, wv, [1024 + g * 512 for g in range(6)], [t * 128 for t in range(9)], epi_kv, wtring)
            self.side_drain()

    def gla_B(self, l):
        nc, fw = self.nc, self.fw
        p = f"l{l}_"
        with self.phase("glaB"):
            gc, d_gc = self.sb("gc", [128, 1024], F32)
            on, d_on = self.sb("onG", [128, 4], F32)
            fw.dma(fw.sp, gc[:], self.gla_c, writes=[d_gc])
            fw.dma(fw.sp, on[:], self.W[p + "onT"], writes=[d_on])
            S, _ = self.sb("S", [128, 4, 2, 512], F32)
            Sb, _ = self.sb("Sb", [128, 4, 2, 512], BF16)
            d_S = [Dep() for _ in range(4)]
            d_Sb = [Dep() for _ in range(4)]
            qk_r = self.ring("qkG", 3, [128, 16, 128], F32)
            kv_r = self.ring("kvG", 3, [128, 3072], BF16)
            lt_r = self.ring("ltG", 3, [128, 1024], F32)
            of_r = self.ring("ofG", 4, [128, 16, 128], F32)
            sr_r = self.ring("srG", 4, [128, 16, 128], BF16)
            e13_r = self.ring("e13", 4, [128, 2, 256], F32)
            e2_r = self.ring("e2", 2, [128, 2, 128], F32)
            qt_r = self.ring("qtG", 3, [128, 2, 128], BF16)
            kt_r = self.ring("ktG", 3, [128, 2, 128], BF16)
            qi_r = self.ring("qiG", 5, [128, 2, 128], BF16)
            er_r = self.ring("erG", 2, [128, 256], F32)
            kc_r = self.ring("kcG", 5, [128, 256], BF16)
            am_r = self.ring("amG", 3, [128, 128], BF16)
            os_r = self.ring("osG", 3, [128, 4, 128], F32)
            sq_r = self.ring("sqG", 3, [128, 4, 128], BF16)
            rs_r = self.ring("rsG", 2, [128, 128], F32)
            tm_r = self.ring("tmG", 2, [128, 4, 128], F32)
            fin_r = self.ring("finG", 2, [128, 4, 128], BF16)
            xb_r = Ring(self.banks[0:2])
            rb_r = Ring(self.banks[2:4])
            ab_r = Ring(self.banks[4:5])
            ob_r = Ring(self.banks[5:6])
            sk_r = Ring(self.banks[6:7])
            ss_r = Ring(self.banks[7:8])

            for b in range(2):
                for dirn in range(2):
                    for hh in range(4):
                        fw.op(fw.dve, lambda: nc.vector.memset(S[:, hh], 0.0), writes=[d_S[hh]])
                        fw.op(fw.pool, lambda: nc.gpsimd.memset(Sb[:, hh], 0.0), writes=[d_Sb[hh]])
                    order = list(range(18)) if dirn == 0 else [1, 0] + list(range(17, 1, -1))
                    TT = gc[:, dirn * 256:(dirn + 1) * 256]
                    TriS = gc[:, 512 + dirn * 128:512 + (dirn + 1) * 128]
                    mask = gc[:, 768 + dirn * 128:768 + (dirn + 1) * 128]
                    dcol = 128 + (127 if dirn == 0 else 0)
                    tiles = {}

                    def load_tile(pos, b=b, dirn=dirn, order=order, tiles=tiles):
                        if pos >= len(order) or pos in tiles:
                            return
                        t0 = order[pos] * 128
                        d = {}
                        d["lt"] = lt_r.next()
                        fw.dma(fw.sp, d["lt"][0][:], self.LT[b, t0:t0 + 128, dirn * 1024:(dirn + 1) * 1024], writes=[d["lt"][1]])
                        d["qk"] = qk_r.next()
                        fw.dma(fw.sp, d["qk"][0][:], self.PAF[b, :, :, t0:t0 + 128].rearrange("c p t -> p c t"), writes=[d["qk"][1]])
                        d["kv"] = kv_r.next()
                        fw.dma(fw.sp, d["kv"][0][:], self.VT[b, t0:t0 + 128, :], writes=[d["kv"][1]])
                        if dirn == 1:
                            d["of"] = of_r.next()
                            fw.dma(fw.sp, d["of"][0][:], self.OF[b, :, :, t0:t0 + 128].rearrange("c p t -> p c t"), writes=[d["of"][1]])
                            d["sr"] = sr_r.next()
                            fw.dma(fw.sp, d["sr"][0][:], self.SR[b, :, :, t0:t0 + 128].rearrange("c p t -> p c t"), writes=[d["sr"][1]])
                        tiles[pos] = d

                    def item(pos, hh, b=b, dirn=dirn, order=order, tiles=tiles, TT=TT, TriS=TriS, mask=mask, dcol=dcol):
                        t = order[pos]
                        t0 = t * 128
                        if hh == 0:
                            load_tile(pos)
                            load_tile(pos + 1)
                        td = tiles[pos]
                        lt, d_lt = td["lt"]
                        qk, d_qk = td["qk"]
                        kv, d_kv = td["kv"]
                        if t < 2:
                            oblk, ocol = b * 2 + t, 0
                        else:
                            ltok = (t - 2) * 128
                            oblk, ocol = b * 2 + ltok // 1024, 128 + ltok % 1024
                        xb, d_xb = xb_r.next()
                        fw.group(fw.pe, [self.mm(xb[:, dc * 256:(dc + 1) * 256], lt[:, hh * 256 + dc * 128:hh * 256 + (dc + 1) * 128],
                                                 TT, True, True) for dc in range(2)], reads=[d_lt, d_gc], writes=[d_xb])
                        rbb, d_rb = rb_r.next()
                        fw.group(fw.pe, [self.mm(rbb[:, 0:256], TriS, lt[:, hh * 256:(hh + 1) * 256], True, True)],
                                 reads=[d_lt, d_gc], writes=[d_rb])
                        yield
                        xv = xb[:].rearrange("p (c i) -> p c i", c=2)
                        e13, d_e13 = e13_r.next()
                        fw.op(fw.act, lambda: nc.scalar.activation(e13[:], xv, AF.Exp), reads=[d_xb], writes=[d_e13])
                        e2, d_e2 = e2_r.next()
                        fw.op(fw.act, lambda: nc.scalar.activation(e2[:], xv[:, :, 0:128], AF.Exp, scale=-1.0), reads=[d_xb], writes=[d_e2])
                        er, d_er = er_r.next()
                        fw.op(fw.act, lambda: nc.scalar.activation(er[:], rbb[:, 0:256], AF.Exp), reads=[d_rb], writes=[d_er])
                        qv = qk[:, 2 * hh:2 * hh + 2, :]
                        kvv = qk[:, 8 + 2 * hh:8 + 2 * hh + 2, :]
                        qt, d_qt = qt_r.next()
                        fw.op(fw.dve, lambda: nc.vector.scalar_tensor_tensor(qt[:], qv, 0.0625, e13[:, :, 0:128], ALU.mult, ALU.mult),
                              reads=[d_qk, d_e13], writes=[d_qt])
                        kt_, d_kt = kt_r.next()
                        fw.op(fw.pool, lambda: nc.gpsimd.tensor_tensor(kt_[:], kvv, e2[:], ALU.mult), reads=[d_qk, d_e2], writes=[d_kt])
                        qi, d_qi = qi_r.next()
                        fw.op(fw.dve, lambda: nc.vector.scalar_tensor_tensor(qi[:], qv, 0.0625, e13[:, :, 128:256], ALU.mult, ALU.mult),
                              reads=[d_qk, d_e13], writes=[d_qi])
                        kc, d_kc = kc_r.next()
                        fw.op(fw.pool, lambda: nc.gpsimd.tensor_tensor(kc[:], kv[:, hh * 256:(hh + 1) * 256], er[:], ALU.mult),
                              reads=[d_kv, d_er], writes=[d_kc])
                        yield
                        abb, d_ab = ab_r.next()
                        fw.group(fw.pe, [self.mm(abb[:, 0:128], kt_[:, dc, :], qt[:, dc, :], dc == 0, dc == 1) for dc in range(2)],
                                 reads=[d_kt, d_qt], writes=[d_ab])
                        am, d_am = am_r.next()
                        fw.op(fw.dve, lambda: nc.vector.tensor_tensor(am[:], abb[:, 0:128], mask, ALU.mult), reads=[d_ab, d_gc], writes=[d_am])
                        yield
                        ob, d_ob = ob_r.next()
                        fns = []
                        for dvc in range(4):
                            oo = ob[:, dvc * 128:(dvc + 1) * 128]
                            vcol = 1024 + hh * 512 + dvc * 128
                            fns.append(self.mm(oo, kv[:, vcol:vcol + 128], am[:], True, False))
                            for dc in range(2):
                                fns.append(self.mm(oo, Sb[:, hh, dc, dvc * 128:(dvc + 1) * 128], qi[:, dc, :], False, dc == 1))
                        fw.group(fw.pe, fns, reads=[d_kv, d_am, d_Sb[hh], d_qi], writes=[d_ob])
                        for dc in range(2):
                            sbk, d_sbk = sk_r.next()
                            fw.group(fw.pe, [self.mm(sbk[:, :], kc[:, dc * 128:(dc + 1) * 128],
                                                     kv[:, 1024 + hh * 512:1024 + (hh + 1) * 512], True, True)],
                                     reads=[d_kc, d_kv], writes=[d_sbk])
                            fw.op(fw.dve, lambda: nc.vector.scalar_tensor_tensor(S[:, hh, dc, :], S[:, hh, dc, :],
                                                                                 e13[:, dc, dcol:dcol + 1], sbk[:, :],
                                                                                 ALU.mult, ALU.add),
                                  reads=[d_sbk, d_e13], writes=[d_S[hh]])
                            fw.op(fw.act, lambda: nc.scalar.copy(Sb[:, hh, dc, :], S[:, hh, dc, :]), reads=[d_S[hh]], writes=[d_Sb[hh]])
                        ov = ob[:].rearrange("p (c i) -> p c i", c=4)
                        os_, d_os = os_r.next()
                        if dirn == 0:
                            fw.op(fw.act, lambda: nc.scalar.copy(os_[:], ov), reads=[d_ob], writes=[d_os])
                            fw.dma(fw.sp, self.OF[b, 4 * hh:4 * hh + 4, :, t0:t0 + 128].rearrange("c p t -> p c t"), os_[:],
                                   reads=[d_os], writes=[Dep()])
                            return
                        of, d_of = td["of"]
                        sr, d_sr = td["sr"]
                        fw.op(fw.dve, lambda: nc.vector.tensor_tensor(os_[:], ov, of[:, 4 * hh:4 * hh + 4, :], ALU.add),
                              reads=[d_ob, d_of], writes=[d_os])
                        sq, d_sq = sq_r.next()
                        fw.op(fw.act, lambda: nc.scalar.activation(sq[:], os_[:], AF.Square), reads=[d_os], writes=[d_sq])
                        yield
                        ssb, d_ssb = ss_r.next()
                        fw.group(fw.pe, [self.mm(ssb[:, 0:128], self.onesB[:], sq[:, dvc, :], dvc == 0, dvc == 3) for dvc in range(4)],
                                 reads=[d_sq, self.d_ones], writes=[d_ssb])
                        rs, d_rs = rs_r.next()
                        fw.op(fw.act, lambda: nc.scalar.activation(rs[:], ssb[:, 0:128], AF.Ln, bias=self.epsc[:, 0:1], scale=1.0 / 512),
                              reads=[d_ssb, self.d_epsc], writes=[d_rs])
                        fw.op(fw.act, lambda: nc.scalar.activation(rs[:], rs[:], AF.Exp, scale=-0.5), reads=[d_rs], writes=[d_rs])
                        tm, d_tm = tm_r.next()
                        fw.op(fw.dve, lambda: nc.vector.tensor_tensor(tm[:], os_[:], rs[:].unsqueeze(1).to_broadcast([128, 4, 128]), ALU.mult),
                              reads=[d_os, d_rs], writes=[d_tm])
                        fw.op(fw.pool, lambda: nc.gpsimd.tensor_tensor(tm[:], tm[:], on[:].unsqueeze(2).to_broadcast([128, 4, 128]), ALU.mult),
                              reads=[d_tm, d_on], writes=[d_tm])
                        fin, d_fin = fin_r.next()
                        fw.op(fw.pool, lambda: nc.gpsimd.tensor_tensor(fin[:], tm[:], sr[:, 4 * hh:4 * hh + 4, :], ALU.mult),
                              reads=[d_tm, d_sr], writes=[d_fin])
                        fw.dma(fw.sp, self.OT[oblk, 4 * hh:4 * hh + 4, :, ocol:ocol + 128].rearrange("c p t -> p c t"), fin[:],
                               reads=[d_fin], writes=[Dep()])

                    self.pipeline(item(pos, hh) for pos in range(18) for hh in range(4))


def _vecT(v, k):
    return np.ascontiguousarray(np.asarray(v, np.float32).reshape(k, 128).T)


def _bf(a):
    return np.ascontiguousarray(np.asarray(a).astype(ml_dtypes.bfloat16))


def _pk(m):
    K = m.shape[0] // 128
    return np.ascontiguousarray(m.reshape(K, 128, m.shape[1]).transpose(1, 0, 2).reshape(128, K * m.shape[1]))


_CONST_CACHE = {}


def _constants(layers):
    key = tuple(layers)
    if key in _CONST_CACHE:
        return _CONST_CACHE[key]
    c = {}
    c["identF"] = np.eye(128, dtype=np.float32)
    c["onesB"] = _bf(np.ones((128, 128), np.float32))
    if 1 in layers or 2 in layers:
        t = np.arange(SEQ)
        row = (t // 64).astype(np.float32)
        col = (t % 64).astype(np.float32)
        half = 64
        inv_freq = (np.float32(10000.0) ** (-np.arange(0, half, 2, dtype=np.float32) / np.float32(half))).astype(np.float32)
        ang_r = row[:, None] * inv_freq[None, :]
        ang_c = col[:, None] * inv_freq[None, :]
        ang = np.concatenate([ang_r, ang_r, ang_c, ang_c], axis=-1).astype(np.float32)
        c["rope"] = np.ascontiguousarray(np.concatenate([np.cos(ang).T, np.sin(ang).T], axis=1).astype(np.float32))
        R = np.zeros((128, 128), np.float32)
        for d in range(128):
            q = d // 32
            if q % 2 == 0:
                R[d, d + 32] = -1.0
            else:
                R[d, d - 32] = 1.0
        c["RT"] = _bf(R.T)
    if 3 in layers:
        def dft(n):
            k = np.arange(n, dtype=np.float64)
            ang = 2.0 * np.pi * np.outer(k, k) / n
            return np.cos(ang) / np.sqrt(n), np.sin(ang) / np.sqrt(n)
        cc, sc = dft(512)
        cl, sl = dft(SEQ)
        c["dftc"] = _bf(np.concatenate([_pk(cc), _pk(sc)], axis=1))
        c["dftl"] = _bf(np.concatenate([_pk(cl), _pk(sl)], axis=1))
    if 0 in layers:
        j = np.arange(128)[:, None]
        i = np.arange(128)[None, :]
        s = -1.0 / 16.0
        tri_f = (j <= i).astype(np.float64)
        tri_b = (j >= i).astype(np.float64)
        m1_f = tri_f - (j <= 63)
        m1_b = tri_b - (j >= 64)
        tris_f = (j > i).astype(np.float64)
        tris_b = (j < i).astype(np.float64)
        c["gla_c"] = np.ascontiguousarray(np.concatenate(
            [s * m1_f, s * tri_f, s * m1_b, s * tri_b, s * tris_f, s * tris_b, tri_f, tri_b], axis=1).astype(np.float32))
    _CONST_CACHE[key] = c
    return c


def _prep_inputs(inputs, layers, x_over=None, ctx_over=None):
    f32 = lambda a: np.ascontiguousarray(np.asarray(a, np.float32))
    shared = dict(_constants(layers))
    for l in layers:
        p = f"l{l}_"
        shared[p + "mod_w"] = f32(inputs[p + "mod_w"])
        shared[p + "mod_bT"] = _vecT(inputs[p + "mod_b"], 96)
        shared[p + "n1T"] = _vecT(inputs[p + "norm1"], KD)
        shared[p + "n2T"] = _vecT(inputs[p + "norm2"], KD)
        shared[p + "ffn_w_in"] = f32(inputs[p + "ffn_w_in"])
        shared[p + "ffn_w_out"] = f32(inputs[p + "ffn_w_out"])
        if l == 0:
            shared[p + "w_in"] = f32(inputs[p + "gla_w_in"])
            z16 = np.zeros((16, 1024), np.float32)
            shared[p + "wgaf"] = np.ascontiguousarray(np.concatenate(
                [f32(inputs[p + "gla_wg_f"]), z16, f32(inputs[p + "gla_bg_f"])[None]], axis=0))
            shared[p + "wgab"] = np.ascontiguousarray(np.concatenate(
                [z16, f32(inputs[p + "gla_wg_b"]), f32(inputs[p + "gla_bg_b"])[None]], axis=0))
            shared[p + "onT"] = _vecT(inputs[p + "gla_out_norm"], 4)
            shared[p + "w_out"] = f32(inputs[p + "gla_w_out"])
        elif l == 1:
            shared[p + "w_in"] = f32(inputs[p + "gqa_w_in"])
            shared[p + "qk"] = np.ascontiguousarray(np.stack([f32(inputs[p + "gqa_q_norm"]), f32(inputs[p + "gqa_k_norm"])], axis=1))
            shared[p + "w_out"] = f32(inputs[p + "gqa_w_out"])
        elif l == 2:
            shared[p + "w_in"] = f32(inputs[p + "diff_w_in"])
            shared[p + "qk"] = np.ascontiguousarray(np.stack([f32(inputs[p + "diff_q_norm"]), f32(inputs[p + "diff_k_norm"])], axis=1))
            lv = np.concatenate([f32(inputs[p + "diff_lq1"]), f32(inputs[p + "diff_lk1"]),
                                 f32(inputs[p + "diff_lq2"]), f32(inputs[p + "diff_lk2"])])
            shared[p + "lvec"] = np.ascontiguousarray(np.broadcast_to(lv[None, :], (128, 512)))
            shared[p + "onT"] = _vecT(inputs[p + "diff_out_norm"], 2)
            shared[p + "w_out"] = f32(inputs[p + "diff_w_out"])
        else:
            shared[p + "w_out"] = f32(inputs[p + "fnet_w_out"])
    x = f32(inputs["x"]) if x_over is None else x_over
    ctx = f32(inputs["ctx"]) if ctx_over is None else ctx_over
    c = f32(inputs["c"])
    c_ctx = f32(inputs["c_ctx"])
    nb = x.shape[0] // 2
    maps = []
    for i in range(nb):
        m = dict(shared)
        m["x"] = np.ascontiguousarray(x[2 * i:2 * i + 2])
        m["ctx"] = np.ascontiguousarray(ctx[2 * i:2 * i + 2])
        cv = np.stack([c[2 * i], c[2 * i + 1], c_ctx], axis=0)
        m["cvecT"] = np.ascontiguousarray(cv.reshape(3, KD, 128).transpose(2, 1, 0).reshape(128, KD * 3))
        maps.append(m)
    return maps


_NC_CACHE = {}


def _get_nc(layers):
    key = tuple(layers)
    if key not in _NC_CACHE:
        _NC_CACHE[key] = Builder(layers).build()
    return _NC_CACHE[key]


def run_layers(inputs, layers, x_over=None, ctx_over=None, trace=False):
    maps = _prep_inputs(inputs, layers, x_over, ctx_over)
    nc = _get_nc(layers)
    res = run_bass_kernel_spmd(nc, maps, core_ids=list(range(len(maps))), trace=trace)
    out = np.concatenate([r["y"] for r in res.results], axis=0)
    return out, res


def kernel(**inputs):
    out, _ = run_layers(inputs, (0, 1, 2, 3))
    return out.astype(np.float32, copy=False)
```

```python
import math
from contextlib import ExitStack, contextmanager

import numpy as np
import ml_dtypes

import concourse.bass as bass
import concourse.mybir as mybir
from concourse.bass_utils import run_bass_kernel_spmd

F32 = mybir.dt.float32
BF16 = mybir.dt.bfloat16
AF = mybir.ActivationFunctionType
ALU = mybir.AluOpType
AX = mybir.AxisListType

N_CORES = 8
D = 2048
KD = 16
SEQ = 2048
CTX = 256
NTOK = CTX + SEQ
TB = 1152
FFN_H = 5632
NF = FFN_H // 128
EPS = 1e-6
SEM_LIMIT = 30000


class Dep:
    __slots__ = ("w", "r", "ep")

    def __init__(self):
        self.w = {}
        self.r = {}
        self.ep = -1


class Eng:
    def __init__(self, fw, name, eng):
        self.fw = fw
        self.name = name
        self.eng = eng
        self.sem = None
        self.count = 0
        self.known = {}
        self.nsem = 0
        self.n_inst = 0
        self.n_wait = 0

    def rotate(self):
        self.sem = self.fw.new_sem(f"{self.name}_s{self.nsem}")
        self.nsem += 1
        self.count = 0

    def wait(self, toks):
        for s, v in toks.items():
            if self.known.get(s, 0) < v:
                self.eng.wait_ge(s, v)
                self.known[s] = v
                self.n_wait += 1


class FW:
    def __init__(self, nc, n_dma_slots=24):
        self.nc = nc
        self.stack = ExitStack()
        self.sem_count = 0
        self.epoch = 0
        self.pe = Eng(self, "pe", nc.tensor)
        self.act = Eng(self, "act", nc.scalar)
        self.dve = Eng(self, "dve", nc.vector)
        self.pool = Eng(self, "pool", nc.gpsimd)
        self.sp = Eng(self, "sp", nc.sync)
        self.engs = [self.pe, self.act, self.dve, self.pool, self.sp]
        for e in self.engs:
            e.rotate()
        self.slots = [[self.new_sem(f"dma{i}"), 0] for i in range(n_dma_slots)]
        self.slot_i = 0
        self.n_dma = 0

    def new_sem(self, name):
        self.sem_count += 1
        return self.stack.enter_context(self.nc.semaphore(name))

    def _sync(self, d):
        if d.ep != self.epoch:
            d.w = {}
            d.r = {}
            d.ep = self.epoch

    def _collect(self, reads, writes):
        toks = {}
        for d in reads:
            self._sync(d)
            for s, v in d.w.items():
                if toks.get(s, 0) < v:
                    toks[s] = v
        for d in writes:
            self._sync(d)
            for s, v in d.w.items():
                if toks.get(s, 0) < v:
                    toks[s] = v
            for s, v in d.r.items():
                if toks.get(s, 0) < v:
                    toks[s] = v
        return toks

    def _record(self, tok, reads, writes):
        s, v = tok
        for d in reads:
            if d.r.get(s, 0) < v:
                d.r[s] = v
        for d in writes:
            d.w = {s: v}
            d.r = {}

    def op(self, E, fn, reads=(), writes=()):
        return self.group(E, [fn], reads, writes)

    def group(self, E, fns, reads=(), writes=()):
        toks = self._collect(reads, writes)
        if E is self.pe:
            toks.pop(E.sem, None)
        E.wait(toks)
        ins = None
        for fn in fns:
            ins = fn()
            E.n_inst += 1
        if E.count >= SEM_LIMIT:
            E.rotate()
        ins.then_inc(E.sem, 1)
        E.count += 1
        tok = (E.sem, E.count)
        self._record(tok, reads, writes)
        return tok

    def dma(self, Q, out, in_, reads=(), writes=()):
        toks = self._collect(reads, writes)
        slot = self.slots[self.slot_i]
        self.slot_i = (self.slot_i + 1) % len(self.slots)
        if slot[1] > 0:
            toks[slot[0]] = max(toks.get(slot[0], 0), slot[1])
        Q.wait(toks)
        if slot[1] + 16 > SEM_LIMIT:
            slot[0] = self.new_sem(f"dmar{self.sem_count}")
            slot[1] = 0
        self.nc_dma(Q, out, in_).then_inc(slot[0], 16)
        slot[1] += 16
        self.n_dma += 1
        tok = (slot[0], slot[1])
        self._record(tok, reads, writes)
        return tok

    def nc_dma(self, Q, out, in_):
        return Q.eng.dma_start(out=out, in_=in_)

    def barrier(self):
        toks = {}
        for E in self.engs:
            if E.count > 0:
                toks[E.sem] = E.count
        for s, v in self.slots:
            if v > 0:
                toks[s] = v
        for E in self.engs:
            E.wait(dict(toks))
        self.epoch += 1

    def close(self):
        self.stack.close()


class Ring:
    def __init__(self, items):
        self.items = items
        self.i = 0

    def next(self):
        it = self.items[self.i]
        self.i = (self.i + 1) % len(self.items)
        return it


class Builder:
    def __init__(self, layers=(0, 1, 2, 3), debug_out=None):
        self.layers = list(layers)
        self.nc = bass.Bass("TRN2", target_bir_lowering=False)
        Builder.last = self
        self.fw = FW(self.nc)
        self.gs = ExitStack()
        self.ps = None
        self.inputs = {}
        self.consts = {}

    def din(self, name, shape, dt=F32):
        t = self.nc.dram_tensor(name, list(shape), dt, kind="ExternalInput").ap()
        self.inputs[name] = t
        return t

    def dscr(self, name, shape, dt):
        return self.nc.dram_tensor(name, list(shape), dt, kind="Internal").ap()

    def _uname(self, name):
        self.ucount = getattr(self, "ucount", 0) + 1
        return f"s{self.ucount}_{name}"

    def gsb(self, name, shape, dt):
        return self.gs.enter_context(self.nc.sbuf_tensor(self._uname(name), list(shape), dt)), Dep()

    def sb(self, name, shape, dt):
        return self.ps.enter_context(self.nc.sbuf_tensor(self._uname(name), list(shape), dt)), Dep()

    def ring(self, name, n, shape, dt):
        return Ring([self.sb(f"{name}{i}", shape, dt) for i in range(n)])

    @contextmanager
    def phase(self, name):
        self.fw.barrier()
        with ExitStack() as ps:
            self.ps = ps
            yield
            self.fw.barrier()
        self.ps = None

    def mm(self, out, lhsT, rhs, start, stop):
        nc = self.nc
        return lambda: nc.tensor.matmul(out, lhsT, rhs, start=start, stop=stop)

    def build(self):
        nc, fw = self.nc, self.fw
        L = self.layers
        self.x_in = self.din("x", [2, SEQ, D])
        self.ctx_in = self.din("ctx", [2, CTX, D])
        self.cvecT = self.din("cvecT", [128, KD * 3])
        self.identF_d = self.din("identF", [128, 128])
        self.onesB_d = self.din("onesB", [128, 128], BF16)
        self.W = {}
        for l in L:
            p = f"l{l}_"
            self.W[p + "mod_w"] = self.din(p + "mod_w", [D, 6 * D])
            self.W[p + "mod_bT"] = self.din(p + "mod_bT", [128, 96])
            self.W[p + "n1T"] = self.din(p + "n1T", [128, KD])
            self.W[p + "n2T"] = self.din(p + "n2T", [128, KD])
            self.W[p + "ffn_w_in"] = self.din(p + "ffn_w_in", [D, 2 * FFN_H])
            self.W[p + "ffn_w_out"] = self.din(p + "ffn_w_out", [FFN_H, D])
        if 0 in L:
            p = "l0_"
            self.W[p + "w_in"] = self.din(p + "w_in", [D, 6176])
            self.W[p + "wgaf"] = self.din(p + "wgaf", [33, 1024])
            self.W[p + "wgab"] = self.din(p + "wgab", [33, 1024])
            self.W[p + "onT"] = self.din(p + "onT", [128, 4])
            self.W[p + "w_out"] = self.din(p + "w_out", [D, D])
            self.gla_c = self.din("gla_c", [128, 1024])
        if 1 in L:
            p = "l1_"
            self.W[p + "w_in"] = self.din(p + "w_in", [D, 3072])
            self.W[p + "qk"] = self.din(p + "qk", [128, 2])
            self.W[p + "w_out"] = self.din(p + "w_out", [D, D])
        if 2 in L:
            p = "l2_"
            self.W[p + "w_in"] = self.din(p + "w_in", [D, 6144])
            self.W[p + "qk"] = self.din(p + "qk", [128, 2])
            self.W[p + "lvec"] = self.din(p + "lvec", [128, 512])
            self.W[p + "onT"] = self.din(p + "onT", [128, 2])
            self.W[p + "w_out"] = self.din(p + "w_out", [D, D])
        if 1 in L or 2 in L:
            self.rope_d = self.din("rope", [128, 2 * SEQ])
            self.RT_d = self.din("RT", [128, 128], BF16)
        if 3 in L:
            p = "l3_"
            self.W[p + "w_out"] = self.din(p + "w_out", [D, D])
            self.dftc_d = self.din("dftc", [128, 2 * 4 * 512], BF16)
            self.dftl_d = self.din("dftl", [128, 2 * 16 * SEQ], BF16)
        self.y_out = nc.dram_tensor("y", [2, SEQ, D], F32, kind="ExternalOutput").ap()

        self.xT = self.dscr("xT", [4, KD, 128, TB], F32)
        self.OT = self.dscr("OT", [4, KD, 128, TB], BF16)
        self.RS = self.dscr("RS", [2, 4, 128, TB], F32)
        self.PA = self.dscr("PA", [2, 32, 128, NTOK], BF16)
        self.VT = self.dscr("VT", [2, NTOK, 3072], BF16)
        if 0 in L:
            self.PAF = self.dscr("PAF", [2, 16, 128, NTOK], F32)
            self.SR = self.dscr("SR", [2, 16, 128, NTOK], BF16)
            self.LT = self.dscr("LT", [2, NTOK, 2048], F32)
            self.OF = self.dscr("OF", [2, 16, 128, NTOK], F32)
        if 3 in L:
            self.AB = self.dscr("AB", [2, 4, 2, 16, 128, 512], BF16)

        self.banks = []
        for i in range(8):
            t = self.gs.enter_context(nc.psum_tensor(f"bank{i}", [128, 512], F32))
            self.banks.append((t, Dep()))
        self.bank_ring = Ring(self.banks)
        self.identF, self.d_ident = self.gsb("identF_s", [128, 128], F32)
        self.onesB, self.d_ones = self.gsb("onesB_s", [128, 128], BF16)
        self.epsc, self.d_epsc = self.gsb("epsc", [128, 2], F32)
        self.coef = {}
        for l in L:
            self.coef[l] = self.gsb(f"coef{l}", [128, 96, 3], F32)
        self.sTb, self.d_sTb = self.gsb("sTb", [128, KD, 3], BF16)
        self.side = None
        self.side_every = 4
        self.side_count = 0

        with nc.Block() as block:
            @block.sync
            def _(sync):
                self.emit_all()
        self.gs.close()
        fw.close()
        return nc

    def emit_all(self):
        nc, fw = self.nc, self.fw
        fw.dma(fw.sp, self.identF[:], self.identF_d, writes=[self.d_ident])
        fw.dma(fw.sp, self.onesB[:], self.onesB_d, writes=[self.d_ones])
        fw.op(fw.dve, lambda: nc.vector.memset(self.epsc[:, 0:1], EPS), writes=[self.d_epsc])
        fw.op(fw.dve, lambda: nc.vector.memset(self.epsc[:, 1:2], 1.0), writes=[self.d_epsc])
        self.phase_mod()
        self.phase_in()
        for l in self.layers:
            kind = l % 4
            ctx_out = l < 2
            if kind == 0:
                self.gla_A(l)
                self.gla_B(l)
            elif kind == 1:
                self.attn_A(l, diff=False)
                self.gqa_B(l)
            elif kind == 2:
                self.attn_A(l, diff=True)
                self.diff_B(l)
            else:
                self.fnet_A(l)
                self.fnet_B(l)
            self.phase_C1(l, ctx_out)
            self.phase_C2(l, ctx_out)
        self.phase_out()

    def phase_mod(self):
        nc, fw = self.nc, self.fw
        with self.phase("mod"):
            cT, d_cT = self.sb("cT", [128, KD, 3], F32)
            fw.dma(fw.sp, cT[:].rearrange("p k r -> p (k r)"), self.cvecT, writes=[d_cT])
            fw.op(fw.act, lambda: nc.scalar.activation(self.sTb[:], cT[:], AF.Silu), reads=[d_cT], writes=[self.d_sTb])
            for _ in self.mod_gen(self.layers[0]):
                pass

    def mod_gen(self, l):
        nc, fw = self.nc, self.fw
        p = f"l{l}_"
        mw = self.W[p + "mod_w"].rearrange("(k p) m -> p k m", p=128)
        cf, d_cf = self.coef[l]
        wring = self.ring("modw", 2, [128, KD, 512], BF16)
        row_r = self.ring("modrow", 2, [3, 512], F32)
        mbT, d_mb = self.sb("mbT", [128, 96], F32)
        nT, d_nT = self.sb("nT", [128, 2, KD], F32)
        fw.dma(fw.sp, mbT[:], self.W[p + "mod_bT"], writes=[d_mb])
        fw.dma(fw.sp, nT[:, 0, :], self.W[p + "n1T"], writes=[d_nT])
        fw.dma(fw.sp, nT[:, 1, :], self.W[p + "n2T"], writes=[d_nT])
        tiles = {}

        def load(j):
            if j < 24 and j not in tiles:
                tiles[j] = wring.next()
                fw.dma(fw.pool, tiles[j][0][:], mw[:, :, j * 512:(j + 1) * 512], writes=[tiles[j][1]])

        d_parts = [Dep() for _ in range(24)]
        for j in range(24):
            load(j)
            load(j + 1)
            wt, wd = tiles.pop(j)
            bank, bd = self.bank_ring.next()
            fw.group(fw.pe, [self.mm(bank[0:3, :], self.sTb[:, k, :], wt[:, k, :], k == 0, k == KD - 1)
                             for k in range(KD)], reads=[wd, self.d_sTb], writes=[bd])
            row, d_row = row_r.next()
            fw.op(fw.act, lambda: nc.scalar.copy(row[:], bank[0:3, :]), reads=[bd], writes=[d_row])
            yield
            bank2, bd2 = self.bank_ring.next()
            fw.group(fw.pe, [
                (lambda q=q: nc.tensor.transpose(bank2[:, q * 3:(q + 1) * 3], row[0:3, q * 128:(q + 1) * 128],
                                                 self.identF[0:3, 0:3]))
                for q in range(4)], reads=[d_row, self.d_ident], writes=[bd2])
            fw.op(fw.dve, lambda: nc.vector.tensor_tensor(
                cf[:, 4 * j:4 * j + 4, :], bank2[:, 0:12].rearrange("p (c r) -> p c r", r=3),
                mbT[:, 4 * j:4 * j + 4].unsqueeze(2).to_broadcast([128, 4, 3]), ALU.add),
                reads=[bd2, d_mb], writes=[d_parts[j]])
            yield
        for w, c0 in ((0, 16), (1, 64)):
            fw.op(fw.dve, lambda w=w, c0=c0: nc.vector.scalar_tensor_tensor(
                cf[:, c0:c0 + 16, :], cf[:, c0:c0 + 16, :], 1.0,
                nT[:, w, :].unsqueeze(2).to_broadcast([128, KD, 3]), ALU.add, ALU.mult),
                reads=[d_nT] + d_parts, writes=[d_cf])

    def host_side(self, l):
        nxt = [x for x in self.layers if x > l]
        if nxt:
            self.side = self.mod_gen(nxt[0])
            self.side_count = 0

    def side_tick(self):
        if self.side is not None:
            self.side_count += 1
            if self.side_count % self.side_every == 0:
                try:
                    next(self.side)
                except StopIteration:
                    self.side = None

    def side_drain(self):
        if self.side is not None:
            for _ in self.side:
                pass
            self.side = None

    def block_tiles(self, blk):
        b, h = divmod(blk, 2)
        tiles = [(self.ctx_in[b, h * 128:(h + 1) * 128, :], self.y_out[b, 0:128, :], 0, True)]
        for t in range(8):
            r0 = h * 1024 + t * 128
            tiles.append((self.x_in[b, r0:r0 + 128, :], self.y_out[b, r0:r0 + 128, :], 128 + t * 128, False))
        return tiles

    def phase_in(self):
        nc, fw = self.nc, self.fw
        with self.phase("in"):
            xin = self.ring("xin", 3, [128, D], F32)
            xst = self.ring("xst", 3, [128, KD, 128], F32)
            sqt = self.ring("sqt", 3, [128, KD, 128], BF16)
            rs, d_rs = self.sb("rs_in", [128, TB], F32)
            ring5 = Ring(self.banks[0:5])
            sbank = Ring(self.banks[5:8])
            for blk in range(4):
                xv = self.xT[blk].rearrange("k p t -> p k t")
                pending = []
                for (src, _, c0, is_ctx) in self.block_tiles(blk):
                    xi, d_xi = xin.next()
                    fw.dma(fw.sp, xi[:], src, writes=[d_xi])
                    xs, d_xs = xst.next()
                    for g in range(4):
                        bank, bd = ring5.next()
                        fw.group(fw.pe, [
                            (lambda q=q: nc.tensor.transpose(bank[:, q * 128:(q + 1) * 128],
                                                             xi[:, (g * 4 + q) * 128:(g * 4 + q + 1) * 128],
                                                             self.identF[:]))
                            for q in range(4)], reads=[d_xi, self.d_ident], writes=[bd])
                        dst = xs[:, g * 4:(g + 1) * 4, :]
                        srcb = bank[:].rearrange("p (q t) -> p q t", q=4)
                        if g % 2 == 0:
                            fw.op(fw.act, lambda: nc.scalar.copy(dst, srcb), reads=[bd], writes=[d_xs])
                        else:
                            fw.op(fw.dve, lambda: nc.vector.tensor_copy(dst, srcb), reads=[bd], writes=[d_xs])
                    while pending:
                        pending.pop(0)()
                    fw.dma(fw.sp, xv[:, :, c0:c0 + 128], xs[:], reads=[d_xs], writes=[Dep()])
                    sq, d_sq = sqt.next()
                    fw.op(fw.act, lambda: nc.scalar.activation(sq[:], xs[:], AF.Square), reads=[d_xs], writes=[d_sq])

                    def stat(sq=sq, d_sq=d_sq, c0=c0):
                        sb_, d_sb = sbank.next()
                        fw.group(fw.pe, [self.mm(sb_[:, 0:128], self.onesB[:], sq[:, k, :], k == 0, k == KD - 1)
                                         for k in range(KD)], reads=[d_sq, self.d_ones], writes=[d_sb])
                        fw.op(fw.act, lambda: nc.scalar.activation(rs[:, c0:c0 + 128], sb_[:, 0:128], AF.Ln,
                                                                   bias=self.epsc[:, 0:1], scale=1.0 / D),
                              reads=[d_sb, self.d_epsc], writes=[d_rs])
                    pending.append(stat)
                while pending:
                    pending.pop(0)()
                fw.op(fw.act, lambda: nc.scalar.activation(rs[:], rs[:], AF.Exp, scale=-0.5), reads=[d_rs], writes=[d_rs])
                fw.dma(fw.sp, self.RS[0, blk], rs[:], reads=[d_rs], writes=[Dep()])

    def phase_out(self):
        nc, fw = self.nc, self.fw
        with self.phase("out"):
            xin = self.ring("xo_in", 2, [128, KD, 128], F32)
            xst = self.ring("xo_st", 2, [128, D], F32)
            outs = []
            for blk in range(4):
                xv = self.xT[blk].rearrange("k p t -> p k t")
                for (_, dst_d, c0, is_ctx) in self.block_tiles(blk):
                    if is_ctx:
                        continue
                    xi, d_xi = xin.next()
                    fw.dma(fw.sp, xi[:], xv[:, :, c0:c0 + 128], writes=[d_xi])
                    xs, d_xs = xst.next()
                    for g in range(4):
                        bank, bd = self.bank_ring.next()
                        fw.group(fw.pe, [
                            (lambda q=q: nc.tensor.transpose(bank[:, q * 128:(q + 1) * 128],
                                                             xi[:, g * 4 + q, :], self.identF[:]))
                            for q in range(4)], reads=[d_xi, self.d_ident], writes=[bd])
                        dst = xs[:, g * 512:(g + 1) * 512]
                        if g % 2 == 0:
                            fw.op(fw.act, lambda: nc.scalar.copy(dst, bank[:]), reads=[bd], writes=[d_xs])
                        else:
                            fw.op(fw.dve, lambda: nc.vector.tensor_copy(dst, bank[:]), reads=[bd], writes=[d_xs])
                    dd = Dep()
                    fw.dma(fw.sp, dst_d, xs[:], reads=[d_xs], writes=[dd])
                    outs.append(dd)

    @staticmethod
    def subs(ctx):
        return ([(0, 128)] if ctx else []) + [(128, 640), (640, TB)]

    def regions(self, blk, ctx):
        b = blk // 2
        return ([(0, 128, 2)] if ctx else []) + [(128, TB, b)]

    def alloc_norm(self, n_xc=3):
        self.xc_ring = self.ring("xc", n_xc, [128, TB], F32)
        self.tmp_ring = self.ring("tmpf", 2, [128, TB], F32)
        self.rstd, self.d_rstd = self.sb("rstd", [128, TB], F32)

    def norm_mod(self, l, blk, which, ctx, hT, d_hT):
        nc, fw = self.nc, self.fw
        cf, d_cf = self.coef[l]
        cB, cA = (0, 16) if which == 0 else (48, 64)
        subs = self.subs(ctx)
        a0 = subs[0][0]
        fw.dma(fw.sp, self.rstd[:, a0:TB], self.RS[which, blk, :, a0:TB], writes=[self.d_rstd])
        for k in range(KD):
            xc, d_xc = self.xc_ring.next()
            fw.dma(fw.sp, xc[:, a0:TB], self.xT[blk, k, :, a0:TB], writes=[d_xc])
            tm, d_tm = self.tmp_ring.next()
            fw.op(fw.dve, lambda: nc.vector.tensor_tensor(tm[:, a0:TB], xc[:, a0:TB], self.rstd[:, a0:TB], ALU.mult),
                  reads=[d_xc, self.d_rstd], writes=[d_tm])
            for (r0, r1, r) in self.regions(blk, ctx):
                fw.op(fw.act, lambda: nc.scalar.activation(hT[:, k, r0:r1], tm[:, r0:r1], AF.Identity,
                                                           bias=cf[:, cB + k, r:r + 1], scale=cf[:, cA + k, r:r + 1]),
                      reads=[d_tm, d_cf], writes=[d_hT])

    def stats_begin(self, subs):
        return {"subs": subs, "banks": [self.banks[5 + i] for i in range(len(subs))], "pending": []}

    def stats_push(self, st, src, d_src, k, si, sq, d_sq):
        nc, fw = self.nc, self.fw
        s0, s1 = st["subs"][si]
        fw.op(fw.act, lambda: nc.scalar.activation(sq[:, s0:s1], src[:, s0:s1], AF.Square), reads=[d_src], writes=[d_sq])
        bk, bd = st["banks"][si]
        st["pending"].append(lambda: fw.group(
            fw.pe, [self.mm(bk[:, 0:s1 - s0], self.onesB[:], sq[:, s0:s1], k == 0, k == KD - 1)],
            reads=[d_sq, self.d_ones], writes=[bd]))

    def stats_flush(self, st):
        while st["pending"]:
            st["pending"].pop(0)()

    def stats_end(self, st, which, blk, rs, d_rs):
        nc, fw = self.nc, self.fw
        self.stats_flush(st)
        a0 = st["subs"][0][0]
        for (bk, bd), (s0, s1) in zip(st["banks"], st["subs"]):
            fw.op(fw.act, lambda: nc.scalar.activation(rs[:, s0:s1], bk[:, 0:s1 - s0], AF.Ln,
                                                       bias=self.epsc[:, 0:1], scale=1.0 / D),
                  reads=[bd, self.d_epsc], writes=[d_rs])
        fw.op(fw.act, lambda: nc.scalar.activation(rs[:, a0:TB], rs[:, a0:TB], AF.Exp, scale=-0.5),
              reads=[d_rs], writes=[d_rs])
        fw.dma(fw.sp, self.RS[which, blk, :, a0:TB], rs[:, a0:TB], reads=[d_rs], writes=[Dep()])

    def pipeline(self, gens):
        def step(g):
            try:
                next(g)
                return True
            except StopIteration:
                return False
        active = []
        for g in gens:
            alive = step(g)
            active = [a for a in active if step(a)]
            if alive:
                active.append(g)
        while active:
            active = [a for a in active if step(a)]

    def linear(self, hT, d_hT, kd, wview, cols, subs, epilogue, wring, mw=128):
        nc, fw = self.nc, self.fw
        wts = {}
        PF = 2

        def load(fi):
            if fi < len(cols) and fi not in wts:
                wts[fi] = wring.next()
                fw.dma(fw.pool, wts[fi][0][:, 0:kd, 0:mw], wview[:, :, cols[fi]:cols[fi] + mw], writes=[wts[fi][1]])

        def item(fi, c0, si, s0, s1):
            if si == 0:
                for j in range(fi, fi + PF + 1):
                    load(j)
            wt, wd = wts[fi]
            bank, bd = self.bank_ring.next()
            fw.group(fw.pe, [self.mm(bank[0:mw, 0:s1 - s0], wt[:, k, 0:mw], hT[:, k, s0:s1], k == 0, k == kd - 1)
                             for k in range(kd)], reads=[wd, d_hT], writes=[bd])
            self.side_tick()
            r = epilogue(fi, si, (s0, s1), bank, bd)
            if r is not None:
                yield from r

        self.pipeline(item(fi, c0, si, s0, s1) for fi, c0 in enumerate(cols) for si, (s0, s1) in enumerate(subs))

    def linear_tok(self, hT, d_hT, wview, col_groups, tok_tiles, epilogue, wtring):
        nc, fw = self.nc, self.fw
        for gi, c0 in enumerate(col_groups):
            wt, wd = wtring.next()
            fw.dma(fw.pool, wt[:], wview[:, :, c0:c0 + 512], writes=[wd])
            for ti, t0 in enumerate(tok_tiles):
                bank, bd = self.bank_ring.next()
                fw.group(fw.pe, [self.mm(bank[:, :], hT[:, k, t0:t0 + 128], wt[:, k, :], k == 0, k == KD - 1)
                                 for k in range(KD)], reads=[wd, d_hT], writes=[bd])
                self.side_tick()
                epilogue(gi, ti, t0, bank, bd)

    @staticmethod
    def nat(blk, col):
        h = blk % 2
        if col < 128:
            return h * 128 + col
        return CTX + h * 1024 + (col - 128)

    def phase_C1(self, l, ctx):
        nc, fw = self.nc, self.fw
        cf, d_cf = self.coef[l]
        wv = self.W[f"l{l}_w_out"].rearrange("(k p) m -> p k m", p=128)
        with self.phase("C1"):
            oT_ring = self.ring("oT", 2, [128, KD, TB], BF16)
            wring = self.ring("wC1", 4, [128, KD, 128], BF16)
            xc_ring = self.ring("xc1", 3, [128, TB], F32)
            xo_ring = self.ring("xo1", 3, [128, TB], F32)
            sq_ring = self.ring("sq1", 3, [128, TB], BF16)
            rs, d_rs = self.sb("rs1", [128, TB], F32)
            subs = self.subs(ctx)
            a0 = subs[0][0]
            saved_ring = self.bank_ring
            self.bank_ring = Ring(self.banks[0:5])
            oTs = {}

            for blk in range(4):
                b = blk // 2
                if blk not in oTs:
                    oTs[blk] = oT_ring.next()
                    fw.dma(fw.sp, oTs[blk][0][:, :, a0:TB], self.OT[blk].rearrange("k p t -> p k t")[:, :, a0:TB],
                           writes=[oTs[blk][1]])
                if blk + 1 < 4:
                    oTs[blk + 1] = oT_ring.next()
                    fw.dma(fw.sp, oTs[blk + 1][0][:, :, a0:TB], self.OT[blk + 1].rearrange("k p t -> p k t")[:, :, a0:TB],
                           writes=[oTs[blk + 1][1]])
                oT, d_oT = oTs[blk]
                state = {}
                st = self.stats_begin(subs)

                def epi(fi, si, rng, bank, bd):
                    s0, s1 = rng
                    if si == 1:
                        self.stats_flush(st)
                    if si == 0:
                        state["xc"] = xc_ring.next()
                        state["xo"] = xo_ring.next()
                        state["sq"] = sq_ring.next()
                        fw.dma(fw.sp, state["xc"][0][:, a0:TB], self.xT[blk, fi, :, a0:TB], writes=[state["xc"][1]])
                    xc, d_xc = state["xc"]
                    xo, d_xo = state["xo"]
                    r = 2 if s1 <= 128 else b
                    fw.op(fw.dve, lambda: nc.vector.scalar_tensor_tensor(
                        xo[:, s0:s1], bank[:, 0:s1 - s0], cf[:, 32 + fi, r:r + 1], xc[:, s0:s1], ALU.mult, ALU.add),
                        reads=[bd, d_xc, d_cf], writes=[d_xo])
                    self.stats_push(st, xo, d_xo, fi, si, state["sq"][0], state["sq"][1])
                    if si == len(subs) - 1:
                        fw.dma(fw.sp, self.xT[blk, fi, :, a0:TB], xo[:, a0:TB], reads=[d_xo], writes=[Dep()])

                self.linear(oT, d_oT, KD, wv, [c * 128 for c in range(KD)], subs, epi, wring)
                self.stats_end(st, 1, blk, rs, d_rs)
            self.bank_ring = saved_ring

    def phase_C2(self, l, ctx):
        nc, fw = self.nc, self.fw
        cf, d_cf = self.coef[l]
        w_in = self.W[f"l{l}_ffn_w_in"].rearrange("(k p) m -> p k m", p=128)
        w_out = self.W[f"l{l}_ffn_w_out"].rearrange("(f p) m -> p f m", p=128)
        with self.phase("C2"):
            self.alloc_norm(n_xc=2)
            sq_ring = self.ring("sq2", 2, [128, TB], BF16)
            saved_ring = self.bank_ring
            self.bank_ring = Ring(self.banks[0:5])
            make_stats = l != self.layers[-1]
            hT, d_hT = self.sb("hT", [128, KD, TB], BF16)
            act, _ = self.sb("act", [128, NF, TB], BF16)
            wring = self.ring("wffn", 4, [128, KD, 128], BF16)
            woring = self.ring("wffo", 3, [128, NF // 2, 128], BF16)
            sg_ring = self.ring("sg", 2, [128, 512], F32)
            subs = self.subs(ctx)
            a0 = subs[0][0]
            self.norm_mod(l, 0, 1, ctx, hT, d_hT)
            for blk in range(4):
                b = blk // 2
                d_act = [Dep() for _ in range(NF)]
                for f in range(NF):
                    wg, d_wg = wring.next()
                    wu, d_wu = wring.next()
                    fw.dma(fw.pool, wg[:], w_in[:, :, f * 128:(f + 1) * 128], writes=[d_wg])
                    fw.dma(fw.pool, wu[:], w_in[:, :, FFN_H + f * 128:FFN_H + (f + 1) * 128], writes=[d_wu])
                    for (s0, s1) in subs:
                        n = s1 - s0
                        bg, d_bg = self.bank_ring.next()
                        bu, d_bu = self.bank_ring.next()
                        fw.group(fw.pe, [self.mm(bg[:, 0:n], wg[:, k, :], hT[:, k, s0:s1], k == 0, k == KD - 1)
                                         for k in range(KD)], reads=[d_wg, d_hT], writes=[d_bg])
                        fw.group(fw.pe, [self.mm(bu[:, 0:n], wu[:, k, :], hT[:, k, s0:s1], k == 0, k == KD - 1)
                                         for k in range(KD)], reads=[d_wu, d_hT], writes=[d_bu])
                        sg, d_sg = sg_ring.next()
                        fw.op(fw.act, lambda: nc.scalar.activation(sg[:, 0:n], bg[:, 0:n], AF.Silu),
                              reads=[d_bg], writes=[d_sg])
                        fw.op(fw.dve, lambda: nc.vector.tensor_tensor(act[:, f, s0:s1], sg[:, 0:n], bu[:, 0:n], ALU.mult),
                              reads=[d_sg, d_bu], writes=[d_act[f]])
                if blk < 3:
                    self.norm_mod(l, blk + 1, 1, ctx, hT, d_hT)
                H = NF // 2
                st = self.stats_begin(subs)
                for dch in range(KD):
                    wh = []
                    for hf in range(2):
                        wt, d_wt = woring.next()
                        fw.dma(fw.pool, wt[:], w_out[:, hf * H:(hf + 1) * H, dch * 128:(dch + 1) * 128], writes=[d_wt])
                        wh.append((wt, d_wt))
                    xc, d_xc = self.xc_ring.next()
                    fw.dma(fw.sp, xc[:, a0:TB], self.xT[blk, dch, :, a0:TB], writes=[d_xc])
                    xo, d_xo = self.tmp_ring.next()
                    obanks = [self.bank_ring.next() for _ in subs]
                    for hf in range(2):
                        wt, d_wt = wh[hf]
                        for (bank, bd), (s0, s1) in zip(obanks, subs):
                            n = s1 - s0
                            fw.group(fw.pe, [self.mm(bank[:, 0:n], wt[:, f, :], act[:, hf * H + f, s0:s1],
                                                     hf == 0 and f == 0, hf == 1 and f == H - 1) for f in range(H)],
                                     reads=[d_wt] + d_act[hf * H:(hf + 1) * H], writes=[bd])
                    self.stats_flush(st)
                    sq, d_sq = sq_ring.next()
                    for si, ((bank, bd), (s0, s1)) in enumerate(zip(obanks, subs)):
                        n = s1 - s0
                        r = 2 if s1 <= 128 else b
                        fw.op(fw.dve, lambda: nc.vector.scalar_tensor_tensor(
                            xo[:, s0:s1], bank[:, 0:n], cf[:, 80 + dch, r:r + 1], xc[:, s0:s1], ALU.mult, ALU.add),
                            reads=[bd, d_xc, d_cf], writes=[d_xo])
                        if make_stats:
                            self.stats_push(st, xo, d_xo, dch, si, sq, d_sq)
                    fw.dma(fw.sp, self.xT[blk, dch, :, a0:TB], xo[:, a0:TB], reads=[d_xo], writes=[Dep()])
                if make_stats:
                    rs, d_rs = self.tmp_ring.next()
                    self.stats_end(st, 0, blk, rs, d_rs)
            self.bank_ring = saved_ring

    def fnet_A(self, l):
        nc, fw = self.nc, self.fw
        with self.phase("fnetA"):
            self.alloc_norm()
            hT, d_hT = self.sb("hT", [128, KD, TB], BF16)
            dc, d_dc = self.sb("dftc", [128, 2, 4, 512], BF16)
            st_ring = self.ring("abst", 4, [128, 512], BF16)
            fw.dma(fw.sp, dc[:].rearrange("p a k m -> p (a k m)"), self.dftc_d, writes=[d_dc])
            for blk in range(4):
                b, h = divmod(blk, 2)
                self.norm_mod(l, blk, 0, False, hT, d_hT)
                for t in range(8):
                    c0 = 128 + t * 128
                    tt = h * 8 + t
                    for g in range(4):
                        for a in range(2):
                            bank, bd = self.bank_ring.next()
                            fw.group(fw.pe, [self.mm(bank[:, :], hT[:, 4 * g + kc, c0:c0 + 128], dc[:, a, kc, :],
                                                     kc == 0, kc == 3) for kc in range(4)],
                                     reads=[d_hT, d_dc], writes=[bd])
                            st, d_st = st_ring.next()
                            if a == 0:
                                fw.op(fw.act, lambda: nc.scalar.copy(st[:], bank[:]), reads=[bd], writes=[d_st])
                            else:
                                fw.op(fw.dve, lambda: nc.vector.tensor_scalar(st[:], bank[:], -1.0, None, ALU.mult),
                                      reads=[bd], writes=[d_st])
                            fw.dma(fw.sp, self.AB[b, g, a, tt], st[:], reads=[d_st], writes=[Dep()])

    def fnet_B(self, l):
        nc, fw = self.nc, self.fw
        with self.phase("fnetB"):
            dl, d_dl = self.sb("dftl", [128, 2, 16, SEQ], BF16)
            ab_ring = self.ring("ab", 2, [128, 2, 16, 512], BF16)
            st_ring = self.ring("yst", 3, [128, 512], BF16)
            for a in range(2):
                for q in range(4):
                    fw.dma(fw.sp, dl[:, a, q * 4:(q + 1) * 4, :].rearrange("p k m -> p (k m)"),
                           self.dftl_d[:, (a * 16 + q * 4) * SEQ:(a * 16 + q * 4 + 4) * SEQ], writes=[d_dl])
            for b in range(2):
                for g in range(4):
                    ab, d_ab = ab_ring.next()
                    for a in range(2):
                        fw.dma(fw.sp, ab[:, a], self.AB[b, g, a].rearrange("t p c -> p t c"), writes=[d_ab])
                    for cc in range(4):
                        for tb in range(4):
                            bank, bd = self.bank_ring.next()
                            fns = []
                            for a in range(2):
                                for tt in range(16):
                                    fns.append(self.mm(bank[:, :], ab[:, a, tt, cc * 128:(cc + 1) * 128],
                                                       dl[:, a, tt, tb * 512:(tb + 1) * 512],
                                                       a == 0 and tt == 0, a == 1 and tt == 15))
                            fw.group(fw.pe, fns, reads=[d_ab, d_dl], writes=[bd])
                            st, d_st = st_ring.next()
                            if (cc + tb) % 2 == 0:
                                fw.op(fw.act, lambda: nc.scalar.copy(st[:], bank[:]), reads=[bd], writes=[d_st])
                            else:
                                fw.op(fw.dve, lambda: nc.vector.tensor_copy(st[:], bank[:]), reads=[bd], writes=[d_st])
                            blk = b * 2 + tb // 2
                            col = 128 + (tb % 2) * 512
                            fw.dma(fw.sp, self.OT[blk, 4 * g + cc, :, col:col + 512], st[:], reads=[d_st], writes=[Dep()])

    def attn_A(self, l, diff):
        nc, fw = self.nc, self.fw
        p = f"l{l}_"
        wv = self.W[p + "w_in"].rearrange("(k p) m -> p k m", p=128)
        nq = 16
        nk = 16 if diff else 4
        v0 = (nq + nk) * 128
        nvg = 4 if diff else 1
        with self.phase("attnA"):
            self.alloc_norm()
            hT, d_hT = self.sb("hT", [128, KD, TB], BF16)
            rope, d_rope = self.sb("rope", [128, 2, SEQ], F32)
            RT, d_RT = self.sb("RT", [128, 128], BF16)
            qk, d_qk = self.sb("qk", [128, 2], F32)
            wring = self.ring("wA", 4, [128, KD, 128], BF16)
            wtring = self.ring("wtA", 2, [128, KD, 512], BF16)
            sq_r = self.ring("sqA", 3, [128, 512], BF16)
            raw_r = self.ring("rawA", 2, [128, 512], F32)
            t_r = self.ring("tA", 2, [128, 512], F32)
            qn_r = self.ring("qnA", 3, [128, 512], BF16)
            t1_r = self.ring("t1A", 2, [128, 512], F32)
            t2_r = self.ring("t2A", 2, [128, 512], F32)
            qf_r = self.ring("qfA", 3, [128, 512], BF16)
            vst_r = self.ring("vstA", 3, [128, 512], BF16)
            fw.dma(fw.sp, rope[:].rearrange("p a t -> p (a t)"), self.rope_d, writes=[d_rope])
            fw.dma(fw.sp, RT[:], self.RT_d, writes=[d_RT])
            fw.dma(fw.sp, qk[:], self.W[p + "qk"], writes=[d_qk])
            self.host_side(l)
            for blk in range(4):
                b, h = divmod(blk, 2)
                self.norm_mod(l, blk, 0, True, hT, d_hT)

                def epi(fi, si, rng, bank, bd):
                    s0, s1 = rng
                    n = s1 - s0
                    is_q = fi < nq
                    is_ctx = s1 <= 128
                    if diff and is_q and is_ctx:
                        return
                    g = qk[:, 0:1] if is_q else qk[:, 1:2]
                    sq, d_sq = sq_r.next()
                    fw.op(fw.act, lambda: nc.scalar.activation(sq[:, 0:n], bank[:, 0:n], AF.Square), reads=[bd], writes=[d_sq])
                    yield
                    ssb, d_ssb = self.bank_ring.next()
                    fw.group(fw.pe, [self.mm(ssb[:, 0:n], self.onesB[:], sq[:, 0:n], True, True)],
                             reads=[d_sq, self.d_ones], writes=[d_ssb])
                    t, d_t = t_r.next()
                    fw.op(fw.act, lambda: nc.scalar.activation(t[:, 0:n], ssb[:, 0:n], AF.Ln, bias=self.epsc[:, 0:1],
                                                               scale=1.0 / 128), reads=[d_ssb, self.d_epsc], writes=[d_t])
                    fw.op(fw.act, lambda: nc.scalar.activation(t[:, 0:n], t[:, 0:n], AF.Exp, scale=-0.5), reads=[d_t], writes=[d_t])
                    nat0 = self.nat(blk, s0)
                    if is_ctx:
                        qf, d_qf = qf_r.next()
                        fw.op(fw.dve, lambda: nc.vector.scalar_tensor_tensor(qf[:, 0:n], bank[:, 0:n], g, t[:, 0:n],
                                                                             ALU.mult, ALU.mult),
                              reads=[bd, d_t, d_qk], writes=[d_qf])
                        fw.dma(fw.sp, self.PA[b, fi, :, nat0:nat0 + n], qf[:, 0:n], reads=[d_qf], writes=[Dep()])
                        return
                    qn, d_qn = qn_r.next()
                    fw.op(fw.dve, lambda: nc.vector.scalar_tensor_tensor(qn[:, 0:n], bank[:, 0:n], g, t[:, 0:n],
                                                                         ALU.mult, ALU.mult),
                          reads=[bd, d_t, d_qk], writes=[d_qn])
                    yield
                    rb, d_rb = self.bank_ring.next()
                    fw.group(fw.pe, [self.mm(rb[:, 0:n], RT[:], qn[:, 0:n], True, True)], reads=[d_qn, d_RT], writes=[d_rb])
                    lt0 = nat0 - CTX
                    t1, d_t1 = t1_r.next()
                    fw.op(fw.pool, lambda: nc.gpsimd.tensor_tensor(t1[:, 0:n], qn[:, 0:n], rope[:, 0, lt0:lt0 + n], ALU.mult),
                          reads=[d_qn, d_rope], writes=[d_t1])
                    t2, d_t2 = t2_r.next()
                    fw.op(fw.dve, lambda: nc.vector.tensor_tensor(t2[:, 0:n], rb[:, 0:n], rope[:, 1, lt0:lt0 + n], ALU.mult),
                          reads=[d_rb, d_rope], writes=[d_t2])
                    qf, d_qf = qf_r.next()
                    fw.op(fw.pool, lambda: nc.gpsimd.tensor_tensor(qf[:, 0:n], t1[:, 0:n], t2[:, 0:n], ALU.add),
                          reads=[d_t1, d_t2], writes=[d_qf])
                    fw.dma(fw.sp, self.PA[b, fi, :, nat0:nat0 + n], qf[:, 0:n], reads=[d_qf], writes=[Dep()])

                self.linear(hT, d_hT, KD, wv, [c * 128 for c in range(nq + nk)], self.subs(True), epi, wring)

                def epi_v(gi, ti, t0, bank, bd):
                    vs, d_vs = vst_r.next()
                    if ti % 2 == 0:
                        fw.op(fw.act, lambda: nc.scalar.copy(vs[:], bank[:]), reads=[bd], writes=[d_vs])
                    else:
                        fw.op(fw.dve, lambda: nc.vector.tensor_copy(vs[:], bank[:]), reads=[bd], writes=[d_vs])
                    n0 = self.nat(blk, t0)
                    fw.dma(fw.sp, self.VT[b, n0:n0 + 128, gi * 512:(gi + 1) * 512], vs[:], reads=[d_vs], writes=[Dep()])

                self.linear_tok(hT, d_hT, wv, [v0 + g * 512 for g in range(nvg)], [t * 128 for t in range(9)], epi_v, wtring)
            self.side_drain()

    def qblocks(self, b, n_lat, with_ctx):
        out = []
        if with_ctx:
            out.append((0, 256, [0, 1], [(b * 2, 0, 0, 128), (b * 2 + 1, 0, 128, 128)]))
        for j in range(SEQ // n_lat):
            lt = j * n_lat
            out.append((CTX + lt, n_lat, list(range(18)), [(b * 2 + lt // 1024, 128 + lt % 1024, 0, n_lat)]))
        return out

    def gqa_B(self, l):
        nc, fw = self.nc, self.fw
        scale = 128 ** -0.5
        with self.phase("gqaB"):
            k_r = self.ring("kB", 2, [128, NTOK], BF16)
            v_r = self.ring("vB", 2, [128, 18, 128], BF16)
            q_r = self.ring("qB", 3, [128, 512], BF16)
            e_r = self.ring("eB", 4, [128, 512], BF16)
            rd_r = self.ring("rdB", 2, [128, 512], F32)
            o_r = self.ring("oB", 2, [128, 512], BF16)
            st_r = Ring([self.banks[0], self.banks[1], self.banks[2], self.banks[7]])
            o_b = Ring([self.banks[3], self.banks[5]])
            d_b = Ring([self.banks[4], self.banks[6]])

            def qblock_items(b, kT, d_kT, Vg, d_V, hq, q0, n, kts, dsts):
                cx = {}
                last = len(kts) - 1

                def item(i):
                    kt = kts[i]
                    if i == 0:
                        cx["q"] = q_r.next()
                        fw.dma(fw.sp, cx["q"][0][:, 0:n], self.PA[b, hq, :, q0:q0 + n], writes=[cx["q"][1]])
                        cx["O"] = o_b.next()
                        cx["D"] = d_b.next()
                    q, d_q = cx["q"]
                    O, d_O = cx["O"]
                    Dn, d_D = cx["D"]
                    bk, bd = st_r.next()
                    fw.group(fw.pe, [self.mm(bk[:, 0:n], kT[:, kt * 128:(kt + 1) * 128], q[:, 0:n], True, True)],
                             reads=[d_kT, d_q], writes=[bd])
                    yield
                    E, d_E = e_r.next()
                    fw.op(fw.act, lambda: nc.scalar.activation(E[:, 0:n], bk[:, 0:n], AF.Exp, scale=scale),
                          reads=[bd], writes=[d_E])
                    yield
                    fw.group(fw.pe, [self.mm(O[:, 0:n], Vg[:, kt, :], E[:, 0:n], i == 0, i == last),
                                     self.mm(Dn[:, 0:n], self.onesB[:], E[:, 0:n], i == 0, i == last)],
                             reads=[d_E, d_V, self.d_ones], writes=[d_O, d_D])
                    if i == last:
                        rd, d_rd = rd_r.next()
                        fw.op(fw.dve, lambda: nc.vector.reciprocal(rd[:, 0:n], Dn[:, 0:n]), reads=[d_D], writes=[d_rd])
                        o, d_o = o_r.next()
                        fw.op(fw.dve, lambda: nc.vector.tensor_tensor(o[:, 0:n], O[:, 0:n], rd[:, 0:n], ALU.mult),
                              reads=[d_O, d_rd], writes=[d_o])
                        for (blk, col, off, ln) in dsts:
                            fw.dma(fw.sp, self.OT[blk, hq, :, col:col + ln], o[:, off:off + ln], reads=[d_o], writes=[Dep()])

                return [item(i) for i in range(len(kts))]

            def gens():
                for b in range(2):
                    for g in range(4):
                        kT, d_kT = k_r.next()
                        fw.dma(fw.sp, kT[:], self.PA[b, 16 + g], writes=[d_kT])
                        Vg, d_V = v_r.next()
                        fw.dma(fw.sp, Vg[:], self.VT[b, :, g * 128:(g + 1) * 128].rearrange("(t p) d -> p t d", p=128), writes=[d_V])
                        for j in range(4):
                            for (q0, n, kts, dsts) in self.qblocks(b, 512, True):
                                for it in qblock_items(b, kT, d_kT, Vg, d_V, 4 * g + j, q0, n, kts, dsts):
                                    yield it

            self.pipeline(gens())

    def diff_B(self, l):
        nc, fw = self.nc, self.fw
        scale = 128 ** -0.5
        lam_init = 0.8 - 0.6 * math.exp(-0.3 * l)
        p = f"l{l}_"
        with self.phase("diffB"):
            lv, d_lv = self.sb("lv", [128, 512], F32)
            lt, d_lt = self.sb("ltmp", [128, 2, 128], F32)
            ls, d_ls = self.sb("ls", [128, 4], F32)
            on, d_on = self.sb("on", [128, 2], F32)
            fw.dma(fw.sp, lv[:], self.W[p + "lvec"], writes=[d_lv])
            fw.dma(fw.sp, on[:], self.W[p + "onT"], writes=[d_on])
            for m in range(2):
                fw.op(fw.dve, lambda: nc.vector.tensor_tensor(lt[:, m, :], lv[:, m * 256:m * 256 + 128],
                                                              lv[:, m * 256 + 128:m * 256 + 256], ALU.mult),
                      reads=[d_lv], writes=[d_lt])
                fw.op(fw.dve, lambda: nc.vector.reduce_sum(ls[:, m:m + 1], lt[:, m, :], axis=AX.X), reads=[d_lt], writes=[d_ls])
            fw.op(fw.act, lambda: nc.scalar.activation(ls[:, 0:2], ls[:, 0:2], AF.Exp), reads=[d_ls], writes=[d_ls])
            fw.op(fw.dve, lambda: nc.vector.tensor_tensor(ls[:, 2:3], ls[:, 1:2], ls[:, 0:1], ALU.subtract), reads=[d_ls], writes=[d_ls])
            fw.op(fw.dve, lambda: nc.vector.tensor_scalar(ls[:, 3:4], ls[:, 2:3], -lam_init, None, ALU.add), reads=[d_ls], writes=[d_ls])
            fw.op(fw.dve, lambda: nc.vector.tensor_scalar(on[:], on[:], 1.0 - lam_init, None, ALU.mult), reads=[d_on], writes=[d_on])
            neglam = ls[:, 3:4]
            k_r = self.ring("kD", 2, [128, 2, NTOK], BF16)
            v_r = self.ring("vD", 2, [128, 18, 256], BF16)
            q_r = self.ring("qD", 3, [128, 2, 256], BF16)
            e_r = self.ring("eD", 4, [128, 512], BF16)
            rd_r = self.ring("rdD", 2, [128, 512], F32)
            t1_r = self.ring("t1D", 2, [128, 2, 256], F32)
            t2_r = self.ring("t2D", 2, [128, 2, 256], F32)
            sq_r = self.ring("sqD", 2, [128, 2, 256], BF16)
            rs_r = self.ring("rsD", 2, [128, 256], F32)
            o_r = self.ring("oD", 2, [128, 2, 256], BF16)
            st_r = Ring(self.banks[0:2])
            ssb, d_ssb = self.banks[2]
            o_sets = Ring([[self.banks[3], self.banks[4]], [self.banks[5], self.banks[6]]])
            Dn, d_D = self.banks[7]

            def qblock_items(b, h, kT, d_kT, Vh, d_V, q0, n, kts, dsts):
                cx = {}
                last = len(kts) - 1

                def item(i):
                    kt = kts[i]
                    if i == 0:
                        cx["q"] = q_r.next()
                        fw.dma(fw.sp, cx["q"][0][:], self.PA[b, 2 * h:2 * h + 2, :, q0:q0 + n].rearrange("c p t -> p c t"),
                               writes=[cx["q"][1]])
                        cx["O"] = o_sets.next()
                    q, d_q = cx["q"]
                    Ob = cx["O"]
                    bk, bd = st_r.next()
                    fw.group(fw.pe, [self.mm(bk[:, m * 256:(m + 1) * 256], kT[:, m, kt * 128:(kt + 1) * 128], q[:, m, :], True, True)
                                     for m in range(2)], reads=[d_kT, d_q], writes=[bd])
                    yield
                    E, d_E = e_r.next()
                    fw.op(fw.act, lambda: nc.scalar.activation(E[:], bk[:], AF.Exp, scale=scale), reads=[bd], writes=[d_E])
                    yield
                    fns = []
                    for dv in range(2):
                        fns.append(self.mm(Ob[dv][0][:, :], Vh[:, kt, dv * 128:(dv + 1) * 128], E[:], i == 0, i == last))
                    fns.append(self.mm(Dn[:, :], self.onesB[:], E[:], i == 0, i == last))
                    fw.group(fw.pe, fns, reads=[d_E, d_V, self.d_ones], writes=[x[1] for x in Ob] + [d_D])
                    if i != last:
                        return
                    t1, d_t1 = t1_r.next()
                    t2, d_t2 = t2_r.next()
                    fw.op(fw.act, lambda: nc.scalar.copy(t1[:, 0, :], Ob[0][0][:, 0:256]), reads=[Ob[0][1]], writes=[d_t1])
                    fw.op(fw.act, lambda: nc.scalar.copy(t2[:, 0, :], Ob[0][0][:, 256:512]), reads=[Ob[0][1]], writes=[d_t2])
                    fw.op(fw.dve, lambda: nc.vector.tensor_copy(t1[:, 1, :], Ob[1][0][:, 0:256]), reads=[Ob[1][1]], writes=[d_t1])
                    fw.op(fw.dve, lambda: nc.vector.tensor_copy(t2[:, 1, :], Ob[1][0][:, 256:512]), reads=[Ob[1][1]], writes=[d_t2])
                    rd, d_rd = rd_r.next()
                    fw.op(fw.act, lambda: nc.scalar.activation(rd[:], Dn[:], AF.Ln), reads=[d_D], writes=[d_rd])
                    fw.op(fw.act, lambda: nc.scalar.activation(rd[:], rd[:], AF.Exp, scale=-1.0), reads=[d_rd], writes=[d_rd])
                    fw.op(fw.dve, lambda: nc.vector.tensor_tensor(t1[:], t1[:], rd[:, 0:256].unsqueeze(1).to_broadcast([128, 2, 256]), ALU.mult),
                          reads=[d_rd], writes=[d_t1])
                    fw.op(fw.dve, lambda: nc.vector.tensor_tensor(t2[:], t2[:], rd[:, 256:512].unsqueeze(1).to_broadcast([128, 2, 256]), ALU.mult),
                          reads=[d_rd], writes=[d_t2])
                    fw.op(fw.dve, lambda: nc.vector.scalar_tensor_tensor(t1[:], t2[:], neglam, t1[:], ALU.mult, ALU.add),
                          reads=[d_t1, d_t2, d_ls], writes=[d_t1])
                    sq, d_sq = sq_r.next()
                    fw.op(fw.act, lambda: nc.scalar.activation(sq[:], t1[:], AF.Square), reads=[d_t1], writes=[d_sq])
                    yield
                    fw.group(fw.pe, [self.mm(ssb[:, 0:256], self.onesB[:], sq[:, dv, :], dv == 0, dv == 1) for dv in range(2)],
                             reads=[d_sq, self.d_ones], writes=[d_ssb])
                    rs, d_rs = rs_r.next()
                    fw.op(fw.act, lambda: nc.scalar.activation(rs[:], ssb[:, 0:256], AF.Ln, bias=self.epsc[:, 0:1], scale=1.0 / 256),
                          reads=[d_ssb, self.d_epsc], writes=[d_rs])
                    fw.op(fw.act, lambda: nc.scalar.activation(rs[:], rs[:], AF.Exp, scale=-0.5), reads=[d_rs], writes=[d_rs])
                    o, d_o = o_r.next()
                    for dv in range(2):
                        fw.op(fw.dve, lambda: nc.vector.scalar_tensor_tensor(o[:, dv, :], t1[:, dv, :], on[:, dv:dv + 1], rs[:],
                                                                             ALU.mult, ALU.mult),
                              reads=[d_t1, d_rs, d_on], writes=[d_o])
                    (blk, col, off, ln) = dsts[0]
                    fw.dma(fw.sp, self.OT[blk, 2 * h:2 * h + 2, :, col:col + ln].rearrange("c p t -> p c t"), o[:],
                           reads=[d_o], writes=[Dep()])

                return [item(i) for i in range(len(kts))]

            def gens():
                for b in range(2):
                    for h in range(8):
                        kT, d_kT = k_r.next()
                        fw.dma(fw.sp, kT[:], self.PA[b, 16 + 2 * h:16 + 2 * h + 2].rearrange("c p t -> p c t"), writes=[d_kT])
                        Vh, d_V = v_r.next()
                        fw.dma(fw.sp, Vh[:], self.VT[b, :, h * 256:(h + 1) * 256].rearrange("(t p) d -> p t d", p=128), writes=[d_V])
                        for (q0, n, kts, dsts) in self.qblocks(b, 256, False):
                            for it in qblock_items(b, h, kT, d_kT, Vh, d_V, q0, n, kts, dsts):
                                yield it

            self.pipeline(gens())

    def gla_A(self, l):
        nc, fw = self.nc, self.fw
        p = f"l{l}_"
        wv = self.W[p + "w_in"].rearrange("(k p) m -> p k m", p=128)
        with self.phase("glaA"):
            self.alloc_norm()
            hT, d_hT = self.sb("hT", [128, KD, TB], BF16)
            wring = self.ring("wG", 4, [128, KD, 128], BF16)
            wtring = self.ring("wtG", 2, [128, KD, 512], BF16)
            zT, d_zT = self.sb("zT", [33, TB], F32)
            wga, d_wga = self.sb("wga", [33, 2, 1024], F32)
            f_r = self.ring("fstG", 3, [128, 512], F32)
            b_r = self.ring("bstG", 3, [128, 512], BF16)
            e_r = self.ring("etG", 2, [128, 512], F32)
            l_r = self.ring("lstG", 2, [128, 512], F32)
            fw.dma(fw.sp, wga[:, 0, :], self.W[p + "wgaf"], writes=[d_wga])
            fw.dma(fw.sp, wga[:, 1, :], self.W[p + "wgab"], writes=[d_wga])
            fw.op(fw.dve, lambda: nc.vector.memset(zT[32:33, :], 1.0), writes=[d_zT])
            subs = self.subs(True)
            self.host_side(l)
            for blk in range(4):
                b, h = divmod(blk, 2)
                self.norm_mod(l, blk, 0, True, hT, d_hT)

                def epi_z(fi, si, rng, bank, bd):
                    s0, s1 = rng
                    fw.op(fw.act, lambda: nc.scalar.copy(zT[0:32, s0:s1], bank[0:32, 0:s1 - s0]), reads=[bd], writes=[d_zT])

                self.linear(hT, d_hT, KD, wv, [6144], subs, epi_z, wring, mw=32)
                for ti in range(9):
                    t0 = ti * 128
                    n0 = self.nat(blk, t0)
                    for dirn in range(2):
                        for half in range(2):
                            bank, bd = self.bank_ring.next()
                            fw.group(fw.pe, [self.mm(bank[:, :], zT[0:33, t0:t0 + 128],
                                                     wga[0:33, dirn, half * 512:(half + 1) * 512], True, True)],
                                     reads=[d_zT, d_wga], writes=[bd])
                            et, d_et = e_r.next()
                            fw.op(fw.act, lambda: nc.scalar.activation(et[:], bank[:], AF.Exp, scale=-1.0), reads=[bd], writes=[d_et])
                            ls, d_l = l_r.next()
                            fw.op(fw.act, lambda: nc.scalar.activation(ls[:], et[:], AF.Ln, bias=self.epsc[:, 1:2], scale=1.0),
                                  reads=[d_et, self.d_epsc], writes=[d_l])
                            c0 = dirn * 1024 + half * 512
                            fw.dma(fw.sp, self.LT[b, n0:n0 + 128, c0:c0 + 512], ls[:], reads=[d_l], writes=[Dep()])

                def epi_qk(fi, si, rng, bank, bd):
                    s0, s1 = rng
                    n = s1 - s0
                    st, d_st = f_r.next()
                    if (fi + si) % 2 == 0:
                        fw.op(fw.act, lambda: nc.scalar.copy(st[:, 0:n], bank[:, 0:n]), reads=[bd], writes=[d_st])
                    else:
                        fw.op(fw.dve, lambda: nc.vector.tensor_copy(st[:, 0:n], bank[:, 0:n]), reads=[bd], writes=[d_st])
                    n0 = self.nat(blk, s0)
                    fw.dma(fw.sp, self.PAF[b, fi, :, n0:n0 + n], st[:, 0:n], reads=[d_st], writes=[Dep()])

                self.linear(hT, d_hT, KD, wv, [c * 128 for c in range(16)], subs, epi_qk, wring)

                def epi_r(fi, si, rng, bank, bd):
                    s0, s1 = rng
                    n = s1 - s0
                    st, d_st = b_r.next()
                    fw.op(fw.act, lambda: nc.scalar.activation(st[:, 0:n], bank[:, 0:n], AF.Silu), reads=[bd], writes=[d_st])
                    n0 = self.nat(blk, s0)
                    fw.dma(fw.sp, self.SR[b, fi, :, n0:n0 + n], st[:, 0:n], reads=[d_st], writes=[Dep()])

                self.linear(hT, d_hT, KD, wv, [4096 + c * 128 for c in range(16)], subs, epi_r, wring)

                def epi_kv(gi, ti, t0, bank, bd):
                    st, d_st = b_r.next()
                    if ti % 2 == 0:
                        fw.op(fw.act, lambda: nc.scalar.copy(st[:], bank[:]), reads=[bd], writes=[d_st])
                    else:
                        fw.op(fw.dve, lambda: nc.vector.tensor_copy(st[:], bank[:]), reads=[bd], writes=[d_st])
                    n0 = self.nat(blk, t0)
                    fw.dma(fw.sp, self.VT[b, n0:n0 + 128, gi * 512:(gi + 1) * 512], st[:], reads=[d_st], writes=[Dep()])

                self.linear_tok(hT, d_hT, wv, [1024 + g * 512 for g in range(6)], [t * 128 for t in range(9)], epi_kv, wtring)
            self.side_drain()

    def gla_B(self, l):
        nc, fw = self.nc, self.fw
        p = f"l{l}_"
        with self.phase("glaB"):
            gc, d_gc = self.sb("gc", [128, 1024], F32)
            on, d_on = self.sb("onG", [128, 4], F32)
            fw.dma(fw.sp, gc[:], self.gla_c, writes=[d_gc])
            fw.dma(fw.sp, on[:], self.W[p + "onT"], writes=[d_on])
            S, _ = self.sb("S", [128, 4, 2, 512], F32)
            Sb, _ = self.sb("Sb", [128, 4, 2, 512], BF16)
            d_S = [Dep() for _ in range(4)]
            d_Sb = [Dep() for _ in range(4)]
            qk_r = self.ring("qkG", 3, [128, 16, 128], F32)
            kv_r = self.ring("kvG", 3, [128, 3072], BF16)
            lt_r = self.ring("ltG", 3, [128, 1024], F32)
            of_r = self.ring("ofG", 4, [128, 16, 128], F32)
            sr_r = self.ring("srG", 4, [128, 16, 128], BF16)
            e13_r = self.ring("e13", 4, [128, 2, 256], F32)
            e2_r = self.ring("e2", 2, [128, 2, 128], F32)
            qt_r = self.ring("qtG", 3, [128, 2, 128], BF16)
            kt_r = self.ring("ktG", 3, [128, 2, 128], BF16)
            qi_r = self.ring("qiG", 5, [128, 2, 128], BF16)
            er_r = self.ring("erG", 2, [128, 256], F32)
            kc_r = self.ring("kcG", 5, [128, 256], BF16)
            am_r = self.ring("amG", 3, [128, 128], BF16)
            os_r = self.ring("osG", 3, [128, 4, 128], F32)
            sq_r = self.ring("sqG", 3, [128, 4, 128], BF16)
            rs_r = self.ring("rsG", 2, [128, 128], F32)
            tm_r = self.ring("tmG", 2, [128, 4, 128], F32)
            fin_r = self.ring("finG", 2, [128, 4, 128], BF16)
            xb_r = Ring(self.banks[0:2])
            rb_r = Ring(self.banks[2:4])
            ab_r = Ring(self.banks[4:5])
            ob_r = Ring(self.banks[5:6])
            sk_r = Ring(self.banks[6:7])
            ss_r = Ring(self.banks[7:8])

            for b in range(2):
                for dirn in range(2):
                    for hh in range(4):
                        fw.op(fw.dve, lambda: nc.vector.memset(S[:, hh], 0.0), writes=[d_S[hh]])
                        fw.op(fw.pool, lambda: nc.gpsimd.memset(Sb[:, hh], 0.0), writes=[d_Sb[hh]])
                    order = list(range(18)) if dirn == 0 else [1, 0] + list(range(17, 1, -1))
                    TT = gc[:, dirn * 256:(dirn + 1) * 256]
                    TriS = gc[:, 512 + dirn * 128:512 + (dirn + 1) * 128]
                    mask = gc[:, 768 + dirn * 128:768 + (dirn + 1) * 128]
                    dcol = 128 + (127 if dirn == 0 else 0)
                    tiles = {}

                    def load_tile(pos, b=b, dirn=dirn, order=order, tiles=tiles):
                        if pos >= len(order) or pos in tiles:
                            return
                        t0 = order[pos] * 128
                        d = {}
                        d["lt"] = lt_r.next()
                        fw.dma(fw.sp, d["lt"][0][:], self.LT[b, t0:t0 + 128, dirn * 1024:(dirn + 1) * 1024], writes=[d["lt"][1]])
                        d["qk"] = qk_r.next()
                        fw.dma(fw.sp, d["qk"][0][:], self.PAF[b, :, :, t0:t0 + 128].rearrange("c p t -> p c t"), writes=[d["qk"][1]])
                        d["kv"] = kv_r.next()
                        fw.dma(fw.sp, d["kv"][0][:], self.VT[b, t0:t0 + 128, :], writes=[d["kv"][1]])
                        if dirn == 1:
                            d["of"] = of_r.next()
                            fw.dma(fw.sp, d["of"][0][:], self.OF[b, :, :, t0:t0 + 128].rearrange("c p t -> p c t"), writes=[d["of"][1]])
                            d["sr"] = sr_r.next()
                            fw.dma(fw.sp, d["sr"][0][:], self.SR[b, :, :, t0:t0 + 128].rearrange("c p t -> p c t"), writes=[d["sr"][1]])
                        tiles[pos] = d

                    def item(pos, hh, b=b, dirn=dirn, order=order, tiles=tiles, TT=TT, TriS=TriS, mask=mask, dcol=dcol):
                        t = order[pos]
                        t0 = t * 128
                        if hh == 0:
                            load_tile(pos)
                            load_tile(pos + 1)
                        td = tiles[pos]
                        lt, d_lt = td["lt"]
                        qk, d_qk = td["qk"]
                        kv, d_kv = td["kv"]
                        if t < 2:
                            oblk, ocol = b * 2 + t, 0
                        else:
                            ltok = (t - 2) * 128
                            oblk, ocol = b * 2 + ltok // 1024, 128 + ltok % 1024
                        xb, d_xb = xb_r.next()
                        fw.group(fw.pe, [self.mm(xb[:, dc * 256:(dc + 1) * 256], lt[:, hh * 256 + dc * 128:hh * 256 + (dc + 1) * 128],
                                                 TT, True, True) for dc in range(2)], reads=[d_lt, d_gc], writes=[d_xb])
                        rbb, d_rb = rb_r.next()
                        fw.group(fw.pe, [self.mm(rbb[:, 0:256], TriS, lt[:, hh * 256:(hh + 1) * 256], True, True)],
                                 reads=[d_lt, d_gc], writes=[d_rb])
                        yield
                        xv = xb[:].rearrange("p (c i) -> p c i", c=2)
                        e13, d_e13 = e13_r.next()
                        fw.op(fw.act, lambda: nc.scalar.activation(e13[:], xv, AF.Exp), reads=[d_xb], writes=[d_e13])
                        e2, d_e2 = e2_r.next()
                        fw.op(fw.act, lambda: nc.scalar.activation(e2[:], xv[:, :, 0:128], AF.Exp, scale=-1.0), reads=[d_xb], writes=[d_e2])
                        er, d_er = er_r.next()
                        fw.op(fw.act, lambda: nc.scalar.activation(er[:], rbb[:, 0:256], AF.Exp), reads=[d_rb], writes=[d_er])
                        qv = qk[:, 2 * hh:2 * hh + 2, :]
                        kvv = qk[:, 8 + 2 * hh:8 + 2 * hh + 2, :]
                        qt, d_qt = qt_r.next()
                        fw.op(fw.dve, lambda: nc.vector.scalar_tensor_tensor(qt[:], qv, 0.0625, e13[:, :, 0:128], ALU.mult, ALU.mult),
                              reads=[d_qk, d_e13], writes=[d_qt])
                        kt_, d_kt = kt_r.next()
                        fw.op(fw.pool, lambda: nc.gpsimd.tensor_tensor(kt_[:], kvv, e2[:], ALU.mult), reads=[d_qk, d_e2], writes=[d_kt])
                        qi, d_qi = qi_r.next()
                        fw.op(fw.dve, lambda: nc.vector.scalar_tensor_tensor(qi[:], qv, 0.0625, e13[:, :, 128:256], ALU.mult, ALU.mult),
                              reads=[d_qk, d_e13], writes=[d_qi])
                        kc, d_kc = kc_r.next()
                        fw.op(fw.pool, lambda: nc.gpsimd.tensor_tensor(kc[:], kv[:, hh * 256:(hh + 1) * 256], er[:], ALU.mult),
                              reads=[d_kv, d_er], writes=[d_kc])
                        yield
                        abb, d_ab = ab_r.next()
                        fw.group(fw.pe, [self.mm(abb[:, 0:128], kt_[:, dc, :], qt[:, dc, :], dc == 0, dc == 1) for dc in range(2)],
                                 reads=[d_kt, d_qt], writes=[d_ab])
                        am, d_am = am_r.next()
                        fw.op(fw.dve, lambda: nc.vector.tensor_tensor(am[:], abb[:, 0:128], mask, ALU.mult), reads=[d_ab, d_gc], writes=[d_am])
                        yield
                        ob, d_ob = ob_r.next()
                        fns = []
                        for dvc in range(4):
                            oo = ob[:, dvc * 128:(dvc + 1) * 128]
                            vcol = 1024 + hh * 512 + dvc * 128
                            fns.append(self.mm(oo, kv[:, vcol:vcol + 128], am[:], True, False))
                            for dc in range(2):
                                fns.append(self.mm(oo, Sb[:, hh, dc, dvc * 128:(dvc + 1) * 128], qi[:, dc, :], False, dc == 1))
                        fw.group(fw.pe, fns, reads=[d_kv, d_am, d_Sb[hh], d_qi], writes=[d_ob])
                        for dc in range(2):
                            sbk, d_sbk = sk_r.next()
                            fw.group(fw.pe, [self.mm(sbk[:, :], kc[:, dc * 128:(dc + 1) * 128],
                                                     kv[:, 1024 + hh * 512:1024 + (hh + 1) * 512], True, True)],
                                     reads=[d_kc, d_kv], writes=[d_sbk])
                            fw.op(fw.dve, lambda: nc.vector.scalar_tensor_tensor(S[:, hh, dc, :], S[:, hh, dc, :],
                                                                                 e13[:, dc, dcol:dcol + 1], sbk[:, :],
                                                                                 ALU.mult, ALU.add),
                                  reads=[d_sbk, d_e13], writes=[d_S[hh]])
                            fw.op(fw.act, lambda: nc.scalar.copy(Sb[:, hh, dc, :], S[:, hh, dc, :]), reads=[d_S[hh]], writes=[d_Sb[hh]])
                        ov = ob[:].rearrange("p (c i) -> p c i", c=4)
                        os_, d_os = os_r.next()
                        if dirn == 0:
                            fw.op(fw.act, lambda: nc.scalar.copy(os_[:], ov), reads=[d_ob], writes=[d_os])
                            fw.dma(fw.sp, self.OF[b, 4 * hh:4 * hh + 4, :, t0:t0 + 128].rearrange("c p t -> p c t"), os_[:],
                                   reads=[d_os], writes=[Dep()])
                            return
                        of, d_of = td["of"]
                        sr, d_sr = td["sr"]
                        fw.op(fw.dve, lambda: nc.vector.tensor_tensor(os_[:], ov, of[:, 4 * hh:4 * hh + 4, :], ALU.add),
                              reads=[d_ob, d_of], writes=[d_os])
                        sq, d_sq = sq_r.next()
                        fw.op(fw.act, lambda: nc.scalar.activation(sq[:], os_[:], AF.Square), reads=[d_os], writes=[d_sq])
                        yield
                        ssb, d_ssb = ss_r.next()
                        fw.group(fw.pe, [self.mm(ssb[:, 0:128], self.onesB[:], sq[:, dvc, :], dvc == 0, dvc == 3) for dvc in range(4)],
                                 reads=[d_sq, self.d_ones], writes=[d_ssb])
                        rs, d_rs = rs_r.next()
                        fw.op(fw.act, lambda: nc.scalar.activation(rs[:], ssb[:, 0:128], AF.Ln, bias=self.epsc[:, 0:1], scale=1.0 / 512),
                              reads=[d_ssb, self.d_epsc], writes=[d_rs])
                        fw.op(fw.act, lambda: nc.scalar.activation(rs[:], rs[:], AF.Exp, scale=-0.5), reads=[d_rs], writes=[d_rs])
                        tm, d_tm = tm_r.next()
                        fw.op(fw.dve, lambda: nc.vector.tensor_tensor(tm[:], os_[:], rs[:].unsqueeze(1).to_broadcast([128, 4, 128]), ALU.mult),
                              reads=[d_os, d_rs], writes=[d_tm])
                        fw.op(fw.pool, lambda: nc.gpsimd.tensor_tensor(tm[:], tm[:], on[:].unsqueeze(2).to_broadcast([128, 4, 128]), ALU.mult),
                              reads=[d_tm, d_on], writes=[d_tm])
                        fin, d_fin = fin_r.next()
                        fw.op(fw.pool, lambda: nc.gpsimd.tensor_tensor(fin[:], tm[:], sr[:, 4 * hh:4 * hh + 4, :], ALU.mult),
                              reads=[d_tm, d_sr], writes=[d_fin])
                        fw.dma(fw.sp, self.OT[oblk, 4 * hh:4 * hh + 4, :, ocol:ocol + 128].rearrange("c p t -> p c t"), fin[:],
                               reads=[d_fin], writes=[Dep()])

                    self.pipeline(item(pos, hh) for pos in range(18) for hh in range(4))


def _vecT(v, k):
    return np.ascontiguousarray(np.asarray(v, np.float32).reshape(k, 128).T)


def _bf(a):
    return np.ascontiguousarray(np.asarray(a).astype(ml_dtypes.bfloat16))


def _pk(m):
    K = m.shape[0] // 128
    return np.ascontiguousarray(m.reshape(K, 128, m.shape[1]).transpose(1, 0, 2).reshape(128, K * m.shape[1]))


_CONST_CACHE = {}


def _constants(layers):
    key = tuple(layers)
    if key in _CONST_CACHE:
        return _CONST_CACHE[key]
    c = {}
    c["identF"] = np.eye(128, dtype=np.float32)
    c["onesB"] = _bf(np.ones((128, 128), np.float32))
    if 1 in layers or 2 in layers:
        t = np.arange(SEQ)
        row = (t // 64).astype(np.float32)
        col = (t % 64).astype(np.float32)
        half = 64
        inv_freq = (np.float32(10000.0) ** (-np.arange(0, half, 2, dtype=np.float32) / np.float32(half))).astype(np.float32)
        ang_r = row[:, None] * inv_freq[None, :]
        ang_c = col[:, None] * inv_freq[None, :]
        ang = np.concatenate([ang_r, ang_r, ang_c, ang_c], axis=-1).astype(np.float32)
        c["rope"] = np.ascontiguousarray(np.concatenate([np.cos(ang).T, np.sin(ang).T], axis=1).astype(np.float32))
        R = np.zeros((128, 128), np.float32)
        for d in range(128):
            q = d // 32
            if q % 2 == 0:
                R[d, d + 32] = -1.0
            else:
                R[d, d - 32] = 1.0
        c["RT"] = _bf(R.T)
    if 3 in layers:
        def dft(n):
            k = np.arange(n, dtype=np.float64)
            ang = 2.0 * np.pi * np.outer(k, k) / n
            return np.cos(ang) / np.sqrt(n), np.sin(ang) / np.sqrt(n)
        cc, sc = dft(512)
        cl, sl = dft(SEQ)
        c["dftc"] = _bf(np.concatenate([_pk(cc), _pk(sc)], axis=1))
        c["dftl"] = _bf(np.concatenate([_pk(cl), _pk(sl)], axis=1))
    if 0 in layers:
        j = np.arange(128)[:, None]
        i = np.arange(128)[None, :]
        s = -1.0 / 16.0
        tri_f = (j <= i).astype(np.float64)
        tri_b = (j >= i).astype(np.float64)
        m1_f = tri_f - (j <= 63)
        m1_b = tri_b - (j >= 64)
        tris_f = (j > i).astype(np.float64)
        tris_b = (j < i).astype(np.float64)
        c["gla_c"] = np.ascontiguousarray(np.concatenate(
            [s * m1_f, s * tri_f, s * m1_b, s * tri_b, s * tris_f, s * tris_b, tri_f, tri_b], axis=1).astype(np.float32))
    _CONST_CACHE[key] = c
    return c


def _prep_inputs(inputs, layers, x_over=None, ctx_over=None):
    f32 = lambda a: np.ascontiguousarray(np.asarray(a, np.float32))
    shared = dict(_constants(layers))
    for l in layers:
        p = f"l{l}_"
        shared[p + "mod_w"] = f32(inputs[p + "mod_w"])
        shared[p + "mod_bT"] = _vecT(inputs[p + "mod_b"], 96)
        shared[p + "n1T"] = _vecT(inputs[p + "norm1"], KD)
        shared[p + "n2T"] = _vecT(inputs[p + "norm2"], KD)
        shared[p + "ffn_w_in"] = f32(inputs[p + "ffn_w_in"])
        shared[p + "ffn_w_out"] = f32(inputs[p + "ffn_w_out"])
        if l == 0:
            shared[p + "w_in"] = f32(inputs[p + "gla_w_in"])
            z16 = np.zeros((16, 1024), np.float32)
            shared[p + "wgaf"] = np.ascontiguousarray(np.concatenate(
                [f32(inputs[p + "gla_wg_f"]), z16, f32(inputs[p + "gla_bg_f"])[None]], axis=0))
            shared[p + "wgab"] = np.ascontiguousarray(np.concatenate(
                [z16, f32(inputs[p + "gla_wg_b"]), f32(inputs[p + "gla_bg_b"])[None]], axis=0))
            shared[p + "onT"] = _vecT(inputs[p + "gla_out_norm"], 4)
            shared[p + "w_out"] = f32(inputs[p + "gla_w_out"])
        elif l == 1:
            shared[p + "w_in"] = f32(inputs[p + "gqa_w_in"])
            shared[p + "qk"] = np.ascontiguousarray(np.stack([f32(inputs[p + "gqa_q_norm"]), f32(inputs[p + "gqa_k_norm"])], axis=1))
            shared[p + "w_out"] = f32(inputs[p + "gqa_w_out"])
        elif l == 2:
            shared[p + "w_in"] = f32(inputs[p + "diff_w_in"])
            shared[p + "qk"] = np.ascontiguousarray(np.stack([f32(inputs[p + "diff_q_norm"]), f32(inputs[p + "diff_k_norm"])], axis=1))
            lv = np.concatenate([f32(inputs[p + "diff_lq1"]), f32(inputs[p + "diff_lk1"]),
                                 f32(inputs[p + "diff_lq2"]), f32(inputs[p + "diff_lk2"])])
            shared[p + "lvec"] = np.ascontiguousarray(np.broadcast_to(lv[None, :], (128, 512)))
            shared[p + "onT"] = _vecT(inputs[p + "diff_out_norm"], 2)
            shared[p + "w_out"] = f32(inputs[p + "diff_w_out"])
        else:
            shared[p + "w_out"] = f32(inputs[p + "fnet_w_out"])
    x = f32(inputs["x"]) if x_over is None else x_over
    ctx = f32(inputs["ctx"]) if ctx_over is None else ctx_over
    c = f32(inputs["c"])
    c_ctx = f32(inputs["c_ctx"])
    nb = x.shape[0] // 2
    maps = []
    for i in range(nb):
        m = dict(shared)
        m["x"] = np.ascontiguousarray(x[2 * i:2 * i + 2])
        m["ctx"] = np.ascontiguousarray(ctx[2 * i:2 * i + 2])
        cv = np.stack([c[2 * i], c[2 * i + 1], c_ctx], axis=0)
        m["cvecT"] = np.ascontiguousarray(cv.reshape(3, KD, 128).transpose(2, 1, 0).reshape(128, KD * 3))
        maps.append(m)
    return maps


_NC_CACHE = {}


def _get_nc(layers):
    key = tuple(layers)
    if key not in _NC_CACHE:
        _NC_CACHE[key] = Builder(layers).build()
    return _NC_CACHE[key]


def run_layers(inputs, layers, x_over=None, ctx_over=None, trace=False):
    maps = _prep_inputs(inputs, layers, x_over, ctx_over)
    nc = _get_nc(layers)
    res = run_bass_kernel_spmd(nc, maps, core_ids=list(range(len(maps))), trace=trace)
    out = np.concatenate([r["y"] for r in res.results], axis=0)
    return out, res


def kernel(**inputs):
    out, _ = run_layers(inputs, (0, 1, 2, 3))
    return out.astype(np.float32, copy=False)
```

```python
import math
from contextlib import ExitStack, contextmanager

import numpy as np
import ml_dtypes

import concourse.bass as bass
import concourse.mybir as mybir
from concourse.bass_utils import run_bass_kernel_spmd

F32 = mybir.dt.float32
BF16 = mybir.dt.bfloat16
AF = mybir.ActivationFunctionType
ALU = mybir.AluOpType
AX = mybir.AxisListType

N_CORES = 8
D = 2048
KD = 16
SEQ = 2048
CTX = 256
NTOK = CTX + SEQ
TB = 1152
FFN_H = 5632
NF = FFN_H // 128
EPS = 1e-6
SEM_LIMIT = 30000


class Dep:
    __slots__ = ("w", "r", "ep")

    def __init__(self):
        self.w = {}
        self.r = {}
        self.ep = -1


class Eng:
    def __init__(self, fw, name, eng):
        self.fw = fw
        self.name = name
        self.eng = eng
        self.sem = None
        self.count = 0
        self.known = {}
        self.nsem = 0
        self.n_inst = 0
        self.n_wait = 0

    def rotate(self):
        self.sem = self.fw.new_sem(f"{self.name}_s{self.nsem}")
        self.nsem += 1
        self.count = 0

    def wait(self, toks):
        for s, v in toks.items():
            if self.known.get(s, 0) < v:
                self.eng.wait_ge(s, v)
                self.known[s] = v
                self.n_wait += 1


class FW:
    def __init__(self, nc, n_dma_slots=24):
        self.nc = nc
        self.stack = ExitStack()
        self.sem_count = 0
        self.epoch = 0
        self.pe = Eng(self, "pe", nc.tensor)
        self.act = Eng(self, "act", nc.scalar)
        self.dve = Eng(self, "dve", nc.vector)
        self.pool = Eng(self, "pool", nc.gpsimd)
        self.sp = Eng(self, "sp", nc.sync)
        self.engs = [self.pe, self.act, self.dve, self.pool, self.sp]
        for e in self.engs:
            e.rotate()
        self.slots = [[self.new_sem(f"dma{i}"), 0] for i in range(n_dma_slots)]
        self.slot_i = 0
        self.n_dma = 0

    def new_sem(self, name):
        self.sem_count += 1
        return self.stack.enter_context(self.nc.semaphore(name))

    def _sync(self, d):
        if d.ep != self.epoch:
            d.w = {}
            d.r = {}
            d.ep = self.epoch

    def _collect(self, reads, writes):
        toks = {}
        for d in reads:
            self._sync(d)
            for s, v in d.w.items():
                if toks.get(s, 0) < v:
                    toks[s] = v
        for d in writes:
            self._sync(d)
            for s, v in d.w.items():
                if toks.get(s, 0) < v:
                    toks[s] = v
            for s, v in d.r.items():
                if toks.get(s, 0) < v:
                    toks[s] = v
        return toks

    def _record(self, tok, reads, writes):
        s, v = tok
        for d in reads:
            if d.r.get(s, 0) < v:
                d.r[s] = v
        for d in writes:
            d.w = {s: v}
            d.r = {}

    def op(self, E, fn, reads=(), writes=()):
        return self.group(E, [fn], reads, writes)

    def group(self, E, fns, reads=(), writes=()):
        toks = self._collect(reads, writes)
        if E is self.pe:
            toks.pop(E.sem, None)
        E.wait(toks)
        ins = None
        for fn in fns:
            ins = fn()
            E.n_inst += 1
        if E.count >= SEM_LIMIT:
            E.rotate()
        ins.then_inc(E.sem, 1)
        E.count += 1
        tok = (E.sem, E.count)
        self._record(tok, reads, writes)
        return tok

    def dma(self, Q, out, in_, reads=(), writes=()):
        toks = self._collect(reads, writes)
        slot = self.slots[self.slot_i]
        self.slot_i = (self.slot_i + 1) % len(self.slots)
        if slot[1] > 0:
            toks[slot[0]] = max(toks.get(slot[0], 0), slot[1])
        Q.wait(toks)
        if slot[1] + 16 > SEM_LIMIT:
            slot[0] = self.new_sem(f"dmar{self.sem_count}")
            slot[1] = 0
        self.nc_dma(Q, out, in_).then_inc(slot[0], 16)
        slot[1] += 16
        self.n_dma += 1
        tok = (slot[0], slot[1])
        self._record(tok, reads, writes)
        return tok

    def nc_dma(self, Q, out, in_):
        return Q.eng.dma_start(out=out, in_=in_)

    def barrier(self):
        toks = {}
        for E in self.engs:
            if E.count > 0:
                toks[E.sem] = E.count
        for s, v in self.slots:
            if v > 0:
                toks[s] = v
        for E in self.engs:
            E.wait(dict(toks))
        self.epoch += 1

    def close(self):
        self.stack.close()


class Ring:
    def __init__(self, items):
        self.items = items
        self.i = 0

    def next(self):
        it = self.items[self.i]
        self.i = (self.i + 1) % len(self.items)
        return it


class Builder:
    def __init__(self, layers=(0, 1, 2, 3), debug_out=None):
        self.layers = list(layers)
        self.nc = bass.Bass("TRN2", target_bir_lowering=False)
        Builder.last = self
        self.fw = FW(self.nc)
        self.gs = ExitStack()
        self.ps = None
        self.inputs = {}
        self.consts = {}

    def din(self, name, shape, dt=F32):
        t = self.nc.dram_tensor(name, list(shape), dt, kind="ExternalInput").ap()
        self.inputs[name] = t
        return t

    def dscr(self, name, shape, dt):
        return self.nc.dram_tensor(name, list(shape), dt, kind="Internal").ap()

    def _uname(self, name):
        self.ucount = getattr(self, "ucount", 0) + 1
        return f"s{self.ucount}_{name}"

    def gsb(self, name, shape, dt):
        return self.gs.enter_context(self.nc.sbuf_tensor(self._uname(name), list(shape), dt)), Dep()

    def sb(self, name, shape, dt):
        return self.ps.enter_context(self.nc.sbuf_tensor(self._uname(name), list(shape), dt)), Dep()

    def ring(self, name, n, shape, dt):
        return Ring([self.sb(f"{name}{i}", shape, dt) for i in range(n)])

    @contextmanager
    def phase(self, name):
        self.fw.barrier()
        with ExitStack() as ps:
            self.ps = ps
            yield
            self.fw.barrier()
        self.ps = None

    def mm(self, out, lhsT, rhs, start, stop):
        nc = self.nc
        return lambda: nc.tensor.matmul(out, lhsT, rhs, start=start, stop=stop)

    def build(self):
        nc, fw = self.nc, self.fw
        L = self.layers
        self.x_in = self.din("x", [2, SEQ, D])
        self.ctx_in = self.din("ctx", [2, CTX, D])
        self.cvecT = self.din("cvecT", [128, KD * 3])
        self.identF_d = self.din("identF", [128, 128])
        self.onesB_d = self.din("onesB", [128, 128], BF16)
        self.W = {}
        for l in L:
            p = f"l{l}_"
            self.W[p + "mod_w"] = self.din(p + "mod_w", [D, 6 * D])
            self.W[p + "mod_bT"] = self.din(p + "mod_bT", [128, 96])
            self.W[p + "n1T"] = self.din(p + "n1T", [128, KD])
            self.W[p + "n2T"] = self.din(p + "n2T", [128, KD])
            self.W[p + "ffn_w_in"] = self.din(p + "ffn_w_in", [D, 2 * FFN_H])
            self.W[p + "ffn_w_out"] = self.din(p + "ffn_w_out", [FFN_H, D])
        if 0 in L:
            p = "l0_"
            self.W[p + "w_in"] = self.din(p + "w_in", [D, 6176])
            self.W[p + "wgaf"] = self.din(p + "wgaf", [33, 1024])
            self.W[p + "wgab"] = self.din(p + "wgab", [33, 1024])
            self.W[p + "onT"] = self.din(p + "onT", [128, 4])
            self.W[p + "w_out"] = self.din(p + "w_out", [D, D])
            self.gla_c = self.din("gla_c", [128, 1024])
            self.gla_cb = self.din("gla_cb", [128, 768], BF16)
        if 1 in L:
            p = "l1_"
            self.W[p + "w_in"] = self.din(p + "w_in", [D, 3072])
            self.W[p + "qk"] = self.din(p + "qk", [128, 2])
            self.W[p + "w_out"] = self.din(p + "w_out", [D, D])
        if 2 in L:
            p = "l2_"
            self.W[p + "w_in"] = self.din(p + "w_in", [D, 6144])
            self.W[p + "qk"] = self.din(p + "qk", [128, 2])
            self.W[p + "lvec"] = self.din(p + "lvec", [128, 512])
            self.W[p + "onT"] = self.din(p + "onT", [128, 2])
            self.W[p + "w_out"] = self.din(p + "w_out", [D, D])
        if 1 in L or 2 in L:
            self.rope_d = self.din("rope", [128, 2 * SEQ])
            self.RT_d = self.din("RT", [128, 128], BF16)
        if 3 in L:
            p = "l3_"
            self.W[p + "w_out"] = self.din(p + "w_out", [D, D])
            self.dftc_d = self.din("dftc", [128, 2 * 4 * 512], BF16)
            self.dftl_d = self.din("dftl", [128, 2 * 16 * SEQ], BF16)
        self.y_out = nc.dram_tensor("y", [2, SEQ, D], F32, kind="ExternalOutput").ap()

        self.xT = self.dscr("xT", [4, KD, 128, TB], F32)
        self.OT = self.dscr("OT", [4, KD, 128, TB], BF16)
        self.RS = self.dscr("RS", [2, 4, 128, TB], F32)
        self.PA = self.dscr("PA", [2, 32, 128, NTOK], BF16)
        self.VT = self.dscr("VT", [2, NTOK, 3072], BF16)
        if 0 in L:
            self.PAF = self.dscr("PAF", [2, 16, 128, NTOK], F32)
            self.SR = self.dscr("SR", [2, 16, 128, NTOK], BF16)
            self.LT = self.dscr("LT", [2, NTOK, 2048], F32)
            self.OF = self.dscr("OF", [2, 16, 128, NTOK], F32)
        if 3 in L:
            self.AB = self.dscr("AB", [2, 4, 2, 16, 128, 512], BF16)

        self.banks = []
        for i in range(8):
            t = self.gs.enter_context(nc.psum_tensor(f"bank{i}", [128, 512], F32))
            self.banks.append((t, Dep()))
        self.bank_ring = Ring(self.banks)
        self.identF, self.d_ident = self.gsb("identF_s", [128, 128], F32)
        self.onesB, self.d_ones = self.gsb("onesB_s", [128, 128], BF16)
        self.epsc, self.d_epsc = self.gsb("epsc", [128, 2], F32)
        self.coef = {}
        for l in L:
            self.coef[l] = self.gsb(f"coef{l}", [128, 96, 3], F32)
        self.sTb, self.d_sTb = self.gsb("sTb", [128, KD, 3], BF16)
        self.side = None
        self.side_every = 4
        self.side_count = 0

        with nc.Block() as block:
            @block.sync
            def _(sync):
                self.emit_all()
        self.gs.close()
        fw.close()
        return nc

    def emit_all(self):
        nc, fw = self.nc, self.fw
        fw.dma(fw.sp, self.identF[:], self.identF_d, writes=[self.d_ident])
        fw.dma(fw.sp, self.onesB[:], self.onesB_d, writes=[self.d_ones])
        fw.op(fw.dve, lambda: nc.vector.memset(self.epsc[:, 0:1], EPS), writes=[self.d_epsc])
        fw.op(fw.dve, lambda: nc.vector.memset(self.epsc[:, 1:2], 1.0), writes=[self.d_epsc])
        self.phase_mod()
        self.phase_in()
        for l in self.layers:
            kind = l % 4
            ctx_out = l < 2
            if kind == 0:
                self.gla_A(l)
                self.gla_B(l)
            elif kind == 1:
                self.attn_A(l, diff=False)
                self.gqa_B(l)
            elif kind == 2:
                self.attn_A(l, diff=True)
                self.diff_B(l)
            else:
                self.fnet_A(l)
                self.fnet_B(l)
            self.phase_C1(l, ctx_out)
            self.phase_C2(l, ctx_out)
        self.phase_out()

    def phase_mod(self):
        nc, fw = self.nc, self.fw
        with self.phase("mod"):
            cT, d_cT = self.sb("cT", [128, KD, 3], F32)
            fw.dma(fw.sp, cT[:].rearrange("p k r -> p (k r)"), self.cvecT, writes=[d_cT])
            fw.op(fw.act, lambda: nc.scalar.activation(self.sTb[:], cT[:], AF.Silu), reads=[d_cT], writes=[self.d_sTb])
            for _ in self.mod_gen(self.layers[0]):
                pass

    def mod_gen(self, l):
        nc, fw = self.nc, self.fw
        p = f"l{l}_"
        mw = self.W[p + "mod_w"].rearrange("(k p) m -> p k m", p=128)
        cf, d_cf = self.coef[l]
        wring = self.ring("modw", 2, [128, KD, 512], BF16)
        row_r = self.ring("modrow", 2, [3, 512], F32)
        mbT, d_mb = self.sb("mbT", [128, 96], F32)
        nT, d_nT = self.sb("nT", [128, 2, KD], F32)
        fw.dma(fw.sp, mbT[:], self.W[p + "mod_bT"], writes=[d_mb])
        fw.dma(fw.sp, nT[:, 0, :], self.W[p + "n1T"], writes=[d_nT])
        fw.dma(fw.sp, nT[:, 1, :], self.W[p + "n2T"], writes=[d_nT])
        tiles = {}

        def load(j):
            if j < 24 and j not in tiles:
                tiles[j] = wring.next()
                fw.dma(fw.pool, tiles[j][0][:], mw[:, :, j * 512:(j + 1) * 512], writes=[tiles[j][1]])

        d_parts = [Dep() for _ in range(24)]
        for j in range(24):
            load(j)
            load(j + 1)
            wt, wd = tiles.pop(j)
            bank, bd = self.bank_ring.next()
            fw.group(fw.pe, [self.mm(bank[0:3, :], self.sTb[:, k, :], wt[:, k, :], k == 0, k == KD - 1)
                             for k in range(KD)], reads=[wd, self.d_sTb], writes=[bd])
            row, d_row = row_r.next()
            fw.op(fw.act, lambda: nc.scalar.copy(row[:], bank[0:3, :]), reads=[bd], writes=[d_row])
            yield
            bank2, bd2 = self.bank_ring.next()
            fw.group(fw.pe, [
                (lambda q=q: nc.tensor.transpose(bank2[:, q * 3:(q + 1) * 3], row[0:3, q * 128:(q + 1) * 128],
                                                 self.identF[0:3, 0:3]))
                for q in range(4)], reads=[d_row, self.d_ident], writes=[bd2])
            fw.op(fw.dve, lambda: nc.vector.tensor_tensor(
                cf[:, 4 * j:4 * j + 4, :], bank2[:, 0:12].rearrange("p (c r) -> p c r", r=3),
                mbT[:, 4 * j:4 * j + 4].unsqueeze(2).to_broadcast([128, 4, 3]), ALU.add),
                reads=[bd2, d_mb], writes=[d_parts[j]])
            yield
        for w, c0 in ((0, 16), (1, 64)):
            fw.op(fw.dve, lambda w=w, c0=c0: nc.vector.scalar_tensor_tensor(
                cf[:, c0:c0 + 16, :], cf[:, c0:c0 + 16, :], 1.0,
                nT[:, w, :].unsqueeze(2).to_broadcast([128, KD, 3]), ALU.add, ALU.mult),
                reads=[d_nT] + d_parts, writes=[d_cf])

    def host_side(self, l):
        nxt = [x for x in self.layers if x > l]
        if nxt:
            self.side = self.mod_gen(nxt[0])
            self.side_count = 0

    def side_tick(self):
        if self.side is not None:
            self.side_count += 1
            if self.side_count % self.side_every == 0:
                try:
                    next(self.side)
                except StopIteration:
                    self.side = None

    def side_drain(self):
        if self.side is not None:
            for _ in self.side:
                pass
            self.side = None

    def block_tiles(self, blk):
        b, h = divmod(blk, 2)
        tiles = [(self.ctx_in[b, h * 128:(h + 1) * 128, :], self.y_out[b, 0:128, :], 0, True)]
        for t in range(8):
            r0 = h * 1024 + t * 128
            tiles.append((self.x_in[b, r0:r0 + 128, :], self.y_out[b, r0:r0 + 128, :], 128 + t * 128, False))
        return tiles

    def phase_in(self):
        nc, fw = self.nc, self.fw
        with self.phase("in"):
            xin = self.ring("xin", 3, [128, D], F32)
            xst = self.ring("xst", 3, [128, KD, 128], F32)
            sqt = self.ring("sqt", 3, [128, KD, 128], BF16)
            rs, d_rs = self.sb("rs_in", [128, TB], F32)
            ring5 = Ring(self.banks[0:5])
            sbank = Ring(self.banks[5:8])
            for blk in range(4):
                xv = self.xT[blk].rearrange("k p t -> p k t")
                pending = []
                for (src, _, c0, is_ctx) in self.block_tiles(blk):
                    xi, d_xi = xin.next()
                    fw.dma(fw.sp, xi[:], src, writes=[d_xi])
                    xs, d_xs = xst.next()
                    for g in range(4):
                        bank, bd = ring5.next()
                        fw.group(fw.pe, [
                            (lambda q=q: nc.tensor.transpose(bank[:, q * 128:(q + 1) * 128],
                                                             xi[:, (g * 4 + q) * 128:(g * 4 + q + 1) * 128],
                                                             self.identF[:]))
                            for q in range(4)], reads=[d_xi, self.d_ident], writes=[bd])
                        dst = xs[:, g * 4:(g + 1) * 4, :]
                        srcb = bank[:].rearrange("p (q t) -> p q t", q=4)
                        if g % 2 == 0:
                            fw.op(fw.act, lambda: nc.scalar.copy(dst, srcb), reads=[bd], writes=[d_xs])
                        else:
                            fw.op(fw.dve, lambda: nc.vector.tensor_copy(dst, srcb), reads=[bd], writes=[d_xs])
                    while pending:
                        pending.pop(0)()
                    fw.dma(fw.sp, xv[:, :, c0:c0 + 128], xs[:], reads=[d_xs], writes=[Dep()])
                    sq, d_sq = sqt.next()
                    fw.op(fw.act, lambda: nc.scalar.activation(sq[:], xs[:], AF.Square), reads=[d_xs], writes=[d_sq])

                    def stat(sq=sq, d_sq=d_sq, c0=c0):
                        sb_, d_sb = sbank.next()
                        fw.group(fw.pe, [self.mm(sb_[:, 0:128], self.onesB[:], sq[:, k, :], k == 0, k == KD - 1)
                                         for k in range(KD)], reads=[d_sq, self.d_ones], writes=[d_sb])
                        fw.op(fw.act, lambda: nc.scalar.activation(rs[:, c0:c0 + 128], sb_[:, 0:128], AF.Ln,
                                                                   bias=self.epsc[:, 0:1], scale=1.0 / D),
                              reads=[d_sb, self.d_epsc], writes=[d_rs])
                    pending.append(stat)
                while pending:
                    pending.pop(0)()
                fw.op(fw.act, lambda: nc.scalar.activation(rs[:], rs[:], AF.Exp, scale=-0.5), reads=[d_rs], writes=[d_rs])
                fw.dma(fw.sp, self.RS[0, blk], rs[:], reads=[d_rs], writes=[Dep()])

    def phase_out(self):
        nc, fw = self.nc, self.fw
        with self.phase("out"):
            xin = self.ring("xo_in", 2, [128, KD, 128], F32)
            xst = self.ring("xo_st", 2, [128, D], F32)
            outs = []
            for blk in range(4):
                xv = self.xT[blk].rearrange("k p t -> p k t")
                for (_, dst_d, c0, is_ctx) in self.block_tiles(blk):
                    if is_ctx:
                        continue
                    xi, d_xi = xin.next()
                    fw.dma(fw.sp, xi[:], xv[:, :, c0:c0 + 128], writes=[d_xi])
                    xs, d_xs = xst.next()
                    for g in range(4):
                        bank, bd = self.bank_ring.next()
                        fw.group(fw.pe, [
                            (lambda q=q: nc.tensor.transpose(bank[:, q * 128:(q + 1) * 128],
                                                             xi[:, g * 4 + q, :], self.identF[:]))
                            for q in range(4)], reads=[d_xi, self.d_ident], writes=[bd])
                        dst = xs[:, g * 512:(g + 1) * 512]
                        if g % 2 == 0:
                            fw.op(fw.act, lambda: nc.scalar.copy(dst, bank[:]), reads=[bd], writes=[d_xs])
                        else:
                            fw.op(fw.dve, lambda: nc.vector.tensor_copy(dst, bank[:]), reads=[bd], writes=[d_xs])
                    dd = Dep()
                    fw.dma(fw.sp, dst_d, xs[:], reads=[d_xs], writes=[dd])
                    outs.append(dd)

    @staticmethod
    def subs(ctx):
        return ([(0, 128)] if ctx else []) + [(128, 640), (640, TB)]

    def regions(self, blk, ctx):
        b = blk // 2
        return ([(0, 128, 2)] if ctx else []) + [(128, TB, b)]

    def alloc_norm(self, n_xc=3):
        self.xc_ring = self.ring("xc", n_xc, [128, TB], F32)
        self.tmp_ring = self.ring("tmpf", 2, [128, TB], F32)
        self.rstd, self.d_rstd = self.sb("rstd", [128, TB], F32)

    def norm_mod(self, l, blk, which, ctx, hT, d_hT):
        nc, fw = self.nc, self.fw
        cf, d_cf = self.coef[l]
        cB, cA = (0, 16) if which == 0 else (48, 64)
        subs = self.subs(ctx)
        a0 = subs[0][0]
        fw.dma(fw.sp, self.rstd[:, a0:TB], self.RS[which, blk, :, a0:TB], writes=[self.d_rstd])
        for k in range(KD):
            xc, d_xc = self.xc_ring.next()
            fw.dma(fw.sp, xc[:, a0:TB], self.xT[blk, k, :, a0:TB], writes=[d_xc])
            tm, d_tm = self.tmp_ring.next()
            fw.op(fw.dve, lambda: nc.vector.tensor_tensor(tm[:, a0:TB], xc[:, a0:TB], self.rstd[:, a0:TB], ALU.mult),
                  reads=[d_xc, self.d_rstd], writes=[d_tm])
            for (r0, r1, r) in self.regions(blk, ctx):
                fw.op(fw.act, lambda: nc.scalar.activation(hT[:, k, r0:r1], tm[:, r0:r1], AF.Identity,
                                                           bias=cf[:, cB + k, r:r + 1], scale=cf[:, cA + k, r:r + 1]),
                      reads=[d_tm, d_cf], writes=[d_hT])

    def stats_begin(self, subs):
        return {"subs": subs, "banks": [self.banks[5 + i] for i in range(len(subs))], "pending": []}

    def stats_push(self, st, src, d_src, k, si, sq, d_sq):
        nc, fw = self.nc, self.fw
        s0, s1 = st["subs"][si]
        fw.op(fw.act, lambda: nc.scalar.activation(sq[:, s0:s1], src[:, s0:s1], AF.Square), reads=[d_src], writes=[d_sq])
        bk, bd = st["banks"][si]
        st["pending"].append(lambda: fw.group(
            fw.pe, [self.mm(bk[:, 0:s1 - s0], self.onesB[:], sq[:, s0:s1], k == 0, k == KD - 1)],
            reads=[d_sq, self.d_ones], writes=[bd]))

    def stats_flush(self, st):
        while st["pending"]:
            st["pending"].pop(0)()

    def stats_end(self, st, which, blk, rs, d_rs):
        nc, fw = self.nc, self.fw
        self.stats_flush(st)
        a0 = st["subs"][0][0]
        for (bk, bd), (s0, s1) in zip(st["banks"], st["subs"]):
            fw.op(fw.act, lambda: nc.scalar.activation(rs[:, s0:s1], bk[:, 0:s1 - s0], AF.Ln,
                                                       bias=self.epsc[:, 0:1], scale=1.0 / D),
                  reads=[bd, self.d_epsc], writes=[d_rs])
        fw.op(fw.act, lambda: nc.scalar.activation(rs[:, a0:TB], rs[:, a0:TB], AF.Exp, scale=-0.5),
              reads=[d_rs], writes=[d_rs])
        fw.dma(fw.sp, self.RS[which, blk, :, a0:TB], rs[:, a0:TB], reads=[d_rs], writes=[Dep()])

    def pipeline(self, gens):
        def step(g):
            try:
                next(g)
                return True
            except StopIteration:
                return False
        active = []
        for g in gens:
            alive = step(g)
            active = [a for a in active if step(a)]
            if alive:
                active.append(g)
        while active:
            active = [a for a in active if step(a)]

    def linear(self, hT, d_hT, kd, wview, cols, subs, epilogue, wring, mw=128):
        nc, fw = self.nc, self.fw
        wts = {}
        PF = 2

        def load(fi):
            if fi < len(cols) and fi not in wts:
                wts[fi] = wring.next()
                fw.dma(fw.pool, wts[fi][0][:, 0:kd, 0:mw], wview[:, :, cols[fi]:cols[fi] + mw], writes=[wts[fi][1]])

        def item(fi, c0, si, s0, s1):
            if si == 0:
                for j in range(fi, fi + PF + 1):
                    load(j)
            wt, wd = wts[fi]
            bank, bd = self.bank_ring.next()
            fw.group(fw.pe, [self.mm(bank[0:mw, 0:s1 - s0], wt[:, k, 0:mw], hT[:, k, s0:s1], k == 0, k == kd - 1)
                             for k in range(kd)], reads=[wd, d_hT], writes=[bd])
            self.side_tick()
            r = epilogue(fi, si, (s0, s1), bank, bd)
            if r is not None:
                yield from r

        self.pipeline(item(fi, c0, si, s0, s1) for fi, c0 in enumerate(cols) for si, (s0, s1) in enumerate(subs))

    def linear_tok(self, hT, d_hT, wview, col_groups, tok_tiles, epilogue, wtring):
        nc, fw = self.nc, self.fw
        for gi, c0 in enumerate(col_groups):
            wt, wd = wtring.next()
            fw.dma(fw.pool, wt[:], wview[:, :, c0:c0 + 512], writes=[wd])
            for ti, t0 in enumerate(tok_tiles):
                bank, bd = self.bank_ring.next()
                fw.group(fw.pe, [self.mm(bank[:, :], hT[:, k, t0:t0 + 128], wt[:, k, :], k == 0, k == KD - 1)
                                 for k in range(KD)], reads=[wd, d_hT], writes=[bd])
                self.side_tick()
                epilogue(gi, ti, t0, bank, bd)

    @staticmethod
    def nat(blk, col):
        h = blk % 2
        if col < 128:
            return h * 128 + col
        return CTX + h * 1024 + (col - 128)

    def phase_C1(self, l, ctx):
        nc, fw = self.nc, self.fw
        cf, d_cf = self.coef[l]
        wv = self.W[f"l{l}_w_out"].rearrange("(k p) m -> p k m", p=128)
        with self.phase("C1"):
            oT_ring = self.ring("oT", 2, [128, KD, TB], BF16)
            wring = self.ring("wC1", 4, [128, KD, 128], BF16)
            xc_ring = self.ring("xc1", 3, [128, TB], F32)
            xo_ring = self.ring("xo1", 3, [128, TB], F32)
            sq_ring = self.ring("sq1", 3, [128, TB], BF16)
            rs, d_rs = self.sb("rs1", [128, TB], F32)
            subs = self.subs(ctx)
            a0 = subs[0][0]
            saved_ring = self.bank_ring
            self.bank_ring = Ring(self.banks[0:5])
            oTs = {}

            for blk in range(4):
                b = blk // 2
                if blk not in oTs:
                    oTs[blk] = oT_ring.next()
                    fw.dma(fw.sp, oTs[blk][0][:, :, a0:TB], self.OT[blk].rearrange("k p t -> p k t")[:, :, a0:TB],
                           writes=[oTs[blk][1]])
                if blk + 1 < 4:
                    oTs[blk + 1] = oT_ring.next()
                    fw.dma(fw.sp, oTs[blk + 1][0][:, :, a0:TB], self.OT[blk + 1].rearrange("k p t -> p k t")[:, :, a0:TB],
                           writes=[oTs[blk + 1][1]])
                oT, d_oT = oTs[blk]
                state = {}
                st = self.stats_begin(subs)

                def epi(fi, si, rng, bank, bd):
                    s0, s1 = rng
                    if si == 0:
                        self.stats_flush(st)
                        state["xc"] = xc_ring.next()
                        state["xo"] = xo_ring.next()
                        state["sq"] = sq_ring.next()
                        fw.dma(fw.sp, state["xc"][0][:, a0:TB], self.xT[blk, fi, :, a0:TB], writes=[state["xc"][1]])
                    xc, d_xc = state["xc"]
                    xo, d_xo = state["xo"]
                    r = 2 if s1 <= 128 else b
                    fw.op(fw.dve, lambda: nc.vector.scalar_tensor_tensor(
                        xo[:, s0:s1], bank[:, 0:s1 - s0], cf[:, 32 + fi, r:r + 1], xc[:, s0:s1], ALU.mult, ALU.add),
                        reads=[bd, d_xc, d_cf], writes=[d_xo])
                    self.stats_push(st, xo, d_xo, fi, si, state["sq"][0], state["sq"][1])
                    if si == len(subs) - 1:
                        fw.dma(fw.sp, self.xT[blk, fi, :, a0:TB], xo[:, a0:TB], reads=[d_xo], writes=[Dep()])

                self.linear(oT, d_oT, KD, wv, [c * 128 for c in range(KD)], subs, epi, wring)
                self.stats_end(st, 1, blk, rs, d_rs)
            self.bank_ring = saved_ring

    def phase_C2(self, l, ctx):
        nc, fw = self.nc, self.fw
        cf, d_cf = self.coef[l]
        w_in = self.W[f"l{l}_ffn_w_in"].rearrange("(k p) m -> p k m", p=128)
        w_out = self.W[f"l{l}_ffn_w_out"].rearrange("(f p) m -> p f m", p=128)
        with self.phase("C2"):
            self.alloc_norm(n_xc=2)
            sq_ring = self.ring("sq2", 2, [128, TB], BF16)
            saved_ring = self.bank_ring
            self.bank_ring = Ring(self.banks[0:5])
            make_stats = l != self.layers[-1]
            hT, d_hT = self.sb("hT", [128, KD, TB], BF16)
            act, _ = self.sb("act", [128, NF, TB], BF16)
            wring = self.ring("wffn", 4, [128, KD, 128], BF16)
            woring = self.ring("wffo", 3, [128, NF // 2, 128], BF16)
            sg_ring = self.ring("sg", 2, [128, 512], F32)
            subs = self.subs(ctx)
            a0 = subs[0][0]
            self.norm_mod(l, 0, 1, ctx, hT, d_hT)
            for blk in range(4):
                b = blk // 2
                d_act = [Dep() for _ in range(NF)]
                for f in range(NF):
                    wg, d_wg = wring.next()
                    wu, d_wu = wring.next()
                    fw.dma(fw.pool, wg[:], w_in[:, :, f * 128:(f + 1) * 128], writes=[d_wg])
                    fw.dma(fw.pool, wu[:], w_in[:, :, FFN_H + f * 128:FFN_H + (f + 1) * 128], writes=[d_wu])
                    for (s0, s1) in subs:
                        n = s1 - s0
                        bg, d_bg = self.bank_ring.next()
                        bu, d_bu = self.bank_ring.next()
                        fw.group(fw.pe, [self.mm(bg[:, 0:n], wg[:, k, :], hT[:, k, s0:s1], k == 0, k == KD - 1)
                                         for k in range(KD)], reads=[d_wg, d_hT], writes=[d_bg])
                        fw.group(fw.pe, [self.mm(bu[:, 0:n], wu[:, k, :], hT[:, k, s0:s1], k == 0, k == KD - 1)
                                         for k in range(KD)], reads=[d_wu, d_hT], writes=[d_bu])
                        sg, d_sg = sg_ring.next()
                        fw.op(fw.act, lambda: nc.scalar.activation(sg[:, 0:n], bg[:, 0:n], AF.Silu),
                              reads=[d_bg], writes=[d_sg])
                        fw.op(fw.dve, lambda: nc.vector.tensor_tensor(act[:, f, s0:s1], sg[:, 0:n], bu[:, 0:n], ALU.mult),
                              reads=[d_sg, d_bu], writes=[d_act[f]])
                if blk < 3:
                    self.norm_mod(l, blk + 1, 1, ctx, hT, d_hT)
                H = NF // 2
                st = self.stats_begin(subs)
                for dch in range(KD):
                    wh = []
                    for hf in range(2):
                        wt, d_wt = woring.next()
                        fw.dma(fw.pool, wt[:], w_out[:, hf * H:(hf + 1) * H, dch * 128:(dch + 1) * 128], writes=[d_wt])
                        wh.append((wt, d_wt))
                    xc, d_xc = self.xc_ring.next()
                    fw.dma(fw.sp, xc[:, a0:TB], self.xT[blk, dch, :, a0:TB], writes=[d_xc])
                    xo, d_xo = self.tmp_ring.next()
                    obanks = [self.bank_ring.next() for _ in subs]
                    for hf in range(2):
                        wt, d_wt = wh[hf]
                        for (bank, bd), (s0, s1) in zip(obanks, subs):
                            n = s1 - s0
                            fw.group(fw.pe, [self.mm(bank[:, 0:n], wt[:, f, :], act[:, hf * H + f, s0:s1],
                                                     hf == 0 and f == 0, hf == 1 and f == H - 1) for f in range(H)],
                                     reads=[d_wt] + d_act[hf * H:(hf + 1) * H], writes=[bd])
                    self.stats_flush(st)
                    sq, d_sq = sq_ring.next()
                    for si, ((bank, bd), (s0, s1)) in enumerate(zip(obanks, subs)):
                        n = s1 - s0
                        r = 2 if s1 <= 128 else b
                        fw.op(fw.dve, lambda: nc.vector.scalar_tensor_tensor(
                            xo[:, s0:s1], bank[:, 0:n], cf[:, 80 + dch, r:r + 1], xc[:, s0:s1], ALU.mult, ALU.add),
                            reads=[bd, d_xc, d_cf], writes=[d_xo])
                        if make_stats:
                            self.stats_push(st, xo, d_xo, dch, si, sq, d_sq)
                    fw.dma(fw.sp, self.xT[blk, dch, :, a0:TB], xo[:, a0:TB], reads=[d_xo], writes=[Dep()])
                if make_stats:
                    rs, d_rs = self.tmp_ring.next()
                    self.stats_end(st, 0, blk, rs, d_rs)
            self.bank_ring = saved_ring

    def fnet_A(self, l):
        nc, fw = self.nc, self.fw
        with self.phase("fnetA"):
            self.alloc_norm()
            hT, d_hT = self.sb("hT", [128, KD, TB], BF16)
            dc, d_dc = self.sb("dftc", [128, 2, 4, 512], BF16)
            st_ring = self.ring("abst", 4, [128, 512], BF16)
            fw.dma(fw.sp, dc[:].rearrange("p a k m -> p (a k m)"), self.dftc_d, writes=[d_dc])
            for blk in range(4):
                b, h = divmod(blk, 2)
                self.norm_mod(l, blk, 0, False, hT, d_hT)
                for t in range(8):
                    c0 = 128 + t * 128
                    tt = h * 8 + t
                    for g in range(4):
                        for a in range(2):
                            bank, bd = self.bank_ring.next()
                            fw.group(fw.pe, [self.mm(bank[:, :], hT[:, 4 * g + kc, c0:c0 + 128], dc[:, a, kc, :],
                                                     kc == 0, kc == 3) for kc in range(4)],
                                     reads=[d_hT, d_dc], writes=[bd])
                            st, d_st = st_ring.next()
                            if a == 0:
                                fw.op(fw.act, lambda: nc.scalar.copy(st[:], bank[:]), reads=[bd], writes=[d_st])
                            else:
                                fw.op(fw.dve, lambda: nc.vector.tensor_scalar(st[:], bank[:], -1.0, None, ALU.mult),
                                      reads=[bd], writes=[d_st])
                            fw.dma(fw.sp, self.AB[b, g, a, tt], st[:], reads=[d_st], writes=[Dep()])

    def fnet_B(self, l):
        nc, fw = self.nc, self.fw
        with self.phase("fnetB"):
            dl, d_dl = self.sb("dftl", [128, 2, 16, SEQ], BF16)
            ab_ring = self.ring("ab", 2, [128, 2, 16, 512], BF16)
            st_ring = self.ring("yst", 3, [128, 512], BF16)
            for a in range(2):
                for q in range(4):
                    fw.dma(fw.sp, dl[:, a, q * 4:(q + 1) * 4, :].rearrange("p k m -> p (k m)"),
                           self.dftl_d[:, (a * 16 + q * 4) * SEQ:(a * 16 + q * 4 + 4) * SEQ], writes=[d_dl])
            for b in range(2):
                for g in range(4):
                    ab, d_ab = ab_ring.next()
                    for a in range(2):
                        fw.dma(fw.sp, ab[:, a], self.AB[b, g, a].rearrange("t p c -> p t c"), writes=[d_ab])
                    for cc in range(4):
                        for tb in range(4):
                            bank, bd = self.bank_ring.next()
                            fns = []
                            for a in range(2):
                                for tt in range(16):
                                    fns.append(self.mm(bank[:, :], ab[:, a, tt, cc * 128:(cc + 1) * 128],
                                                       dl[:, a, tt, tb * 512:(tb + 1) * 512],
                                                       a == 0 and tt == 0, a == 1 and tt == 15))
                            fw.group(fw.pe, fns, reads=[d_ab, d_dl], writes=[bd])
                            st, d_st = st_ring.next()
                            if (cc + tb) % 2 == 0:
                                fw.op(fw.act, lambda: nc.scalar.copy(st[:], bank[:]), reads=[bd], writes=[d_st])
                            else:
                                fw.op(fw.dve, lambda: nc.vector.tensor_copy(st[:], bank[:]), reads=[bd], writes=[d_st])
                            blk = b * 2 + tb // 2
                            col = 128 + (tb % 2) * 512
                            fw.dma(fw.sp, self.OT[blk, 4 * g + cc, :, col:col + 512], st[:], reads=[d_st], writes=[Dep()])

    def attn_A(self, l, diff):
        nc, fw = self.nc, self.fw
        p = f"l{l}_"
        wv = self.W[p + "w_in"].rearrange("(k p) m -> p k m", p=128)
        nq = 16
        nk = 16 if diff else 4
        v0 = (nq + nk) * 128
        nvg = 4 if diff else 1
        with self.phase("attnA"):
            self.alloc_norm()
            hT, d_hT = self.sb("hT", [128, KD, TB], BF16)
            rope, d_rope = self.sb("rope", [128, 2, SEQ], F32)
            RT, d_RT = self.sb("RT", [128, 128], BF16)
            qk, d_qk = self.sb("qk", [128, 2], F32)
            wring = self.ring("wA", 4, [128, KD, 128], BF16)
            wtring = self.ring("wtA", 2, [128, KD, 512], BF16)
            sq_r = self.ring("sqA", 3, [128, 512], BF16)
            raw_r = self.ring("rawA", 2, [128, 512], F32)
            t_r = self.ring("tA", 2, [128, 512], F32)
            qn_r = self.ring("qnA", 3, [128, 512], BF16)
            t1_r = self.ring("t1A", 2, [128, 512], F32)
            t2_r = self.ring("t2A", 2, [128, 512], F32)
            qf_r = self.ring("qfA", 3, [128, 512], BF16)
            vst_r = self.ring("vstA", 3, [128, 512], BF16)
            fw.dma(fw.sp, rope[:].rearrange("p a t -> p (a t)"), self.rope_d, writes=[d_rope])
            fw.dma(fw.sp, RT[:], self.RT_d, writes=[d_RT])
            fw.dma(fw.sp, qk[:], self.W[p + "qk"], writes=[d_qk])
            self.host_side(l)
            for blk in range(4):
                b, h = divmod(blk, 2)
                self.norm_mod(l, blk, 0, True, hT, d_hT)

                def epi(fi, si, rng, bank, bd):
                    s0, s1 = rng
                    n = s1 - s0
                    is_q = fi < nq
                    is_ctx = s1 <= 128
                    if diff and is_q and is_ctx:
                        return
                    g = qk[:, 0:1] if is_q else qk[:, 1:2]
                    sq, d_sq = sq_r.next()
                    fw.op(fw.act, lambda: nc.scalar.activation(sq[:, 0:n], bank[:, 0:n], AF.Square), reads=[bd], writes=[d_sq])
                    yield
                    ssb, d_ssb = self.bank_ring.next()
                    fw.group(fw.pe, [self.mm(ssb[:, 0:n], self.onesB[:], sq[:, 0:n], True, True)],
                             reads=[d_sq, self.d_ones], writes=[d_ssb])
                    t, d_t = t_r.next()
                    fw.op(fw.act, lambda: nc.scalar.activation(t[:, 0:n], ssb[:, 0:n], AF.Ln, bias=self.epsc[:, 0:1],
                                                               scale=1.0 / 128), reads=[d_ssb, self.d_epsc], writes=[d_t])
                    fw.op(fw.act, lambda: nc.scalar.activation(t[:, 0:n], t[:, 0:n], AF.Exp, scale=-0.5), reads=[d_t], writes=[d_t])
                    nat0 = self.nat(blk, s0)
                    if is_ctx:
                        qf, d_qf = qf_r.next()
                        fw.op(fw.dve, lambda: nc.vector.scalar_tensor_tensor(qf[:, 0:n], bank[:, 0:n], g, t[:, 0:n],
                                                                             ALU.mult, ALU.mult),
                              reads=[bd, d_t, d_qk], writes=[d_qf])
                        fw.dma(fw.sp, self.PA[b, fi, :, nat0:nat0 + n], qf[:, 0:n], reads=[d_qf], writes=[Dep()])
                        return
                    qn, d_qn = qn_r.next()
                    fw.op(fw.dve, lambda: nc.vector.scalar_tensor_tensor(qn[:, 0:n], bank[:, 0:n], g, t[:, 0:n],
                                                                         ALU.mult, ALU.mult),
                          reads=[bd, d_t, d_qk], writes=[d_qn])
                    yield
                    rb, d_rb = self.bank_ring.next()
                    fw.group(fw.pe, [self.mm(rb[:, 0:n], RT[:], qn[:, 0:n], True, True)], reads=[d_qn, d_RT], writes=[d_rb])
                    lt0 = nat0 - CTX
                    t1, d_t1 = t1_r.next()
                    fw.op(fw.pool, lambda: nc.gpsimd.tensor_tensor(t1[:, 0:n], qn[:, 0:n], rope[:, 0, lt0:lt0 + n], ALU.mult),
                          reads=[d_qn, d_rope], writes=[d_t1])
                    t2, d_t2 = t2_r.next()
                    fw.op(fw.dve, lambda: nc.vector.tensor_tensor(t2[:, 0:n], rb[:, 0:n], rope[:, 1, lt0:lt0 + n], ALU.mult),
                          reads=[d_rb, d_rope], writes=[d_t2])
                    qf, d_qf = qf_r.next()
                    fw.op(fw.pool, lambda: nc.gpsimd.tensor_tensor(qf[:, 0:n], t1[:, 0:n], t2[:, 0:n], ALU.add),
                          reads=[d_t1, d_t2], writes=[d_qf])
                    fw.dma(fw.sp, self.PA[b, fi, :, nat0:nat0 + n], qf[:, 0:n], reads=[d_qf], writes=[Dep()])

                self.linear(hT, d_hT, KD, wv, [c * 128 for c in range(nq + nk)], self.subs(True), epi, wring)

                def epi_v(gi, ti, t0, bank, bd):
                    vs, d_vs = vst_r.next()
                    if ti % 2 == 0:
                        fw.op(fw.act, lambda: nc.scalar.copy(vs[:], bank[:]), reads=[bd], writes=[d_vs])
                    else:
                        fw.op(fw.dve, lambda: nc.vector.tensor_copy(vs[:], bank[:]), reads=[bd], writes=[d_vs])
                    n0 = self.nat(blk, t0)
                    fw.dma(fw.sp, self.VT[b, n0:n0 + 128, gi * 512:(gi + 1) * 512], vs[:], reads=[d_vs], writes=[Dep()])

                self.linear_tok(hT, d_hT, wv, [v0 + g * 512 for g in range(nvg)], [t * 128 for t in range(9)], epi_v, wtring)
            self.side_drain()

    def qblocks(self, b, n_lat, with_ctx):
        out = []
        if with_ctx:
            out.append((0, 256, [0, 1], [(b * 2, 0, 0, 128), (b * 2 + 1, 0, 128, 128)]))
        for j in range(SEQ // n_lat):
            lt = j * n_lat
            out.append((CTX + lt, n_lat, list(range(18)), [(b * 2 + lt // 1024, 128 + lt % 1024, 0, n_lat)]))
        return out

    def gqa_B(self, l):
        nc, fw = self.nc, self.fw
        scale = 128 ** -0.5
        with self.phase("gqaB"):
            k_r = self.ring("kB", 2, [128, NTOK], BF16)
            v_r = self.ring("vB", 2, [128, 18, 128], BF16)
            q_r = self.ring("qB", 3, [128, 512], BF16)
            e_r = self.ring("eB", 4, [128, 512], BF16)
            rd_r = self.ring("rdB", 2, [128, 512], F32)
            o_r = self.ring("oB", 2, [128, 512], BF16)
            st_r = Ring([self.banks[0], self.banks[1], self.banks[2], self.banks[7]])
            o_b = Ring([self.banks[3], self.banks[5]])
            d_b = Ring([self.banks[4], self.banks[6]])

            def qblock_items(b, kT, d_kT, Vg, d_V, hq, q0, n, kts, dsts):
                cx = {}
                last = len(kts) - 1

                def item(i):
                    kt = kts[i]
                    if i == 0:
                        cx["q"] = q_r.next()
                        fw.dma(fw.sp, cx["q"][0][:, 0:n], self.PA[b, hq, :, q0:q0 + n], writes=[cx["q"][1]])
                        cx["O"] = o_b.next()
                        cx["D"] = d_b.next()
                    q, d_q = cx["q"]
                    O, d_O = cx["O"]
                    Dn, d_D = cx["D"]
                    bk, bd = st_r.next()
                    fw.group(fw.pe, [self.mm(bk[:, 0:n], kT[:, kt * 128:(kt + 1) * 128], q[:, 0:n], True, True)],
                             reads=[d_kT, d_q], writes=[bd])
                    yield
                    E, d_E = e_r.next()
                    fw.op(fw.act, lambda: nc.scalar.activation(E[:, 0:n], bk[:, 0:n], AF.Exp, scale=scale),
                          reads=[bd], writes=[d_E])
                    yield
                    fw.group(fw.pe, [self.mm(O[:, 0:n], Vg[:, kt, :], E[:, 0:n], i == 0, i == last),
                                     self.mm(Dn[:, 0:n], self.onesB[:], E[:, 0:n], i == 0, i == last)],
                             reads=[d_E, d_V, self.d_ones], writes=[d_O, d_D])
                    if i == last:
                        rd, d_rd = rd_r.next()
                        fw.op(fw.dve, lambda: nc.vector.reciprocal(rd[:, 0:n], Dn[:, 0:n]), reads=[d_D], writes=[d_rd])
                        o, d_o = o_r.next()
                        fw.op(fw.dve, lambda: nc.vector.tensor_tensor(o[:, 0:n], O[:, 0:n], rd[:, 0:n], ALU.mult),
                              reads=[d_O, d_rd], writes=[d_o])
                        for (blk, col, off, ln) in dsts:
                            fw.dma(fw.sp, self.OT[blk, hq, :, col:col + ln], o[:, off:off + ln], reads=[d_o], writes=[Dep()])

                return [item(i) for i in range(len(kts))]

            def gens():
                for b in range(2):
                    for g in range(4):
                        kT, d_kT = k_r.next()
                        fw.dma(fw.sp, kT[:], self.PA[b, 16 + g], writes=[d_kT])
                        Vg, d_V = v_r.next()
                        fw.dma(fw.sp, Vg[:], self.VT[b, :, g * 128:(g + 1) * 128].rearrange("(t p) d -> p t d", p=128), writes=[d_V])
                        for j in range(4):
                            for (q0, n, kts, dsts) in self.qblocks(b, 512, True):
                                for it in qblock_items(b, kT, d_kT, Vg, d_V, 4 * g + j, q0, n, kts, dsts):
                                    yield it

            self.pipeline(gens())

    def diff_B(self, l):
        nc, fw = self.nc, self.fw
        scale = 128 ** -0.5
        lam_init = 0.8 - 0.6 * math.exp(-0.3 * l)
        p = f"l{l}_"
        with self.phase("diffB"):
            lv, d_lv = self.sb("lv", [128, 512], F32)
            lt, d_lt = self.sb("ltmp", [128, 2, 128], F32)
            ls, d_ls = self.sb("ls", [128, 4], F32)
            on, d_on = self.sb("on", [128, 2], F32)
            fw.dma(fw.sp, lv[:], self.W[p + "lvec"], writes=[d_lv])
            fw.dma(fw.sp, on[:], self.W[p + "onT"], writes=[d_on])
            for m in range(2):
                fw.op(fw.dve, lambda: nc.vector.tensor_tensor(lt[:, m, :], lv[:, m * 256:m * 256 + 128],
                                                              lv[:, m * 256 + 128:m * 256 + 256], ALU.mult),
                      reads=[d_lv], writes=[d_lt])
                fw.op(fw.dve, lambda: nc.vector.reduce_sum(ls[:, m:m + 1], lt[:, m, :], axis=AX.X), reads=[d_lt], writes=[d_ls])
            fw.op(fw.act, lambda: nc.scalar.activation(ls[:, 0:2], ls[:, 0:2], AF.Exp), reads=[d_ls], writes=[d_ls])
            fw.op(fw.dve, lambda: nc.vector.tensor_tensor(ls[:, 2:3], ls[:, 1:2], ls[:, 0:1], ALU.subtract), reads=[d_ls], writes=[d_ls])
            fw.op(fw.dve, lambda: nc.vector.tensor_scalar(ls[:, 3:4], ls[:, 2:3], -lam_init, None, ALU.add), reads=[d_ls], writes=[d_ls])
            fw.op(fw.dve, lambda: nc.vector.tensor_scalar(on[:], on[:], 1.0 - lam_init, None, ALU.mult), reads=[d_on], writes=[d_on])
            neglam = ls[:, 3:4]
            k_r = self.ring("kD", 2, [128, 2, NTOK], BF16)
            v_r = self.ring("vD", 2, [128, 18, 256], BF16)
            q_r = self.ring("qD", 3, [128, 2, 256], BF16)
            e_r = self.ring("eD", 4, [128, 512], BF16)
            rd_r = self.ring("rdD", 2, [128, 512], F32)
            t1_r = self.ring("t1D", 2, [128, 2, 256], F32)
            t2_r = self.ring("t2D", 2, [128, 2, 256], F32)
            sq_r = self.ring("sqD", 2, [128, 2, 256], BF16)
            rs_r = self.ring("rsD", 2, [128, 256], F32)
            o_r = self.ring("oD", 2, [128, 2, 256], BF16)
            st_r = Ring(self.banks[0:2])
            ssb, d_ssb = self.banks[2]
            Ob = [self.banks[3], self.banks[4], self.banks[5], self.banks[6]]
            Dn, d_D = self.banks[7]

            def qblock_items(b, h, kT, d_kT, Vh, d_V, q0, n, kts, dsts):
                cx = {}
                last = len(kts) - 1

                def item(i):
                    kt = kts[i]
                    if i == 0:
                        cx["q"] = q_r.next()
                        fw.dma(fw.sp, cx["q"][0][:], self.PA[b, 2 * h:2 * h + 2, :, q0:q0 + n].rearrange("c p t -> p c t"),
                               writes=[cx["q"][1]])
                    q, d_q = cx["q"]
                    bk, bd = st_r.next()
                    fw.group(fw.pe, [self.mm(bk[:, m * 256:(m + 1) * 256], kT[:, m, kt * 128:(kt + 1) * 128], q[:, m, :], True, True)
                                     for m in range(2)], reads=[d_kT, d_q], writes=[bd])
                    yield
                    E, d_E = e_r.next()
                    fw.op(fw.act, lambda: nc.scalar.activation(E[:], bk[:], AF.Exp, scale=scale), reads=[bd], writes=[d_E])
                    yield
                    fns = []
                    for m in range(2):
                        for dv in range(2):
                            fns.append(self.mm(Ob[m * 2 + dv][0][:, 0:256], Vh[:, kt, dv * 128:(dv + 1) * 128],
                                               E[:, m * 256:(m + 1) * 256], i == 0, i == last))
                    fns.append(self.mm(Dn[:, :], self.onesB[:], E[:], i == 0, i == last))
                    fw.group(fw.pe, fns, reads=[d_E, d_V, self.d_ones], writes=[x[1] for x in Ob] + [d_D])
                    if i != last:
                        return
                    t1, d_t1 = t1_r.next()
                    t2, d_t2 = t2_r.next()
                    for dv in range(2):
                        fw.op(fw.dve, lambda: nc.vector.tensor_copy(t1[:, dv, :], Ob[dv][0][:, 0:256]), reads=[Ob[dv][1]], writes=[d_t1])
                        fw.op(fw.act, lambda: nc.scalar.copy(t2[:, dv, :], Ob[2 + dv][0][:, 0:256]), reads=[Ob[2 + dv][1]], writes=[d_t2])
                    rd, d_rd = rd_r.next()
                    fw.op(fw.act, lambda: nc.scalar.activation(rd[:], Dn[:], AF.Ln), reads=[d_D], writes=[d_rd])
                    fw.op(fw.act, lambda: nc.scalar.activation(rd[:], rd[:], AF.Exp, scale=-1.0), reads=[d_rd], writes=[d_rd])
                    fw.op(fw.dve, lambda: nc.vector.tensor_tensor(t1[:], t1[:], rd[:, 0:256].unsqueeze(1).to_broadcast([128, 2, 256]), ALU.mult),
                          reads=[d_rd], writes=[d_t1])
                    fw.op(fw.dve, lambda: nc.vector.tensor_tensor(t2[:], t2[:], rd[:, 256:512].unsqueeze(1).to_broadcast([128, 2, 256]), ALU.mult),
                          reads=[d_rd], writes=[d_t2])
                    fw.op(fw.dve, lambda: nc.vector.scalar_tensor_tensor(t1[:], t2[:], neglam, t1[:], ALU.mult, ALU.add),
                          reads=[d_t1, d_t2, d_ls], writes=[d_t1])
                    sq, d_sq = sq_r.next()
                    fw.op(fw.act, lambda: nc.scalar.activation(sq[:], t1[:], AF.Square), reads=[d_t1], writes=[d_sq])
                    yield
                    fw.group(fw.pe, [self.mm(ssb[:, 0:256], self.onesB[:], sq[:, dv, :], dv == 0, dv == 1) for dv in range(2)],
                             reads=[d_sq, self.d_ones], writes=[d_ssb])
                    rs, d_rs = rs_r.next()
                    fw.op(fw.act, lambda: nc.scalar.activation(rs[:], ssb[:, 0:256], AF.Ln, bias=self.epsc[:, 0:1], scale=1.0 / 256),
                          reads=[d_ssb, self.d_epsc], writes=[d_rs])
                    fw.op(fw.act, lambda: nc.scalar.activation(rs[:], rs[:], AF.Exp, scale=-0.5), reads=[d_rs], writes=[d_rs])
                    o, d_o = o_r.next()
                    for dv in range(2):
                        fw.op(fw.dve, lambda: nc.vector.scalar_tensor_tensor(o[:, dv, :], t1[:, dv, :], on[:, dv:dv + 1], rs[:],
                                                                             ALU.mult, ALU.mult),
                              reads=[d_t1, d_rs, d_on], writes=[d_o])
                    (blk, col, off, ln) = dsts[0]
                    fw.dma(fw.sp, self.OT[blk, 2 * h:2 * h + 2, :, col:col + ln].rearrange("c p t -> p c t"), o[:],
                           reads=[d_o], writes=[Dep()])

                return [item(i) for i in range(len(kts))]

            def gens():
                for b in range(2):
                    for h in range(8):
                        kT, d_kT = k_r.next()
                        fw.dma(fw.sp, kT[:], self.PA[b, 16 + 2 * h:16 + 2 * h + 2].rearrange("c p t -> p c t"), writes=[d_kT])
                        Vh, d_V = v_r.next()
                        fw.dma(fw.sp, Vh[:], self.VT[b, :, h * 256:(h + 1) * 256].rearrange("(t p) d -> p t d", p=128), writes=[d_V])
                        for (q0, n, kts, dsts) in self.qblocks(b, 256, False):
                            for it in qblock_items(b, h, kT, d_kT, Vh, d_V, q0, n, kts, dsts):
                                yield it

            self.pipeline(gens())

    def gla_A(self, l):
        nc, fw = self.nc, self.fw
        p = f"l{l}_"
        wv = self.W[p + "w_in"].rearrange("(k p) m -> p k m", p=128)
        with self.phase("glaA"):
            self.alloc_norm()
            hT, d_hT = self.sb("hT", [128, KD, TB], BF16)
            wring = self.ring("wG", 4, [128, KD, 128], BF16)
            wtring = self.ring("wtG", 2, [128, KD, 512], BF16)
            zT, d_zT = self.sb("zT", [33, TB], F32)
            wga, d_wga = self.sb("wga", [33, 2, 1024], F32)
            f_r = self.ring("fstG", 3, [128, 512], F32)
            b_r = self.ring("bstG", 3, [128, 512], BF16)
            e_r = self.ring("etG", 2, [128, 512], F32)
            l_r = self.ring("lstG", 2, [128, 512], F32)
            fw.dma(fw.sp, wga[:, 0, :], self.W[p + "wgaf"], writes=[d_wga])
            fw.dma(fw.sp, wga[:, 1, :], self.W[p + "wgab"], writes=[d_wga])
            fw.op(fw.dve, lambda: nc.vector.memset(zT[32:33, :], 1.0), writes=[d_zT])
            subs = self.subs(True)
            self.host_side(l)
            for blk in range(4):
                b, h = divmod(blk, 2)
                self.norm_mod(l, blk, 0, True, hT, d_hT)

                def epi_z(fi, si, rng, bank, bd):
                    s0, s1 = rng
                    fw.op(fw.act, lambda: nc.scalar.copy(zT[0:32, s0:s1], bank[0:32, 0:s1 - s0]), reads=[bd], writes=[d_zT])

                self.linear(hT, d_hT, KD, wv, [6144], subs, epi_z, wring, mw=32)
                for ti in range(9):
                    t0 = ti * 128
                    n0 = self.nat(blk, t0)
                    for dirn in range(2):
                        for half in range(2):
                            bank, bd = self.bank_ring.next()
                            fw.group(fw.pe, [self.mm(bank[:, :], zT[0:33, t0:t0 + 128],
                                                     wga[0:33, dirn, half * 512:(half + 1) * 512], True, True)],
                                     reads=[d_zT, d_wga], writes=[bd])
                            et, d_et = e_r.next()
                            fw.op(fw.act, lambda: nc.scalar.activation(et[:], bank[:], AF.Exp, scale=-1.0), reads=[bd], writes=[d_et])
                            ls, d_l = l_r.next()
                            fw.op(fw.act, lambda: nc.scalar.activation(ls[:], et[:], AF.Ln, bias=self.epsc[:, 1:2], scale=1.0),
                                  reads=[d_et, self.d_epsc], writes=[d_l])
                            c0 = dirn * 1024 + half * 512
                            fw.dma(fw.sp, self.LT[b, n0:n0 + 128, c0:c0 + 512], ls[:], reads=[d_l], writes=[Dep()])

                def epi_qk(fi, si, rng, bank, bd):
                    s0, s1 = rng
                    n = s1 - s0
                    st, d_st = f_r.next()
                    if (fi + si) % 2 == 0:
                        fw.op(fw.act, lambda: nc.scalar.copy(st[:, 0:n], bank[:, 0:n]), reads=[bd], writes=[d_st])
                    else:
                        fw.op(fw.dve, lambda: nc.vector.tensor_copy(st[:, 0:n], bank[:, 0:n]), reads=[bd], writes=[d_st])
                    n0 = self.nat(blk, s0)
                    fw.dma(fw.sp, self.PAF[b, fi, :, n0:n0 + n], st[:, 0:n], reads=[d_st], writes=[Dep()])

                self.linear(hT, d_hT, KD, wv, [c * 128 for c in range(16)], subs, epi_qk, wring)

                def epi_r(fi, si, rng, bank, bd):
                    s0, s1 = rng
                    n = s1 - s0
                    st, d_st = b_r.next()
                    fw.op(fw.act, lambda: nc.scalar.activation(st[:, 0:n], bank[:, 0:n], AF.Silu), reads=[bd], writes=[d_st])
                    n0 = self.nat(blk, s0)
                    fw.dma(fw.sp, self.SR[b, fi, :, n0:n0 + n], st[:, 0:n], reads=[d_st], writes=[Dep()])

                self.linear(hT, d_hT, KD, wv, [4096 + c * 128 for c in range(16)], subs, epi_r, wring)

                def epi_kv(gi, ti, t0, bank, bd):
                    st, d_st = b_r.next()
                    if ti % 2 == 0:
                        fw.op(fw.act, lambda: nc.scalar.copy(st[:], bank[:]), reads=[bd], writes=[d_st])
                    else:
                        fw.op(fw.dve, lambda: nc.vector.tensor_copy(st[:], bank[:]), reads=[bd], writes=[d_st])
                    n0 = self.nat(blk, t0)
                    fw.dma(fw.sp, self.VT[b, n0:n0 + 128, gi * 512:(gi + 1) * 512], st[:], reads=[d_st], writes=[Dep()])

                self.linear_tok(hT, d_hT, wv, [1024 + g * 512 for g in range(6)], [t * 128 for t in range(9)], epi_kv, wtring)
            self.side_drain()

    def gla_B(self, l):
        nc, fw = self.nc, self.fw
        p = f"l{l}_"
        with self.phase("glaB"):
            gc, d_gc = self.sb("gc", [128, 1024], F32)
            on, d_on = self.sb("onG", [128, 4], F32)
            fw.dma(fw.sp, gc[:], self.gla_c, writes=[d_gc])
            gcb, d_gcb = self.sb("gcb", [128, 768], BF16)
            fw.dma(fw.sp, gcb[:], self.gla_cb, writes=[d_gcb])
            fw.dma(fw.sp, on[:], self.W[p + "onT"], writes=[d_on])
            S, _ = self.sb("S", [128, 4, 2, 512], F32)
            Sb, _ = self.sb("Sb", [128, 4, 2, 512], BF16)
            d_S = [Dep() for _ in range(4)]
            d_Sb = [Dep() for _ in range(4)]
            qk_r = self.ring("qkG", 3, [128, 16, 128], F32)
            kv_r = self.ring("kvG", 3, [128, 3072], BF16)
            lt_r = self.ring("ltG", 3, [128, 1024], BF16)
            of_r = self.ring("ofG", 4, [128, 16, 128], F32)
            sr_r = self.ring("srG", 4, [128, 16, 128], BF16)
            e13_r = self.ring("e13", 4, [128, 2, 256], F32)
            e2_r = self.ring("e2", 2, [128, 2, 128], F32)
            qt_r = self.ring("qtG", 3, [128, 2, 128], BF16)
            kt_r = self.ring("ktG", 3, [128, 2, 128], BF16)
            qi_r = self.ring("qiG", 5, [128, 2, 128], BF16)
            er_r = self.ring("erG", 2, [128, 256], F32)
            kc_r = self.ring("kcG", 5, [128, 256], BF16)
            am_r = self.ring("amG", 3, [128, 128], BF16)
            os_r = self.ring("osG", 3, [128, 4, 128], F32)
            sq_r = self.ring("sqG", 3, [128, 4, 128], BF16)
            rs_r = self.ring("rsG", 2, [128, 128], F32)
            tm_r = self.ring("tmG", 2, [128, 4, 128], F32)
            fin_r = self.ring("finG", 2, [128, 4, 128], BF16)
            xb_r = Ring(self.banks[0:2])
            rb_r = Ring(self.banks[2:4])
            ab_r = Ring(self.banks[4:5])
            ob_r = Ring(self.banks[5:6])
            sk_r = Ring(self.banks[6:7])
            ss_r = Ring(self.banks[7:8])

            for b in range(2):
                for dirn in range(2):
                    for hh in range(4):
                        fw.op(fw.dve, lambda: nc.vector.memset(S[:, hh], 0.0), writes=[d_S[hh]])
                        fw.op(fw.pool, lambda: nc.gpsimd.memset(Sb[:, hh], 0.0), writes=[d_Sb[hh]])
                    order = list(range(18)) if dirn == 0 else [1, 0] + list(range(17, 1, -1))
                    TT = gcb[:, dirn * 256:(dirn + 1) * 256]
                    TriS = gcb[:, 512 + dirn * 128:512 + (dirn + 1) * 128]
                    mask = gc[:, 768 + dirn * 128:768 + (dirn + 1) * 128]
                    dcol = 128 + (127 if dirn == 0 else 0)
                    tiles = {}

                    def load_tile(pos, b=b, dirn=dirn, order=order, tiles=tiles):
                        if pos >= len(order) or pos in tiles:
                            return
                        t0 = order[pos] * 128
                        d = {}
                        d["lt"] = lt_r.next()
                        fw.dma(fw.pool, d["lt"][0][:], self.LT[b, t0:t0 + 128, dirn * 1024:(dirn + 1) * 1024], writes=[d["lt"][1]])
                        d["qk"] = qk_r.next()
                        fw.dma(fw.sp, d["qk"][0][:], self.PAF[b, :, :, t0:t0 + 128].rearrange("c p t -> p c t"), writes=[d["qk"][1]])
                        d["kv"] = kv_r.next()
                        fw.dma(fw.sp, d["kv"][0][:], self.VT[b, t0:t0 + 128, :], writes=[d["kv"][1]])
                        if dirn == 1:
                            d["of"] = of_r.next()
                            fw.dma(fw.sp, d["of"][0][:], self.OF[b, :, :, t0:t0 + 128].rearrange("c p t -> p c t"), writes=[d["of"][1]])
                            d["sr"] = sr_r.next()
                            fw.dma(fw.sp, d["sr"][0][:], self.SR[b, :, :, t0:t0 + 128].rearrange("c p t -> p c t"), writes=[d["sr"][1]])
                        tiles[pos] = d

                    def item(pos, hh, b=b, dirn=dirn, order=order, tiles=tiles, TT=TT, TriS=TriS, mask=mask, dcol=dcol):
                        t = order[pos]
                        t0 = t * 128
                        if hh == 0:
                            load_tile(pos)
                            load_tile(pos + 1)
                        td = tiles[pos]
                        lt, d_lt = td["lt"]
                        qk, d_qk = td["qk"]
                        kv, d_kv = td["kv"]
                        if t < 2:
                            oblk, ocol = b * 2 + t, 0
                        else:
                            ltok = (t - 2) * 128
                            oblk, ocol = b * 2 + ltok // 1024, 128 + ltok % 1024
                        xb, d_xb = xb_r.next()
                        fw.group(fw.pe, [self.mm(xb[:, dc * 256:(dc + 1) * 256], lt[:, hh * 256 + dc * 128:hh * 256 + (dc + 1) * 128],
                                                 TT, True, True) for dc in range(2)], reads=[d_lt, d_gcb], writes=[d_xb])
                        rbb, d_rb = rb_r.next()
                        fw.group(fw.pe, [self.mm(rbb[:, 0:256], TriS, lt[:, hh * 256:(hh + 1) * 256], True, True)],
                                 reads=[d_lt, d_gcb], writes=[d_rb])
                        yield
                        xv = xb[:].rearrange("p (c i) -> p c i", c=2)
                        e13, d_e13 = e13_r.next()
                        fw.op(fw.act, lambda: nc.scalar.activation(e13[:], xv, AF.Exp), reads=[d_xb], writes=[d_e13])
                        e2, d_e2 = e2_r.next()
                        fw.op(fw.act, lambda: nc.scalar.activation(e2[:], xv[:, :, 0:128], AF.Exp, scale=-1.0), reads=[d_xb], writes=[d_e2])
                        er, d_er = er_r.next()
                        fw.op(fw.act, lambda: nc.scalar.activation(er[:], rbb[:, 0:256], AF.Exp), reads=[d_rb], writes=[d_er])
                        qv = qk[:, 2 * hh:2 * hh + 2, :]
                        kvv = qk[:, 8 + 2 * hh:8 + 2 * hh + 2, :]
                        qt, d_qt = qt_r.next()
                        fw.op(fw.dve, lambda: nc.vector.scalar_tensor_tensor(qt[:], qv, 0.0625, e13[:, :, 0:128], ALU.mult, ALU.mult),
                              reads=[d_qk, d_e13], writes=[d_qt])
                        kt_, d_kt = kt_r.next()
                        fw.op(fw.pool, lambda: nc.gpsimd.tensor_tensor(kt_[:], kvv, e2[:], ALU.mult), reads=[d_qk, d_e2], writes=[d_kt])
                        qi, d_qi = qi_r.next()
                        fw.op(fw.dve, lambda: nc.vector.scalar_tensor_tensor(qi[:], qv, 0.0625, e13[:, :, 128:256], ALU.mult, ALU.mult),
                              reads=[d_qk, d_e13], writes=[d_qi])
                        kc, d_kc = kc_r.next()
                        fw.op(fw.pool, lambda: nc.gpsimd.tensor_tensor(kc[:], kv[:, hh * 256:(hh + 1) * 256], er[:], ALU.mult),
                              reads=[d_kv, d_er], writes=[d_kc])
                        yield
                        abb, d_ab = ab_r.next()
                        fw.group(fw.pe, [self.mm(abb[:, 0:128], kt_[:, dc, :], qt[:, dc, :], dc == 0, dc == 1) for dc in range(2)],
                                 reads=[d_kt, d_qt], writes=[d_ab])
                        am, d_am = am_r.next()
                        fw.op(fw.dve, lambda: nc.vector.tensor_tensor(am[:], abb[:, 0:128], mask, ALU.mult), reads=[d_ab, d_gc], writes=[d_am])
                        yield
                        ob, d_ob = ob_r.next()
                        fns = []
                        for dvc in range(4):
                            oo = ob[:, dvc * 128:(dvc + 1) * 128]
                            vcol = 1024 + hh * 512 + dvc * 128
                            fns.append(self.mm(oo, kv[:, vcol:vcol + 128], am[:], True, False))
                            for dc in range(2):
                                fns.append(self.mm(oo, Sb[:, hh, dc, dvc * 128:(dvc + 1) * 128], qi[:, dc, :], False, dc == 1))
                        fw.group(fw.pe, fns, reads=[d_kv, d_am, d_Sb[hh], d_qi], writes=[d_ob])
                        for dc in range(2):
                            sbk, d_sbk = sk_r.next()
                            fw.group(fw.pe, [self.mm(sbk[:, :], kc[:, dc * 128:(dc + 1) * 128],
                                                     kv[:, 1024 + hh * 512:1024 + (hh + 1) * 512], True, True)],
                                     reads=[d_kc, d_kv], writes=[d_sbk])
                            fw.op(fw.dve, lambda: nc.vector.scalar_tensor_tensor(S[:, hh, dc, :], S[:, hh, dc, :],
                                                                                 e13[:, dc, dcol:dcol + 1], sbk[:, :],
                                                                                 ALU.mult, ALU.add),
                                  reads=[d_sbk, d_e13], writes=[d_S[hh]])
                            fw.op(fw.act, lambda: nc.scalar.copy(Sb[:, hh, dc, :], S[:, hh, dc, :]), reads=[d_S[hh]], writes=[d_Sb[hh]])
                        ov = ob[:].rearrange("p (c i) -> p c i", c=4)
                        os_, d_os = os_r.next()
                        if dirn == 0:
                            fw.op(fw.act, lambda: nc.scalar.copy(os_[:], ov), reads=[d_ob], writes=[d_os])
                            fw.dma(fw.sp, self.OF[b, 4 * hh:4 * hh + 4, :, t0:t0 + 128].rearrange("c p t -> p c t"), os_[:],
                                   reads=[d_os], writes=[Dep()])
                            return
                        of, d_of = td["of"]
                        sr, d_sr = td["sr"]
                        fw.op(fw.dve, lambda: nc.vector.tensor_tensor(os_[:], ov, of[:, 4 * hh:4 * hh + 4, :], ALU.add),
                              reads=[d_ob, d_of], writes=[d_os])
                        sq, d_sq = sq_r.next()
                        fw.op(fw.act, lambda: nc.scalar.activation(sq[:], os_[:], AF.Square), reads=[d_os], writes=[d_sq])
                        yield
                        ssb, d_ssb = ss_r.next()
                        fw.group(fw.pe, [self.mm(ssb[:, 0:128], self.onesB[:], sq[:, dvc, :], dvc == 0, dvc == 3) for dvc in range(4)],
                                 reads=[d_sq, self.d_ones], writes=[d_ssb])
                        rs, d_rs = rs_r.next()
                        fw.op(fw.act, lambda: nc.scalar.activation(rs[:], ssb[:, 0:128], AF.Ln, bias=self.epsc[:, 0:1], scale=1.0 / 512),
                              reads=[d_ssb, self.d_epsc], writes=[d_rs])
                        fw.op(fw.act, lambda: nc.scalar.activation(rs[:], rs[:], AF.Exp, scale=-0.5), reads=[d_rs], writes=[d_rs])
                        tm, d_tm = tm_r.next()
                        fw.op(fw.dve, lambda: nc.vector.tensor_tensor(tm[:], os_[:], rs[:].unsqueeze(1).to_broadcast([128, 4, 128]), ALU.mult),
                              reads=[d_os, d_rs], writes=[d_tm])
                        fw.op(fw.pool, lambda: nc.gpsimd.tensor_tensor(tm[:], tm[:], on[:].unsqueeze(2).to_broadcast([128, 4, 128]), ALU.mult),
                              reads=[d_tm, d_on], writes=[d_tm])
                        fin, d_fin = fin_r.next()
                        fw.op(fw.pool, lambda: nc.gpsimd.tensor_tensor(fin[:], tm[:], sr[:, 4 * hh:4 * hh + 4, :], ALU.mult),
                              reads=[d_tm, d_sr], writes=[d_fin])
                        fw.dma(fw.sp, self.OT[oblk, 4 * hh:4 * hh + 4, :, ocol:ocol + 128].rearrange("c p t -> p c t"), fin[:],
                               reads=[d_fin], writes=[Dep()])

                    self.pipeline(item(pos, hh) for pos in range(18) for hh in range(4))


def _vecT(v, k):
    return np.ascontiguousarray(np.asarray(v, np.float32).reshape(k, 128).T)


def _bf(a):
    return np.ascontiguousarray(np.asarray(a).astype(ml_dtypes.bfloat16))


def _pk(m):
    K = m.shape[0] // 128
    return np.ascontiguousarray(m.reshape(K, 128, m.shape[1]).transpose(1, 0, 2).reshape(128, K * m.shape[1]))


_CONST_CACHE = {}


def _constants(layers):
    key = tuple(layers)
    if key in _CONST_CACHE:
        return _CONST_CACHE[key]
    c = {}
    c["identF"] = np.eye(128, dtype=np.float32)
    c["onesB"] = _bf(np.ones((128, 128), np.float32))
    if 1 in layers or 2 in layers:
        t = np.arange(SEQ)
        row = (t // 64).astype(np.float32)
        col = (t % 64).astype(np.float32)
        half = 64
        inv_freq = (np.float32(10000.0) ** (-np.arange(0, half, 2, dtype=np.float32) / np.float32(half))).astype(np.float32)
        ang_r = row[:, None] * inv_freq[None, :]
        ang_c = col[:, None] * inv_freq[None, :]
        ang = np.concatenate([ang_r, ang_r, ang_c, ang_c], axis=-1).astype(np.float32)
        c["rope"] = np.ascontiguousarray(np.concatenate([np.cos(ang).T, np.sin(ang).T], axis=1).astype(np.float32))
        R = np.zeros((128, 128), np.float32)
        for d in range(128):
            q = d // 32
            if q % 2 == 0:
                R[d, d + 32] = -1.0
            else:
                R[d, d - 32] = 1.0
        c["RT"] = _bf(R.T)
    if 3 in layers:
        def dft(n):
            k = np.arange(n, dtype=np.float64)
            ang = 2.0 * np.pi * np.outer(k, k) / n
            return np.cos(ang) / np.sqrt(n), np.sin(ang) / np.sqrt(n)
        cc, sc = dft(512)
        cl, sl = dft(SEQ)
        c["dftc"] = _bf(np.concatenate([_pk(cc), _pk(sc)], axis=1))
        c["dftl"] = _bf(np.concatenate([_pk(cl), _pk(sl)], axis=1))
    if 0 in layers:
        j = np.arange(128)[:, None]
        i = np.arange(128)[None, :]
        s = -1.0 / 16.0
        tri_f = (j <= i).astype(np.float64)
        tri_b = (j >= i).astype(np.float64)
        m1_f = tri_f - (j <= 63)
        m1_b = tri_b - (j >= 64)
        tris_f = (j > i).astype(np.float64)
        tris_b = (j < i).astype(np.float64)
        c["gla_c"] = np.ascontiguousarray(np.concatenate(
            [s * m1_f, s * tri_f, s * m1_b, s * tri_b, s * tris_f, s * tris_b, tri_f, tri_b], axis=1).astype(np.float32))
        c["gla_cb"] = _bf(c["gla_c"][:, 0:768])
    _CONST_CACHE[key] = c
    return c


def _prep_inputs(inputs, layers, x_over=None, ctx_over=None):
    f32 = lambda a: np.ascontiguousarray(np.asarray(a, np.float32))
    shared = dict(_constants(layers))
    for l in layers:
        p = f"l{l}_"
        shared[p + "mod_w"] = f32(inputs[p + "mod_w"])
        shared[p + "mod_bT"] = _vecT(inputs[p + "mod_b"], 96)
        shared[p + "n1T"] = _vecT(inputs[p + "norm1"], KD)
        shared[p + "n2T"] = _vecT(inputs[p + "norm2"], KD)
        shared[p + "ffn_w_in"] = f32(inputs[p + "ffn_w_in"])
        shared[p + "ffn_w_out"] = f32(inputs[p + "ffn_w_out"])
        if l == 0:
            shared[p + "w_in"] = f32(inputs[p + "gla_w_in"])
            z16 = np.zeros((16, 1024), np.float32)
            shared[p + "wgaf"] = np.ascontiguousarray(np.concatenate(
                [f32(inputs[p + "gla_wg_f"]), z16, f32(inputs[p + "gla_bg_f"])[None]], axis=0))
            shared[p + "wgab"] = np.ascontiguousarray(np.concatenate(
                [z16, f32(inputs[p + "gla_wg_b"]), f32(inputs[p + "gla_bg_b"])[None]], axis=0))
            shared[p + "onT"] = _vecT(inputs[p + "gla_out_norm"], 4)
            shared[p + "w_out"] = f32(inputs[p + "gla_w_out"])
        elif l == 1:
            shared[p + "w_in"] = f32(inputs[p + "gqa_w_in"])
            shared[p + "qk"] = np.ascontiguousarray(np.stack([f32(inputs[p + "gqa_q_norm"]), f32(inputs[p + "gqa_k_norm"])], axis=1))
            shared[p + "w_out"] = f32(inputs[p + "gqa_w_out"])
        elif l == 2:
            shared[p + "w_in"] = f32(inputs[p + "diff_w_in"])
            shared[p + "qk"] = np.ascontiguousarray(np.stack([f32(inputs[p + "diff_q_norm"]), f32(inputs[p + "diff_k_norm"])], axis=1))
            lv = np.concatenate([f32(inputs[p + "diff_lq1"]), f32(inputs[p + "diff_lk1"]),
                                 f32(inputs[p + "diff_lq2"]), f32(inputs[p + "diff_lk2"])])
            shared[p + "lvec"] = np.ascontiguousarray(np.broadcast_to(lv[None, :], (128, 512)))
            shared[p + "onT"] = _vecT(inputs[p + "diff_out_norm"], 2)
            shared[p + "w_out"] = f32(inputs[p + "diff_w_out"])
        else:
            shared[p + "w_out"] = f32(inputs[p + "fnet_w_out"])
    x = f32(inputs["x"]) if x_over is None else x_over
    ctx = f32(inputs["ctx"]) if ctx_over is None else ctx_over
    c = f32(inputs["c"])
    c_ctx = f32(inputs["c_ctx"])
    nb = x.shape[0] // 2
    maps = []
    for i in range(nb):
        m = dict(shared)
        m["x"] = np.ascontiguousarray(x[2 * i:2 * i + 2])
        m["ctx"] = np.ascontiguousarray(ctx[2 * i:2 * i + 2])
        cv = np.stack([c[2 * i], c[2 * i + 1], c_ctx], axis=0)
        m["cvecT"] = np.ascontiguousarray(cv.reshape(3, KD, 128).transpose(2, 1, 0).reshape(128, KD * 3))
        maps.append(m)
    return maps


_NC_CACHE = {}


def _get_nc(layers):
    key = tuple(layers)
    if key not in _NC_CACHE:
        _NC_CACHE[key] = Builder(layers).build()
    return _NC_CACHE[key]


def run_layers(inputs, layers, x_over=None, ctx_over=None, trace=False):
    maps = _prep_inputs(inputs, layers, x_over, ctx_over)
    nc = _get_nc(layers)
    res = run_bass_kernel_spmd(nc, maps, core_ids=list(range(len(maps))), trace=trace)
    out = np.concatenate([r["y"] for r in res.results], axis=0)
    return out, res


def kernel(**inputs):
    out, _ = run_layers(inputs, (0, 1, 2, 3))
    return out.astype(np.float32, copy=False)
```
